# Optimizing a Trainium2 kernel written in Bass

```python
import jax
import jax.numpy as jnp
from jax import lax
import numpy as np

D_MODEL = 1024
BATCH = 32
SEQ = 2048
DEPTH = 2

GRID_W = 64
CTX_LEN = 256
CHUNK = 128
EPS = 1e-6
N_MOD = 6

A_GROUPS = 4
A_DIM = 512
A_GDIM = A_DIM // A_GROUPS
B_HEADS = 4
B_HEAD_DIM = 128
B_DIM = B_HEADS * B_HEAD_DIM
N_GATES = 4
MIX_DIM = A_DIM + B_DIM
K_OFF = 0
V_OFF = K_OFF + B_DIM
G_OFF = V_OFF + B_DIM
Q_OFF = G_OFF + N_GATES * B_HEADS
O_OFF = Q_OFF + B_DIM
U_OFF = O_OFF + B_DIM
VA_OFF = U_OFF + A_DIM
AB_PROJ = VA_OFF + A_DIM
C_HEADS = 4
C_QK_DIM = 256
C_V_DIM = 512
RK_OFF = 0
RV_OFF = RK_OFF + C_HEADS * C_QK_DIM
RQ_OFF = RV_OFF + C_HEADS * C_V_DIM
RG_OFF = RQ_OFF + C_HEADS * C_QK_DIM
C_PROJ = RG_OFF + C_HEADS * C_V_DIM
D_FF = 2816
N_EVEN = (DEPTH + 1) // 2
N_ODD = DEPTH // 2

kernel_name = "hybrid_sgu_mlstm_retention_dit"


def _rms(x, g):
    xf = x.astype(jnp.float32)
    y = xf * lax.rsqrt(jnp.mean(xf * xf, axis=-1, keepdims=True) + EPS)
    return (y * g).astype(x.dtype)


def _ln(x):
    xf = x.astype(jnp.float32)
    xc = xf - jnp.mean(xf, axis=-1, keepdims=True)
    return (xc * lax.rsqrt(jnp.mean(xc * xc, axis=-1, keepdims=True) + EPS)).astype(x.dtype)


def _heads(t, nh):
    b, l, w = t.shape
    return t.reshape(b, l, nh, w // nh).transpose(0, 2, 1, 3)


def _head_norm(y, g):
    b, nh, l, d = y.shape
    yn = _ln(y.astype(jnp.float32))
    return yn.transpose(0, 2, 1, 3).reshape(b, l, nh * d) * g


def _to_chunks(t):
    b, h, l = t.shape[:3]
    return jnp.moveaxis(t.reshape(b, h, l // CHUNK, CHUNK, *t.shape[3:]), 2, 0)


def _from_chunks(t):
    t = jnp.moveaxis(t, 0, 2)
    return t.reshape(t.shape[0], t.shape[1], -1, *t.shape[4:])


def _dwconv1d(x, w):
    xp = jnp.pad(x, ((0, 0), (1, 1), (0, 0)))
    return xp[:, :-2] * w[0] + xp[:, 1:-1] * w[1] + xp[:, 2:] * w[2]


def _dwconv_grid(x, w):
    b, s, ch = x.shape
    rows = s // GRID_W
    img = x.reshape(b, rows, GRID_W, ch)
    y = lax.conv_general_dilated(img, w[:, :, None, :], window_strides=(1, 1), padding="SAME",
                                 dimension_numbers=("NHWC", "HWIO", "NHWC"), feature_group_count=ch)
    return y.reshape(b, s, ch)


def _conv_glu(h, w_up, w_conv, w_down, grid):
    a, g = jnp.split(h @ w_up, 2, axis=-1)
    g = _dwconv_grid(g, w_conv) if grid else _dwconv1d(g, w_conv[1])
    return (jax.nn.gelu(g) * a) @ w_down


def _bidir_scan(scan_fn, init, q_l, in_l, q_c, in_c):
    outs_l, outs_c = [], []
    for d in range(2):
        rev = (lambda t: t) if d == 0 else (lambda t: None if t is None else jnp.flip(t, 2))
        state, o_c = scan_fn(d, [rev(t) for t in in_c], init, rev(q_c))
        _, o_l = scan_fn(d, [rev(t) for t in in_l], state, rev(q_l))
        outs_l.append(rev(o_l))
        outs_c.append(rev(o_c))
    y_c = None if q_c is None else outs_c[0] + outs_c[1]
    return outs_l[0] + outs_l[1], y_c


def _mlstm_scan(direction, inputs, state, q):
    k, v, gates = inputs
    with_q = q is not None
    ig = gates[..., 2 * direction]
    lf = jax.nn.log_sigmoid(gates[..., 2 * direction + 1])
    lower = jnp.tril(jnp.ones((CHUNK, CHUNK), dtype=bool))
    xs = (_to_chunks(k), _to_chunks(v), _to_chunks(ig), _to_chunks(lf))
    if with_q:
        xs = xs + (_to_chunks(q),)

    def step(carry, inp):
        c_mem, n_mem, m = carry
        kc, vc, ic, fc = inp[0], inp[1], inp[2], inp[3]
        bcum = jnp.cumsum(fc, axis=-1)
        btot = bcum[..., -1]
        src = btot[..., None] - bcum + ic
        m_new = jnp.maximum(btot + m, jnp.max(src, axis=-1))
        kw = kc * jnp.exp(src - m_new[..., None])[..., None]
        decay_prev = jnp.exp(btot + m - m_new)
        c_new = decay_prev[..., None, None] * c_mem + jnp.einsum("bhsk,bhsv->bhkv", kw, vc)
        n_new = decay_prev[..., None] * n_mem + jnp.sum(kw, axis=2)
        if not with_q:
            return (c_new, n_new, m_new), None
        qc = inp[4]
        log_d = jnp.where(lower, bcum[..., :, None] - bcum[..., None, :] + ic[..., None, :], -jnp.inf)
        log_prev = bcum + m[..., None]
        m_t = jnp.maximum(log_prev, jnp.max(log_d, axis=-1))
        scores = jnp.einsum("bhtk,bhsk->bhts", qc, kc) * jnp.exp(log_d - m_t[..., None])
        w_prev = jnp.exp(log_prev - m_t)
        num = jnp.einsum("bhts,bhsv->bhtv", scores, vc) + w_prev[..., None] * jnp.einsum("bhtk,bhkv->bhtv", qc, c_mem)
        den = jnp.sum(scores, axis=-1) + w_prev * jnp.einsum("bhtk,bhk->bht", qc, n_mem)
        out = num / jnp.maximum(jnp.abs(den), jnp.exp(-m_t))[..., None]
        return (c_new, n_new, m_new), out

    state, ys = lax.scan(step, state, xs)
    return state, (_from_chunks(ys) if with_q else None)


def _ret_scan(log_g, inputs, state, q):
    k, v = inputs
    with_q = q is not None
    lg = log_g[:, None]
    pos = jnp.arange(CHUNK, dtype=jnp.float32)
    rel = pos[:, None] - pos[None, :]
    d_intra = jnp.exp(jnp.where(rel >= 0, lg[..., None] * rel, -jnp.inf))
    zeta = jnp.exp(lg * (CHUNK - 1 - pos))
    xi = jnp.exp(lg * (pos + 1))
    g_chunk = jnp.exp(log_g * CHUNK)
    xs = (_to_chunks(k), _to_chunks(v))
    if with_q:
        xs = xs + (_to_chunks(q),)

    def step(r_mem, inp):
        kc, vc = inp[0], inp[1]
        r_new = g_chunk[:, None, None] * r_mem + jnp.einsum("bhsk,bhsv->bhkv", kc * zeta[:, :, None], vc)
        if not with_q:
            return r_new, None
        qc = inp[2]
        scores = jnp.einsum("bhtk,bhsk->bhts", qc, kc) * d_intra
        out = jnp.einsum("bhts,bhsv->bhtv", scores, vc) + xi[:, :, None] * jnp.einsum("bhtk,bhkv->bhtv", qc, r_mem)
        return r_new, out

    state, ys = lax.scan(step, state, xs)
    return state, (_from_chunks(ys) if with_q else None)


def _chunk_sgu(u, v, w_s, b_s):
    b, l, _ = v.shape
    vc = _ln(v).reshape(b, l // CHUNK, CHUNK, A_GROUPS, A_GDIM)
    mixed = jnp.einsum("gpq,bnqgd->bnpgd", w_s, vc) + b_s.T[:, :, None]
    return u * mixed.reshape(b, l, A_DIM)


def _mlstm_in(p, qk_conv, gate_b, with_q):
    b, l, _ = p.shape
    k = _heads(jax.nn.silu(_dwconv1d(p[..., K_OFF:V_OFF], qk_conv[:, B_DIM:])), B_HEADS) * B_HEAD_DIM ** -0.5
    v = _heads(p[..., V_OFF:G_OFF], B_HEADS)
    gates = (p[..., G_OFF:Q_OFF].reshape(b, l, N_GATES, B_HEADS).astype(jnp.float32) + gate_b).transpose(0, 3, 1, 2)
    q = _heads(jax.nn.silu(_dwconv1d(p[..., Q_OFF:O_OFF], qk_conv[:, :B_DIM])), B_HEADS) if with_q else None
    return q, [k, v, gates]


def _ab_mixer(h_l, h_c, w_in, qk_conv, gate_b, sgu_w, sgu_b, head_g, w_out, ctx_out):
    p_l = h_l @ w_in
    p_c = h_c @ (w_in if ctx_out else w_in[:, :Q_OFF])
    q_l, in_l = _mlstm_in(p_l, qk_conv, gate_b, True)
    q_c, in_c = _mlstm_in(p_c, qk_conv, gate_b, ctx_out)
    b = h_l.shape[0]
    init = (jnp.zeros((b, B_HEADS, B_HEAD_DIM, B_HEAD_DIM), jnp.float32),
            jnp.zeros((b, B_HEADS, B_HEAD_DIM), jnp.float32),
            jnp.zeros((b, B_HEADS), jnp.float32))
    r_l, r_c = _bidir_scan(_mlstm_scan, init, q_l, in_l, q_c, in_c)

    def merge(p, r):
        u = jax.nn.gelu(p[..., U_OFF:VA_OFF])
        va = jax.nn.gelu(p[..., VA_OFF:])
        o = _heads(jax.nn.sigmoid(p[..., O_OFF:U_OFF]), B_HEADS)
        mem = _head_norm(o * r, head_g).astype(u.dtype)
        return jnp.concatenate([_chunk_sgu(u, va, sgu_w, sgu_b), mem], axis=-1) @ w_out

    return merge(p_l, r_l), (merge(p_c, r_c) if ctx_out else None)


def _ret_in(p, with_q):
    k = _heads(p[..., RK_OFF:RV_OFF], C_HEADS) * C_QK_DIM ** -0.5
    v = _heads(p[..., RV_OFF:RQ_OFF], C_HEADS)
    q = _heads(p[..., RQ_OFF:RG_OFF], C_HEADS) if with_q else None
    return q, [k, v]


def _ret_mixer(h_l, h_c, w_in, decay_logit, head_g, w_out, ctx_out):
    p_l = h_l @ w_in
    p_c = h_c @ (w_in if ctx_out else w_in[:, :RQ_OFF])
    q_l, in_l = _ret_in(p_l, True)
    q_c, in_c = _ret_in(p_c, ctx_out)
    log_g = jax.nn.log_sigmoid(decay_logit.astype(jnp.float32))
    init = jnp.zeros((h_l.shape[0], C_HEADS, C_QK_DIM, C_V_DIM), jnp.float32)
    r_l, r_c = _bidir_scan(lambda d, inp, st, q: _ret_scan(log_g[d], inp, st, q), init, q_l, in_l, q_c, in_c)

    def merge(p, r):
        return (jax.nn.silu(p[..., RG_OFF:]) * _head_norm(r, head_g)) @ w_out

    return merge(p_l, r_l), (merge(p_c, r_c) if ctx_out else None)


def setup_inputs(seed: int = 0) -> dict:
    key = jax.random.key(seed)
    ks = iter(jax.random.split(key, 32))

    def nrm(shape, scale):
        return jax.random.normal(next(ks), shape, jnp.float32) * scale

    gamma0 = 1.0 - 2.0 ** (-5.0 - jnp.arange(C_HEADS, dtype=jnp.float32))
    decay_logit0 = jnp.log(gamma0) - jnp.log1p(-gamma0)
    gate_base = jnp.array([0.0, 4.0, 0.0, 4.0], jnp.float32)[:, None]
    return {
        "x": nrm((BATCH, SEQ, D_MODEL), 1.0),
        "c": nrm((BATCH, D_MODEL), 1.0),
        "ctx": nrm((BATCH, CTX_LEN, D_MODEL), 1.0),
        "c_ctx": nrm((D_MODEL,), 1.0),
        "ada_w": nrm((DEPTH, D_MODEL, N_MOD * D_MODEL), 0.5 * D_MODEL ** -0.5),
        "ada_b": nrm((DEPTH, N_MOD * D_MODEL), 0.02),
        "pre_g": 1.0 + nrm((DEPTH, 2, D_MODEL), 0.05),
        "post_g": 1.0 + nrm((DEPTH, 2, D_MODEL), 0.05),
        "ffn_up": nrm((DEPTH, D_MODEL, 2 * D_FF), D_MODEL ** -0.5),
        "ffn_conv": nrm((DEPTH, 3, 3, D_FF), 1.0 / 3.0),
        "ffn_down": nrm((DEPTH, D_FF, D_MODEL), D_FF ** -0.5),
        "ab_w_in": nrm((N_EVEN, D_MODEL, AB_PROJ), D_MODEL ** -0.5),
        "ab_qk_conv": nrm((N_EVEN, 3, 2 * B_DIM), 3.0 ** -0.5),
        "ab_gate_b": gate_base + nrm((N_EVEN, N_GATES, B_HEADS), 0.1),
        "ab_sgu_w": nrm((N_EVEN, A_GROUPS, CHUNK, CHUNK), CHUNK ** -0.5),
        "ab_sgu_b": nrm((N_EVEN, A_GROUPS, CHUNK), 0.02),
        "ab_head_g": 1.0 + nrm((N_EVEN, B_DIM), 0.05),
        "ab_w_out": nrm((N_EVEN, MIX_DIM, D_MODEL), MIX_DIM ** -0.5),
        "ret_w_in": nrm((N_ODD, D_MODEL, C_PROJ), D_MODEL ** -0.5),
        "ret_decay": decay_logit0 + nrm((N_ODD, 2, C_HEADS), 0.1),
        "ret_head_g": 1.0 + nrm((N_ODD, C_HEADS * C_V_DIM), 0.05),
        "ret_w_out": nrm((N_ODD, C_HEADS * C_V_DIM, D_MODEL), (C_HEADS * C_V_DIM) ** -0.5),
    }


def reference(x, c, ctx, c_ctx, ada_w, ada_b, pre_g, post_g, ffn_up, ffn_conv, ffn_down,
              ab_w_in, ab_qk_conv, ab_gate_b, ab_sgu_w, ab_sgu_b, ab_head_g, ab_w_out,
              ret_w_in, ret_decay, ret_head_g, ret_w_out):
    s_lat = jax.nn.silu(c)
    s_ctx = jax.nn.silu(c_ctx)
    for layer in range(DEPTH):
        last = layer == DEPTH - 1
        j = layer // 2
        m_l = [t[:, None, :] for t in jnp.split(s_lat @ ada_w[layer] + ada_b[layer], N_MOD, axis=-1)]
        m_c = jnp.split(s_ctx @ ada_w[layer] + ada_b[layer], N_MOD, axis=-1)
        h_l = _rms(x, pre_g[layer, 0]) * (1.0 + m_l[1]) + m_l[0]
        h_c = _rms(ctx, pre_g[layer, 0]) * (1.0 + m_c[1]) + m_c[0]
        if layer % 2 == 0:
            y_l, y_c = _ab_mixer(h_l, h_c, ab_w_in[j], ab_qk_conv[j], ab_gate_b[j], ab_sgu_w[j], ab_sgu_b[j],
                                 ab_head_g[j], ab_w_out[j], not last)
        else:
            y_l, y_c = _ret_mixer(h_l, h_c, ret_w_in[j], ret_decay[j], ret_head_g[j], ret_w_out[j], not last)
        x = x + (m_l[2] * _rms(y_l, post_g[layer, 0])).astype(x.dtype)
        f_l = _conv_glu(_rms(x, pre_g[layer, 1]) * (1.0 + m_l[4]) + m_l[3],
                        ffn_up[layer], ffn_conv[layer], ffn_down[layer], True)
        x = x + (m_l[5] * _rms(f_l, post_g[layer, 1])).astype(x.dtype)
        if not last:
            ctx = ctx + (m_c[2] * _rms(y_c, post_g[layer, 0])).astype(ctx.dtype)
            f_c = _conv_glu(_rms(ctx, pre_g[layer, 1]) * (1.0 + m_c[4]) + m_c[3],
                            ffn_up[layer], ffn_conv[layer], ffn_down[layer], False)
            ctx = ctx + (m_c[5] * _rms(f_c, post_g[layer, 1])).astype(ctx.dtype)
    return x
```

```python
from contextlib import ExitStack
import math
import numpy as np
import concourse.bass as bass
import concourse.mybir as mybir
from concourse.bass_utils import run_bass_kernel_spmd

F32 = mybir.dt.float32
BF16 = mybir.dt.bfloat16
AF = mybir.ActivationFunctionType
ALU = mybir.AluOpType
AX = mybir.AxisListType

D = 1024
SEQ = 2048
CTX = 256
T = SEQ + CTX
NT = T // 128
DFF = 2816
NJ = DFF // 128
EPS = 1e-6
AB_PROJ = 3088
C_PROJ = 6144
N_CORES = 8
REORDER = True
FOLD_WAITS = True
POOL_TAP = False


def _esize(dt):
    return 4 if dt == F32 else 2


def _free_elems(ap):
    n = 1
    for d in ap.shape[1:]:
        n *= d
    return n


class Sched:
    LAT = 120.0

    def __init__(self, nc, stack, n_ld=24, n_st=16):
        self.nc = nc
        self.eng = {"pe": nc.tensor, "act": nc.scalar, "dve": nc.vector, "pool": nc.gpsimd, "sp": nc.sync}
        self.sem = {}
        for e in ("pe", "act", "dve", "pool"):
            self.sem[e] = stack.enter_context(nc.semaphore("s_" + e))
        self.dq = {"sp": [], "pool": []}
        for i in range(n_ld):
            k = "ld%d" % i
            self.sem[k] = stack.enter_context(nc.semaphore(k))
            self.dq["sp"].append(k)
        for i in range(n_st):
            k = "st%d" % i
            self.sem[k] = stack.enter_context(nc.semaphore(k))
            self.dq["pool"].append(k)
        self.nodes = []
        self.last_w = {}
        self.readers = {}
        self.rel_node = None
        self.nwaits = 0
        self.ninst = 0
        self.cnt = {e: 0 for e in self.eng}

    @staticmethod
    def keys(x):
        if isinstance(x, (str, tuple)):
            return [x]
        if not hasattr(x, "tensor"):
            return [x.name]
        name = x.tensor.name
        if name != "psall":
            return [name]
        es = _esize(x.dtype)
        off = (x.offset * es) % 16384
        ext = es
        for (stp, cnt) in x.ap[1:]:
            ext += (cnt - 1) * abs(stp) * es
        return [("ps", k) for k in range(off // 2048, (off + ext - 1) // 2048 + 1)]

    def _flat(self, lst):
        out = []
        for x in lst:
            if x is None:
                continue
            for k in self.keys(x):
                if k not in out:
                    out.append(k)
        return out

    def issue(self, e, fn, reads=(), writes=(), dma=False, cost=100.0, lat=0.0, fold=False):
        reads = self._flat(reads)
        writes = self._flat(writes)
        nid = len(self.nodes)
        preds = {}
        for r in reads:
            w = self.last_w.get(r)
            if w is not None:
                preds[w] = True
        for r in reads:
            if isinstance(r, tuple) and r[0] == "ps" and r not in writes:
                writes.append(r)
        for w in writes:
            p = self.last_w.get(w)
            if p is not None and p not in preds:
                preds[p] = False
            for rd in self.readers.get(w, ()):
                if rd not in preds:
                    preds[rd] = False
        preds.pop(nid, None)
        self.nodes.append([e, fn, dma, float(cost), float(lat), list(preds.items()), bool(fold) and FOLD_WAITS])
        for w in writes:
            self.last_w[w] = nid
            self.readers[w] = []
        for r in reads:
            if r not in writes:
                self.readers.setdefault(r, []).append(nid)
        return nid

    def mark(self, name):
        if not hasattr(self, "marks"):
            self.marks = []
        self.marks.append((name, len(self.nodes)))

    def release(self, names):
        preds = {}
        if self.rel_node is not None:
            preds[self.rel_node] = False
        for n in names:
            for k in self.keys(n):
                w = self.last_w.pop(k, None)
                if w is not None:
                    preds[w] = False
                for rd in self.readers.pop(k, ()):
                    preds[rd] = False
        nid = len(self.nodes)
        self.nodes.append([None, None, False, 0.0, 0.0, list(preds.items()), False])
        self.rel_node = nid

    def adopt(self, names):
        if self.rel_node is None:
            return
        for n in names:
            for k in self.keys(n):
                self.readers[k] = [self.rel_node]

    def schedule(self):
        import heapq
        N = len(self.nodes)
        npred = [0] * N
        succ = [[] for _ in range(N)]
        for i, nd in enumerate(self.nodes):
            npred[i] = len(nd[5])
            for (p, _) in nd[5]:
                succ[p].append(i)
        ready_t = [0.0] * N
        start = [0.0] * N
        fin = [0.0] * N
        efree = {e: 0.0 for e in self.eng}
        rq = {e: [] for e in self.eng}
        pend = []
        done = 0

        def make_ready(i, t):
            nd = self.nodes[i]
            if nd[0] is None:
                finish(i, t, t)
            else:
                heapq.heappush(rq[nd[0]], i)
                ready_t[i] = t

        def finish(i, ts, tf):
            nonlocal done
            start[i] = ts
            fin[i] = tf
            done += 1
            for sc in succ[i]:
                npred[sc] -= 1
                if ready_t[sc] < tf + self.LAT:
                    ready_t[sc] = tf + self.LAT
                if npred[sc] == 0:
                    heapq.heappush(pend, (ready_t[sc], sc))

        for i in range(N):
            if npred[i] == 0:
                heapq.heappush(pend, (0.0, i))
        now = 0.0
        while done < N:
            while pend and pend[0][0] <= now:
                t, i = heapq.heappop(pend)
                make_ready(i, t)
            if done >= N:
                break
            progressed = False
            for e in self.eng:
                if efree[e] <= now and rq[e]:
                    i = heapq.heappop(rq[e])
                    nd = self.nodes[i]
                    ts = now
                    efree[e] = ts + nd[3]
                    finish(i, ts, ts + nd[3] + nd[4])
                    progressed = True
            if progressed:
                continue
            cand = [efree[e] for e in self.eng if rq[e] and efree[e] > now]
            if pend:
                cand.append(pend[0][0])
            if not cand:
                raise RuntimeError("scheduler stuck (cyclic dependencies?)")
            now = max(now, min(cand))
        self.sim_ns = max(fin) if N else 0.0
        self.sim_start, self.sim_fin = start, fin
        if not REORDER:
            return list(range(N))
        return sorted(range(N), key=lambda i: (start[i], i))

    def emit(self):
        order = self.schedule()
        known = {e: {} for e in self.eng}
        ev = [None] * len(self.nodes)
        dval = {k: 0 for q in self.dq.values() for k in q}
        dclk = {k: {} for q in self.dq.values() for k in q}
        dnext = {"sp": 0, "pool": 0}

        self.trace = {e: [] for e in self.eng}
        self.nfold = 0

        pend_w = []

        def wait(e, s, v, clk):
            kn = known[e]
            if s is not None and kn.get(s, 0) < v:
                pend_w.append((s, v))
                self.nwaits += 1
                kn[s] = v
            for k2, v2 in clk.items():
                if kn.get(k2, 0) < v2:
                    kn[k2] = v2

        def flush(e, keep_last):
            last = None
            if keep_last and pend_w:
                last = pend_w.pop()
            for (s_, v_) in pend_w:
                self.eng[e].wait_ge(self.sem[s_], v_)
                self.trace[e].append(("w", s_, v_))
            del pend_w[:]
            return last

        for i in order:
            e, fn, dma, cost, lat, preds, fold = self.nodes[i]
            if e is None:
                clk = {}
                for (p, _) in preds:
                    s, v, c2 = ev[p]
                    if s is not None and clk.get(s, 0) < v:
                        clk[s] = v
                    if s is None:
                        for k2, v2 in c2.items():
                            if clk.get(k2, 0) < v2:
                                clk[k2] = v2
                ev[i] = (None, 0, clk)
                continue
            own = None if dma else e
            for (p, raw) in preds:
                s, v, clk = ev[p]
                if s is None:
                    for k2, v2 in clk.items():
                        if k2 == own and e == "pe":
                            continue
                        if known[e].get(k2, 0) < v2:
                            pend_w.append((k2, v2))
                            self.nwaits += 1
                            known[e][k2] = v2
                    continue
                if s == own and (e == "pe" or not raw):
                    continue
                wait(e, s, v, clk)
            if dma:
                q = self.dq[e]
                k = q[dnext[e] % len(q)]
                dnext[e] += 1
                if dval[k] > 0:
                    wait(e, k, dval[k], dclk[k])
                flush(e, False)
                ins = fn()
                dval[k] += 16
                ins.then_inc(self.sem[k], 16)
                self.trace[e].append(("i", k, 16))
                clk = dict(known[e])
                dclk[k] = clk
                ev[i] = (k, dval[k], clk)
            else:
                last = flush(e, fold)
                ins = fn()
                if last is not None:
                    ins._wait_ge(self.sem[last[0]], last[1])
                    self.trace[e].append(("w", last[0], last[1]))
                    self.nfold += 1
                self.cnt[e] += 1
                ins.then_inc(self.sem[e], 1)
                self.trace[e].append(("i", e, 1))
                ev[i] = (e, self.cnt[e], dict(known[e]))
            self.ninst += 1
        for k, v in dval.items():
            if v > 0 and known["sp"].get(k, 0) < v:
                self.eng["sp"].wait_ge(self.sem[k], v)
        for e in ("pe", "act", "dve", "pool"):
            if self.cnt[e] > 0:
                self.eng["sp"].wait_ge(self.sem[e], self.cnt[e])

    def finish(self):
        self.emit()


class PSView:
    def __init__(self, arena, bank, n):
        self.arena = arena
        self.c0 = bank * 512
        self.n = n

    def __getitem__(self, idx):
        rows, cols = idx
        a = 0 if cols.start is None else cols.start
        b = self.n * 512 if cols.stop is None else cols.stop
        if cols.step is None:
            return self.arena[rows, self.c0 + a:self.c0 + b]
        return self.arena[rows, self.c0 + a:self.c0 + b:cols.step]


class Bld:
    def __init__(self, nc, st):
        self.nc = nc
        self.S = Sched(nc, st)
        self.arena = st.enter_context(nc.psum_tensor("psall", [128, 4096], F32))
        self.ps_i = 0
        self.uid = 0
        probe = nc.alloc_sbuf_tensor("sb_probe", [128, 8], F32)
        self.sb_top = (nc.lookup_mloc(probe).addr + 32 + 31) // 32 * 32
        self.sb_limit = nc.SBUF_PARTITION_SIZE_BYTES
        self.sb_peak = self.sb_top

    def psum(self, n=1):
        if n == 2 and self.ps_i % 2 == 1:
            self.ps_i += 1
        v = PSView(self.arena, self.ps_i % 8, n)
        self.ps_i += n
        return v

    class Scope:
        def __init__(self, b):
            self.b = b
            self.st = ExitStack()
            self.names = []

        def __enter__(self):
            self.top0 = self.b.sb_top
            return self

        def sb(self, name, shape, dt):
            self.b.uid += 1
            nbytes = _esize(dt)
            for d in shape[1:]:
                nbytes *= d
            nbytes = (nbytes + 31) // 32 * 32
            off = self.b.sb_top
            assert off + nbytes <= self.b.sb_limit, "SBUF overflow: %s needs %d at %d" % (name, nbytes, off)
            t = self.b.nc.alloc_sbuf_tensor_at("%s_%d" % (name, self.b.uid), list(shape), dt, offset=off)
            self.b.sb_top = off + nbytes
            self.b.sb_peak = max(self.b.sb_peak, self.b.sb_top)
            self.names.append(t)
            self.b.S.adopt([t])
            return t

        def __exit__(self, *a):
            self.b.S.release(self.names)
            self.b.sb_top = self.top0
            return False

    def scope(self):
        return Bld.Scope(self)

    def _k(self, aps, override):
        return list(override) if override is not None else [a for a in aps if a is not None and not isinstance(a, (int, float))]

    def mm(self, out, lhsT, rhs, start=True, stop=True, rk=None, wk=None):
        nc = self.nc
        n = _free_elems(rhs)
        c = (max(n, 64) / 2.4 * (4.0 if rhs.dtype == F32 else 1.0) + 8) * 1.2
        return self.S.issue("pe", lambda: nc.tensor.matmul(out, lhsT=lhsT, rhs=rhs, start=start, stop=stop),
                            reads=self._k([lhsT, rhs], rk), writes=self._k([out], wk), cost=c, lat=60)

    def tr(self, out, in_, ident, rk=None, wk=None):
        nc = self.nc
        c = 128 / 2.4 * (2.0 if in_.dtype == F32 else 1.0) + 8
        return self.S.issue("pe", lambda: nc.tensor.transpose(out, in_, ident),
                            reads=self._k([in_, ident], rk), writes=self._k([out], wk), cost=c, lat=60)

    def act(self, out, in_, func, bias=0.0, scale=1.0, accum=None, rk=None, wk=None):
        nc = self.nc
        kw = {}
        if accum is not None:
            kw["accum_out"] = accum

        def f():
            return nc.scalar.activation(out=out, in_=in_, func=func, bias=bias, scale=scale, **kw)
        c = (200 + _free_elems(in_) / 1.4 + (100 if accum is not None else 0)) * 1.2
        return self.S.issue("act", f, reads=self._k([in_, bias, scale], rk), writes=self._k([out, accum], wk), cost=c, lat=60, fold=(accum is None))

    def _vc(self, e, ap):
        n = _free_elems(ap)
        return (90 + n / 0.96) * 1.14 if e == "dve" else (150 + n * 1.65)

    def tt(self, e, out, a, b, op, rk=None, wk=None):
        eng = self.S.eng[e]
        return self.S.issue(e, lambda: eng.tensor_tensor(out=out, in0=a, in1=b, op=op),
                            reads=self._k([a, b], rk), writes=self._k([out], wk), cost=self._vc(e, out), lat=60, fold=True)

    def ts(self, e, out, a, s1, op0, s2=None, op1=None, rk=None, wk=None):
        eng = self.S.eng[e]

        def f():
            if op1 is None:
                return eng.tensor_scalar(out=out, in0=a, scalar1=s1, scalar2=None, op0=op0)
            return eng.tensor_scalar(out=out, in0=a, scalar1=s1, scalar2=s2, op0=op0, op1=op1)
        return self.S.issue(e, f, reads=self._k([a, s1, s2], rk), writes=self._k([out], wk), cost=self._vc(e, out), lat=60, fold=True)

    def stt(self, out, a, s, b, op0, op1, rk=None, wk=None):
        nc = self.nc
        return self.S.issue("dve", lambda: nc.vector.scalar_tensor_tensor(out=out, in0=a, scalar=s, in1=b, op0=op0, op1=op1),
                            reads=self._k([a, s, b], rk), writes=self._k([out], wk), cost=self._vc("dve", out), lat=60, fold=True)

    def cp(self, e, out, in_, rk=None, wk=None):
        nc = self.nc
        if e == "act":
            f = lambda: nc.scalar.copy(out=out, in_=in_)
            c = 200 + _free_elems(out) / 1.4
        else:
            eng = self.S.eng[e]
            f = lambda: eng.tensor_copy(out=out, in_=in_)
            c = self._vc(e, out)
        return self.S.issue(e, f, reads=self._k([in_], rk), writes=self._k([out], wk), cost=c, lat=60, fold=True)

    def red(self, out, in_, op=ALU.add, rk=None, wk=None):
        nc = self.nc
        return self.S.issue("dve", lambda: nc.vector.tensor_reduce(out=out, in_=in_, axis=AX.X, op=op),
                            reads=self._k([in_], rk), writes=self._k([out], wk), cost=self._vc("dve", in_), lat=60, fold=True)

    def recip(self, out, in_):
        nc = self.nc
        return self.S.issue("dve", lambda: nc.vector.reciprocal(out=out, in_=in_), reads=[in_], writes=[out], cost=self._vc("dve", out), lat=60, fold=True)

    def memset(self, e, ap, val):
        eng = self.S.eng[e]
        return self.S.issue(e, lambda: eng.memset(ap, val), writes=[ap], cost=self._vc(e, ap), lat=60, fold=True)

    def ld(self, out, in_, rk=None, wk=None):
        nc = self.nc
        nbytes = out.shape[0] * _free_elems(out) * _esize(out.dtype)
        return self.S.issue("sp", lambda: nc.sync.dma_start(out=out, in_=in_), reads=self._k([], rk), writes=self._k([out], wk), dma=True,
                            cost=70, lat=2200 + nbytes / 160.0)

    def stq(self, out, in_, rk=None, wk=None):
        nc = self.nc
        nbytes = in_.shape[0] * _free_elems(in_) * _esize(in_.dtype)
        return self.S.issue("pool", lambda: nc.gpsimd.dma_start(out=out, in_=in_), reads=self._k([in_], rk), writes=self._k([], wk), dma=True,
                            cost=600, lat=2500 + nbytes / 160.0)

    def rstd(self, out, ss, n, tmp):
        self.act(tmp, ss, AF.Sqrt, bias=self.eps_ap, scale=1.0 / n)
        self.recip(out, tmp)


def build_program(NS, dbg=False, layers=(0, 1)):
    nc = bass.Bass("TRN2", target_bir_lowering=False)
    R = NS + 1

    def din(name, shape):
        return nc.dram_tensor(name, list(shape), F32, kind="ExternalInput").ap()

    x_in = din("x", [NS, SEQ, D]); c_in = din("c", [NS, D]); ctx_in = din("ctx", [NS, CTX, D]); cctx_in = din("c_ctx", [1, D])
    ada_w = din("ada_w", [2, D, 6 * D]); ada_b = din("ada_b", [2, 6 * D]); pre_g = din("pre_g", [4, D]); post_g = din("post_g", [4, D])
    ffn_up = din("ffn_up", [2, D, 2 * DFF]); ffn_conv = din("ffn_conv", [2 * 9 * NJ, 128]); ffn_down = din("ffn_down", [2, DFF, D])
    ab_w_in = din("ab_w_in", [D, AB_PROJ]); ab_qk_conv = din("ab_qk_conv", [24, 128]); ab_gate_b = din("ab_gate_b", [1, 16])
    ab_sgu_w = din("ab_sgu_w", [4, 128, 128]); ab_sgu_b = din("ab_sgu_b", [4, 128]); ab_head_g = din("ab_head_g", [1, 512])
    ab_w_out = din("ab_w_out", [D, D]); ret_w_in = din("ret_w_in", [D, C_PROJ]); ret_decay = din("ret_decay", [1, 8])
    ret_head_g = din("ret_head_g", [1, 2048]); ret_w_out = din("ret_w_out", [2048, D])
    out_t = nc.dram_tensor("out", [NS, SEQ, D], F32, kind="ExternalOutput").ap()

    skind = "ExternalOutput" if dbg else "Internal"

    def dscr(name, shape, dt, k=None):
        return nc.dram_tensor(name, list(shape), dt, kind=k or "Internal").ap()

    w_in0b = dscr("w_in0b", [D, AB_PROJ], BF16); w_out0b = dscr("w_out0b", [D, D], BF16)
    w_in1b = dscr("w_in1b", [D, C_PROJ], BF16); w_out1b = dscr("w_out1b", [2048, D], BF16)
    up_r = dscr("up_r", [2, NJ, 128, 8, 2, 128], BF16); down_b = dscr("down_b", [2, DFF, D], BF16)
    gsc = dscr("gsc", [2, 2, R, D], F32)
    xm_s = dscr("xm_s", [NS, T, D], F32, skind)
    xo_s = dscr("xo_s", [NS, T, D], F32, skind)
    yb_s = dscr("yb_s", [2, NT, 128, 512], F32)
    og_s = dscr("og_s", [2, NT, 128, 512], F32)
    sgu_s = dscr("sgu_s", [2, NT, 128, 512], BF16)
    act_s = dscr("act_s", [2, NJ, 128, T], BF16)
    yf_s = dscr("yf_s", [2, NT, 128, 512], F32)
    z_s = dscr("z_s", [2, SEQ, 2048], BF16)

    with ExitStack() as st:
        B = Bld(nc, st)
        S = B.S
        G = B.scope()
        st.enter_context(G)
        ident_b = G.sb("ident_b", [128, 128], BF16); ident_f = G.sb("ident_f", [128, 128], F32)
        triU = G.sb("triU", [128, 128], F32); triL = G.sb("triL", [128, 128], F32); ones_f = G.sb("ones_f", [128, 128], F32)
        mskU = G.sb("mskU", [128, 128], BF16); mskL = G.sb("mskL", [128, 128], BF16)
        B.memset("pool", ones_f[:, :], 1.0)
        B.memset("pool", triU[:, :], 1.0); B.memset("pool", triL[:, :], 1.0); B.memset("pool", ident_f[:, :], 1.0)
        S.issue("pool", lambda: nc.gpsimd.affine_select(out=triU[:, :], in_=triU[:, :], pattern=[[1, 128]], compare_op=ALU.is_ge, fill=0.0, base=0, channel_multiplier=-1), reads=[triU], writes=[triU])
        S.issue("pool", lambda: nc.gpsimd.affine_select(out=triL[:, :], in_=triL[:, :], pattern=[[-1, 128]], compare_op=ALU.is_ge, fill=0.0, base=0, channel_multiplier=1), reads=[triL], writes=[triL])
        B.tt("pool", ident_f[:, :], triU[:, :], triL[:, :], ALU.mult)
        B.cp("pool", ident_b[:, :], ident_f[:, :]); B.cp("pool", mskU[:, :], triU[:, :]); B.cp("pool", mskL[:, :], triL[:, :])

        hT = G.sb("hT", [128, 8, T], BF16)
        sT = G.sb("sT", [128, 8, R], F32)
        modsT = G.sb("modsT", [128, 2, 48, R], F32)
        pre_gT = G.sb("pre_gT", [128, 32], F32)
        qkcT = G.sb("qkcT", [128, 24], F32)
        fcvT = G.sb("fcvT", [128, 2 * 9 * NJ], F32)
        sgbT = G.sb("sgbT", [128, 4], F32)
        g1c = G.sb("g1c", [128, 2, 2, R, 8], F32)
        gateb = G.sb("gateb", [128, 16], F32)
        lgd = G.sb("lgd", [128, 8], F32)
        wsT = G.sb("wsT", [128, 4, 128], BF16)

        def rowsT(dst, src_ap, nrows, P):
            with B.scope() as sc:
                stg = sc.sb("rt_stg", [128, 128], F32)
                for r0 in range(0, nrows, 128):
                    n = min(128, nrows - r0)
                    B.ld(stg[0:n, :], src_ap[r0:r0 + n, :])
                    ps = B.psum()
                    B.tr(ps[:, 0:n], stg[0:n, :], ident_f[0:n, 0:n])
                    B.cp("dve", dst[:, r0:r0 + n], ps[:, 0:n])

        cast_cnt = [0]
        cast_gate = [[]]

        def cast_mat(src, dst, key, stf, stb, W):
            for r0 in range(0, src.shape[0], 128):
                for c0 in range(0, src.shape[1], W):
                    n = min(W, src.shape[1] - c0)
                    i = cast_cnt[0] % len(stf)
                    cast_cnt[0] += 1
                    B.ld(stf[i][:, 0:n], src[r0:r0 + 128, c0:c0 + n], rk=cast_gate[0])
                    B.cp(("act", "dve", "pool")[cast_cnt[0] % 3], stb[i][:, 0:n], stf[i][:, 0:n])
                    B.stq(dst[r0:r0 + 128, c0:c0 + n], stb[i][:, 0:n], wk=[key])

        def cast_up(l, stf, stb, JB):
            for kc in range(8):
                for half in range(2):
                    for j0 in range(0, NJ, JB):
                        nj = min(JB, NJ - j0)
                        n = nj * 128
                        i = cast_cnt[0] % len(stf)
                        cast_cnt[0] += 1
                        B.ld(stf[i][:, 0:n], ffn_up[l, kc * 128:(kc + 1) * 128, half * DFF + j0 * 128:half * DFF + j0 * 128 + n], rk=cast_gate[0])
                        B.cp(("act", "dve", "pool")[cast_cnt[0] % 3], stb[i][:, 0:n], stf[i][:, 0:n])
                        B.stq(up_r[l, j0:j0 + nj, :, kc, half, :].rearrange("j p c -> p j c"),
                              stb[i][:, 0:n].rearrange("p (j c) -> p j c", c=128), wk=[("up_r", l)])

        def deferred_casts(stf, stb, W, JB):
            cast_gate[0] = [("defer_gate",)] if 0 in layers else []
            if 1 in layers:
                cast_mat(ret_w_in, w_in1b, ("w_in1b",), stf, stb, W)
                cast_mat(ret_w_out, w_out1b, ("w_out1b",), stf, stb, W)
                cast_up(1, stf, stb, JB)
                cast_mat(ffn_down[1], down_b[1], ("down_b", 1), stf, stb, W)

        dgate = G.sb("dgate", [128, 1], F32)
        dcf = [G.sb("dcf%d" % i, [128, 1024], F32) for i in range(2)]
        dcb = [G.sb("dcb%d" % i, [128, 1024], BF16) for i in range(2)]

        S.mark("prologue")
        with B.scope() as P:
            rowsT(pre_gT, pre_g.rearrange("a (c p) -> (a c) p", p=128), 32, P)
            rowsT(qkcT, ab_qk_conv, 24, P)
            rowsT(fcvT, ffn_conv, 2 * 9 * NJ, P)
            rowsT(sgbT, ab_sgu_b, 4, P)
            B.ld(gateb[:, :], ab_gate_b[0, :].partition_broadcast(128))
            B.ld(lgd[:, :], ret_decay[0, :].partition_broadcast(128))
            wstg = P.sb("wstg", [128, 128], F32)
            for g in range(4):
                B.ld(wstg[:, :], ab_sgu_w[g, :, :])
                ps = B.psum()
                B.tr(ps[:, 0:128], wstg[:, :], ident_f[:, :])
                B.cp("act", wsT[:, g, :], ps[:, 0:128])
            crow = P.sb("crow", [R, D], F32)
            B.ld(crow[0:NS, :], c_in[:, :])
            B.ld(crow[NS:R, :], cctx_in[:, :])
            srow = P.sb("srow", [R, D], F32)
            B.act(srow[:, :], crow[:, :], AF.Silu)
            ps = B.psum()
            for kc in range(8):
                B.tr(ps[:, kc * 8:kc * 8 + R], srow[:, kc * 128:(kc + 1) * 128], ident_f[0:R, 0:R])
            B.cp("dve", sT[:, :, :], ps[:, 0:64].rearrange("p (k r) -> p k r", r=8)[:, :, 0:R])
            adbT = P.sb("adbT", [128, 96], F32)
            rowsT(adbT, ada_b.rearrange("l (c p) -> (l c) p", p=128), 96, P)
            adw = [P.sb("adw%d" % i, [128, 8, D], F32) for i in range(2)]
            brow = P.sb("brow", [R, D], F32); grow = P.sb("grow", [R, D], F32); prow = P.sb("prow", [R, D], F32)
            for l in range(2):
                for m in range(6):
                    w = adw[(l * 6 + m) % 2]
                    B.ld(w[:, :, :], ada_w[l, :, m * D:(m + 1) * D].rearrange("(k p) n -> p k n", p=128))
                    ps = B.psum()
                    for fc in range(8):
                        for kc in range(8):
                            B.mm(ps[:, fc * 8:fc * 8 + R], w[:, kc, fc * 128:(fc + 1) * 128], sT[:, kc, :], start=(kc == 0), stop=(kc == 7))
                    for fc in range(8):
                        B.ts("dve", modsT[:, l, m * 8 + fc, :], ps[:, fc * 8:fc * 8 + R], adbT[:, l * 48 + m * 8 + fc:l * 48 + m * 8 + fc + 1], ALU.add)
                    if m in (2, 5):
                        which = 0 if m == 2 else 1
                        ps2 = B.psum(2)
                        for half in range(2):
                            for kc in range(8):
                                B.mm(ps2[0:R, half * 512:(half + 1) * 512], sT[:, kc, :], w[:, kc, half * 512:(half + 1) * 512], start=(kc == 0), stop=(kc == 7))
                        B.ld(brow[:, :], ada_b[l, m * D:(m + 1) * D].partition_broadcast(R))
                        B.ld(prow[:, :], post_g[l * 2 + which, :].partition_broadcast(R))
                        B.tt("dve", grow[:, :], ps2[0:R, :], brow[:, :], ALU.add)
                        B.tt("dve", grow[:, :], grow[:, :], prow[:, :], ALU.mult)
                        B.stq(gsc[l, which, :, :], grow[:, :], wk=[("gsc", l, which)])
                for which in range(2):
                    msc = 1 if which == 0 else 4
                    for r in range(R):
                        B.stt(g1c[:, l, which, r, :], modsT[:, l, msc * 8:(msc + 1) * 8, r], 1.0,
                              pre_gT[:, (l * 2 + which) * 8:(l * 2 + which) * 8 + 8], ALU.add, ALU.mult)

            NSTG = 4
            cst_f = [P.sb("cst_f%d" % i, [128, 2048], F32) for i in range(NSTG)]
            cst_b = [P.sb("cst_b%d" % i, [128, 2048], BF16) for i in range(NSTG)]
            if 0 in layers:
                cast_mat(ab_w_in, w_in0b, ("w_in0b",), cst_f, cst_b, 2048)
                cast_mat(ab_w_out, w_out0b, ("w_out0b",), cst_f, cst_b, 2048)
                cast_up(0, cst_f, cst_b, 11)
                cast_mat(ffn_down[0], down_b[0], ("down_b", 0), cst_f, cst_b, 2048)
            if 0 not in layers:
                deferred_casts(cst_f, cst_b, 2048, 11)

        def src_tile(layer, b, i):
            if layer == 0:
                if i < 2:
                    return ctx_in[b, i * 128:(i + 1) * 128, :], None
                return x_in[b, (i - 2) * 128:(i - 1) * 128, :], None
            return xo_s[b, i * 128:(i + 1) * 128, :], ("xo", b, i)

        def prenorm_to_hT(W, xt, layer, which, row, i, tagk):
            junk, ss, tmp1, rs, xn = W["junk"], W["ss"], W["tmp1"], W["rs"], W["xn"]
            B.act(junk[:, :], xt[:, :], AF.Square, accum=ss[:, 0:1])
            B.rstd(rs[:, 0:1], ss[:, 0:1], D, tmp1[:, 0:1])
            B.ts("dve", xn[:, :], xt[:, :], rs[:, 0:1], ALU.mult)
            msh = 0 if which == 0 else 3
            ps = B.psum(2)
            for kc in range(8):
                B.tr(ps[:, kc * 128:(kc + 1) * 128], xn[:, kc * 128:(kc + 1) * 128], ident_f[:, :])
            for kc in range(8):
                if kc < 4:
                    B.act(hT[:, kc, i * 128:(i + 1) * 128], ps[:, kc * 128:(kc + 1) * 128], AF.Identity,
                          bias=modsT[:, layer, msh * 8 + kc, row:row + 1], scale=g1c[:, layer, which, row, kc:kc + 1],
                          wk=[("hT", i)])
                else:
                    B.ts("dve", hT[:, kc, i * 128:(i + 1) * 128], ps[:, kc * 128:(kc + 1) * 128],
                         g1c[:, layer, which, row, kc:kc + 1], ALU.mult, modsT[:, layer, msh * 8 + kc, row:row + 1], ALU.add,
                         wk=[("hT", i)])

        def post_residual(W, psY, xt, gt, layer, out_ap, out_key, xm):
            junk, ss, tmp1, rs, t1 = W["junk"], W["ss"], W["tmp1"], W["rs"], W["t1"]
            B.act(junk[:, :], psY[:, :], AF.Square, accum=ss[:, 1:2])
            B.rstd(rs[:, 1:2], ss[:, 1:2], D, tmp1[:, 1:2])
            B.stt(t1[:, :], psY[:, :], rs[:, 1:2], gt[:, :], ALU.mult, ALU.mult)
            B.tt("pool", xm[:, :], t1[:, :], xt[:, :], ALU.add)
            B.stq(out_ap, xm[:, :], wk=[out_key])

        def ffn(layer, b, tiles):
            par = b % 2
            has_ctx = 0 in tiles
            S.mark("L%d ffn_up" % layer)
            with B.scope() as F:
                gp = [F.sb("gp%d" % i, [128, 34, 66], F32) for i in range(2)]
                gc_ = [F.sb("gc", [128, 258], F32) for _i in range(2)]
                ptap_ = [F.sb("ptap", [128, SEQ], F32) for _i in range(2)] if POOL_TAP else [None, None]
                ab_ = [F.sb("ab%d" % i, [128, T], F32) for i in range(2)]
                acc_ = [F.sb("acc", [128, T], F32) for _i in range(2)]
                gl_ = [F.sb("gl", [128, T], F32) for _i in range(2)]
                ao = [F.sb("ao%d" % i, [128, T], BF16) for i in range(2)]
                wj = [F.sb("wj%d" % i, [128, 8, 256], BF16) for i in range(2)]
                for i in range(2):
                    B.memset("pool", gp[i][:, :, :], 0.0)
                for _i in range(2):
                    B.memset("pool", gc_[_i][:, :], 0.0)
                blocks = ([(0, 256)] if has_ctx else []) + [(256 + 512 * k, 512) for k in range(4)]
                for j in range(NJ):
                    w = wj[j % 2]
                    B.ld(w[:, :, :], up_r[layer, j, :, :, :, :].rearrange("p k h c -> p k (h c)"), rk=[("up_r", layer)])
                    gpj, abj, aoj = gp[j % 2], ab_[j % 2], ao[j % 2]
                    acc, gl, gc, ptap = acc_[j % 2], gl_[j % 2], gc_[j % 2], ptap_[j % 2]
                    for (t0, n) in blocks:
                        ps = B.psum(2)
                        rk = [w] + [("hT", t0 // 128 + q) for q in range(n // 128)]
                        for kc in range(8):
                            B.mm(ps[:, 0:n], w[:, kc, 128:256], hT[:, kc, t0:t0 + n], start=(kc == 0), stop=(kc == 7), rk=rk)
                        for kc in range(8):
                            B.mm(ps[:, 512:512 + n], w[:, kc, 0:128], hT[:, kc, t0:t0 + n], start=(kc == 0), stop=(kc == 7), rk=rk)
                        if t0 == 0:
                            B.cp("act", gc[:, 1:257], ps[:, 0:256])
                        else:
                            r0 = (t0 - 256) // 64
                            B.cp("act", gpj[:, 1 + r0:9 + r0, 1:65], ps[:, 0:512].rearrange("p (r c) -> p r c", c=64))
                        B.cp("act", abj[:, t0:t0 + n], ps[:, 512:512 + n])
                    accv = acc[:, 256:T].rearrange("p (r c) -> p r c", c=64)
                    first = True
                    for ty in range(3):
                        for tx in range(3):
                            wcol = fcvT[:, (layer * 9 + ty * 3 + tx) * NJ + j:(layer * 9 + ty * 3 + tx) * NJ + j + 1]
                            src = gpj[:, ty:ty + 32, tx:tx + 64]
                            if POOL_TAP and ty == 2 and tx == 2:
                                B.ts("pool", ptap[:, :].rearrange("p (r c) -> p r c", c=64), src, wcol, ALU.mult)
                                continue
                            if first:
                                B.act(accv, src, AF.Copy, scale=wcol)
                                first = False
                            else:
                                B.stt(accv, src, wcol, accv, ALU.mult, ALU.add)
                    if has_ctx:
                        for tx in range(3):
                            wcol = fcvT[:, (layer * 9 + 3 + tx) * NJ + j:(layer * 9 + 3 + tx) * NJ + j + 1]
                            if tx == 0:
                                B.act(acc[:, 0:256], gc[:, 0:256], AF.Copy, scale=wcol)
                            else:
                                B.stt(acc[:, 0:256], gc[:, tx:tx + 256], wcol, acc[:, 0:256], ALU.mult, ALU.add)
                    t00 = 0 if has_ctx else 256
                    if POOL_TAP:
                        B.tt("pool", acc[:, 256:T], acc[:, 256:T], ptap[:, :], ALU.add)
                    B.act(gl[:, t00:T], acc[:, t00:T], AF.Gelu_apprx_tanh)
                    B.tt("pool", aoj[:, t00:T], gl[:, t00:T], abj[:, t00:T], ALU.mult)
                    B.stq(act_s[par, j, :, t00:T], aoj[:, t00:T], wk=[("act_s", par, j)])
            S.mark("L%d ffn_down" % layer)
            with B.scope() as F:
                wd = F.sb("wd", [128, NJ, D], BF16)
                B.ld(wd[:, :, :], down_b[layer].rearrange("(j p) n -> p j n", p=128), rk=[("down_b", layer)])
                ablk = [F.sb("ablk%d" % i, [128, NJ, 256], BF16) for i in range(2)]
                g5 = F.sb("g5", [128, D], F32); g5c = F.sb("g5c", [128, D], F32)
                B.ld(g5[:, :], gsc[layer, 1, b, :].partition_broadcast(128), rk=[("gsc", layer, 1)])
                if has_ctx:
                    B.ld(g5c[:, :], gsc[layer, 1, NS, :].partition_broadcast(128), rk=[("gsc", layer, 1)])
                Wks = [{"junk": F.sb("f_junk", [128, D], BF16), "ss": F.sb("f_ss", [128, 2], F32), "tmp1": F.sb("f_tmp1", [128, 2], F32),
                        "rs": F.sb("f_rs", [128, 2], F32), "t1": F.sb("f_t1", [128, D], F32)} for _i in range(2)]
                xts = [F.sb("f_xt%d" % i, [128, D], F32) for i in range(2)]
                xos = [F.sb("f_xo%d" % i, [128, D], F32) for i in range(2)]
                nb = 0
                for t0 in range(0 if has_ctx else 256, T, 256):
                    blk = ablk[nb % 2]
                    nb += 1
                    B.ld(blk[:, :, :], act_s[par, :, :, t0:t0 + 256].rearrange("j p t -> p j t"), rk=[("act_s", par, j) for j in range(NJ)])
                    for q in range(2):
                        i = t0 // 128 + q
                        xt = xts[i % 2]; xo = xos[i % 2]
                        B.ld(xt[:, :], xm_s[b, i * 128:(i + 1) * 128, :], rk=[("xm", b, i)])
                        ps = B.psum(2)
                        for half in range(2):
                            for j in range(NJ):
                                B.mm(ps[:, half * 512:(half + 1) * 512], blk[:, j, q * 128:(q + 1) * 128], wd[:, j, half * 512:(half + 1) * 512],
                                     start=(j == 0), stop=(j == NJ - 1))
                        if layer == 1:
                            oap, okey = out_t[b, (i - 2) * 128:(i - 1) * 128, :], ("out", b, i)
                        else:
                            oap, okey = xo_s[b, i * 128:(i + 1) * 128, :], ("xo", b, i)
                        post_residual(Wks[i % 2], ps, xt, g5c if i < 2 else g5, layer, oap, okey, xo)

        def layer0(b):
            par = b % 2
            with B.scope() as M:
                KT = M.sb("KT", [128, 4, T], BF16); QT = M.sb("QT", [128, 4, T], BF16)
                Vx = M.sb("Vx", [128, NT, 4, 129], BF16)
                Gt = M.sb("Gt", [128, NT, 16], F32)
                EA = M.sb("EA", [128, NT, 8], F32); EAW = M.sb("EAW", [128, NT, 8], F32)
                EB = M.sb("EB", [128, NT, 8], F32); EBT = M.sb("EBT", [128, NT, 8], F32)
                C32 = M.sb("C32", [128, 8, 129], F32); Cbf = M.sb("Cbf", [128, 8, 129], BF16)
                Wks, xts = LW["Wks"], LW["xts"]
                NX = len(xts)
                yb = [M.sb("yb%d" % i, [128, 512], F32) for i in range(2)]
                PT = [M.sb("PT%d" % i, [128, 4, 128], BF16) for i in range(2)]
                Kw = [M.sb("Kw%d" % i, [128, 4, 128], BF16) for i in range(2)]
                dn_ = [M.sb("dn", [128, 4], F32) for _i in range(2)]; r2_ = [M.sb("r2", [128, 4], F32) for _i in range(2)]
                S.mark("L0 P1")
                for i in range(NT):
                    xt = xts[i % NX]
                    sap, skey = src_tile(0, b, i)
                    B.ld(xt[:, :], sap, rk=[skey] if skey else [])
                    prenorm_to_hT(Wks[i % 2], xt, 0, 0, NS if i < 2 else b, i, None)
                allh = [("hT", i) for i in range(NT)]
                S.mark("L0 P2a")
                with B.scope() as P2:
                    Wa = P2.sb("Wa", [128, 8, 1024], BF16)
                    B.ld(Wa[:, :, 0:512], w_in0b[:, 0:512].rearrange("(k p) n -> p k n", p=128), rk=[("w_in0b",)])
                    B.ld(Wa[:, :, 512:1024], w_in0b[:, 1040:1552].rearrange("(k p) n -> p k n", p=128), rk=[("w_in0b",)])
                    raw = [P2.sb("raw%d" % i, [128, T + 4], F32) for i in range(2)]
                    cacc = P2.sb("cacc", [128, T + 4], F32)
                    for i in range(2):
                        B.memset("pool", raw[i][:, :], 0.0)
                    for fc in range(8):
                        col0 = fc * 128
                        cch = 4 + fc if fc < 4 else fc - 4
                        rw = raw[fc % 2]
                        blocks = [(0, 256, 1)] + [(256 + 512 * k, 512, 259 + 512 * k) for k in range(4)]
                        for bi in range(0, len(blocks), 2):
                            ps = B.psum(2)
                            for q, (t0, n, off) in enumerate(blocks[bi:bi + 2]):
                                for kc in range(8):
                                    B.mm(ps[:, q * 512:q * 512 + n], Wa[:, kc, col0:col0 + 128], hT[:, kc, t0:t0 + n],
                                         start=(kc == 0), stop=(kc == 7), rk=[Wa] + allh)
                                B.cp("act", rw[:, off:off + n], ps[:, q * 512:q * 512 + n])
                        L = T + 4
                        B.ts("dve", cacc[:, 1:L - 1], rw[:, 0:L - 2], qkcT[:, cch:cch + 1], ALU.mult)
                        B.stt(cacc[:, 1:L - 1], rw[:, 1:L - 1], qkcT[:, 8 + cch:9 + cch], cacc[:, 1:L - 1], ALU.mult, ALU.add)
                        B.stt(cacc[:, 1:L - 1], rw[:, 2:L], qkcT[:, 16 + cch:17 + cch], cacc[:, 1:L - 1], ALU.mult, ALU.add)
                        dst = KT if fc < 4 else QT
                        B.act(dst[:, fc % 4, 0:256], cacc[:, 1:257], AF.Silu)
                        B.act(dst[:, fc % 4, 256:T], cacc[:, 259:259 + SEQ], AF.Silu)
                S.mark("L0 P2b")
                with B.scope() as P2:
                    Wr = P2.sb("Wr", [128, 8, 2064], BF16)
                    B.ld(Wr[:, :, 0:528], w_in0b[:, 512:1040].rearrange("(k p) n -> p k n", p=128), rk=[("w_in0b",)])
                    B.ld(Wr[:, :, 528:2064], w_in0b[:, 1552:3088].rearrange("(k p) n -> p k n", p=128), rk=[("w_in0b",)])
                    ogt = [P2.sb("ogt%d" % i, [128, 512], F32) for i in range(2)]
                    sgt = [P2.sb("sgt%d" % i, [128, 512], BF16) for i in range(2)]
                    u_ = [P2.sb("u", [128, 512], F32) for _i in range(2)]; va_ = [P2.sb("va", [128, 512], F32) for _i in range(2)]
                    sq_ = [P2.sb("sq", [128, 512], BF16) for _i in range(2)]
                    vc_ = [P2.sb("vc", [128, 512], BF16) for _i in range(2)]; s1_ = [P2.sb("s1", [128, 4], F32) for _i in range(2)]
                    def gates_and_prep():
                        S.mark("L0 gates")
                        psg = B.psum(1)
                        for i in range(NT):
                            for kc in range(8):
                                B.mm(psg[:, i * 16:(i + 1) * 16], hT[:, kc, i * 128:(i + 1) * 128], Wr[:, kc, 512:528], start=(kc == 0), stop=(kc == 7), rk=[Wr, ("hT", i)])
                        B.tt("dve", Gt[:, :, :], psg[:, 0:NT * 16].rearrange("p (t g) -> p t g", g=16), gateb[:, :].unsqueeze(1).to_broadcast([128, NT, 16]), ALU.add)
                        E1 = P2.sb("E1", [128, NT, 8], F32); Pn = P2.sb("Pn", [128, NT, 8], F32); A1 = P2.sb("A1", [128, NT, 8], F32); A2 = P2.sb("A2", [128, NT, 8], F32)
                        gv = Gt[:, :, :].rearrange("p t (g h) -> p t g h", h=4)
                        for d in range(2):
                            B.act(E1[:, :, d * 4:(d + 1) * 4], gv[:, :, 1 + 2 * d, :], AF.Exp, scale=-1.0)
                        B.act(Pn[:, :, :], E1[:, :, :], AF.Ln, bias=1.0)
                        ps = B.psum(1)
                        B.mm(ps[:, 0:72], triU[:, :], Pn[:, :, 0:4])
                        B.mm(ps[:, 72:144], triL[:, :], Pn[:, :, 4:8])
                        B.mm(ps[:, 144:288], ones_f[:, :], Pn[:, :, :])
                        for d in range(2):
                            B.tt("dve", A1[:, :, d * 4:(d + 1) * 4], gv[:, :, 2 * d, :], ps[:, d * 72:(d + 1) * 72].rearrange("p (t h) -> p t h", h=4), ALU.add)
                        B.tt("dve", A2[:, :, :], A1[:, :, :], ps[:, 144:288].rearrange("p (t j) -> p t j", j=8), ALU.subtract)
                        B.act(EA[:, :, :], A1[:, :, :], AF.Exp, bias=LNS_ap[:, 0:1])
                        B.act(EAW[:, :, :], A2[:, :, :], AF.Exp, bias=LNS_ap[:, 0:1])
                        for d in range(2):
                            B.act(EB[:, :, d * 4:(d + 1) * 4], ps[:, d * 72:(d + 1) * 72].rearrange("p (t h) -> p t h", h=4), AF.Exp, scale=-1.0)
                        B.act(EBT[:, :, :], ps[:, 144:288].rearrange("p (t j) -> p t j", j=8), AF.Exp, scale=-1.0)
                        if b == 0:
                            B.cp("pool", dgate[:, :], EBT[:, 0, 0:1], wk=[dgate, ("defer_gate",)])

                    B.memset("pool", Vx[:, :, :, :], 1.0)
                    for i in range(NT):
                        tk = slice(i * 128, (i + 1) * 128)
                        psV = B.psum(1)
                        for kc in range(8):
                            B.mm(psV[:, 0:512], hT[:, kc, tk], Wr[:, kc, 0:512], start=(kc == 0), stop=(kc == 7), rk=[Wr, ("hT", i)])
                        B.cp("act" if i % 2 == 0 else "dve", Vx[:, i, :, 0:128], psV[:, 0:512].rearrange("p (h d) -> p h d", h=4))
                    gates_and_prep()
                    for i in range(NT):
                        tk = slice(i * 128, (i + 1) * 128)
                        rk = [Wr, ("hT", i)]
                        u, va, sq, vc, s1 = u_[i % 2], va_[i % 2], sq_[i % 2], vc_[i % 2], s1_[i % 2]
                        psO = B.psum(2); psU = B.psum(2)
                        for (dst, c0, n) in ((psO[:, 0:512], 528, 512),
                                             (psO[:, 512:1024], 1040, 512), (psU[:, 0:512], 1552, 512)):
                            for kc in range(8):
                                B.mm(dst, hT[:, kc, tk], Wr[:, kc, c0:c0 + n], start=(kc == 0), stop=(kc == 7), rk=rk)
                        B.act(ogt[i % 2][:, :], psO[:, 0:512], AF.Sigmoid)
                        B.stq(og_s[par, i, :, :], ogt[i % 2][:, :], wk=[("og", par, i)])
                        B.act(u[:, :], psO[:, 512:1024], AF.Gelu_apprx_tanh)
                        B.act(va[:, :], psU[:, 0:512], AF.Gelu_apprx_tanh, accum=s1[:, 0:1])
                        B.ts("dve", s1[:, 1:2], s1[:, 0:1], -1.0 / 512, ALU.mult)
                        B.act(sq[:, :], va[:, :], AF.Square, bias=s1[:, 1:2], accum=s1[:, 2:3])
                        B.rstd(s1[:, 3:4], s1[:, 2:3], 512, s1[:, 2:3])
                        B.ts("dve", vc[:, :], va[:, :], s1[:, 1:2], ALU.add, s1[:, 3:4], ALU.mult)
                        for g in range(4):
                            B.mm(psU[:, 512 + g * 128:512 + (g + 1) * 128], wsT[:, g, :], vc[:, g * 128:(g + 1) * 128])
                        for g in range(4):
                            B.stt(sgt[i % 2][:, g * 128:(g + 1) * 128], psU[:, 512 + g * 128:512 + (g + 1) * 128], sgbT[:, g:g + 1],
                                  u[:, g * 128:(g + 1) * 128], ALU.add, ALU.mult)
                        B.stq(sgu_s[par, i, :, :], sgt[i % 2][:, :], wk=[("sgu", par, i)])
                step = [0]
                cur_r2 = [None]

                def scan_step(d, c):
                    k = step[0] % 2
                    step[0] += 1
                    dn, r2 = dn_[k], r2_[k]
                    cur_r2[0] = r2
                    msk = mskU if d == 0 else mskL
                    tk = slice(c * 128, (c + 1) * 128)
                    psA = B.psum(1); psAt = B.psum(1)
                    psAb = psAt[:, 0:512].bitcast(BF16)
                    for h in range(4):
                        B.mm(psA[:, h * 128:(h + 1) * 128], KT[:, h, tk], QT[:, h, tk])
                    for h in range(4):
                        B.tr(psAb[:, h * 128:(h + 1) * 128], KT[:, h, tk], ident_b[:, :])
                    for h in range(4):
                        B.stt(PT[k][:, h, :], psA[:, h * 128:(h + 1) * 128], EA[:, c, d * 4 + h:d * 4 + h + 1], msk[:, :], ALU.mult, ALU.mult)
                    for h in range(4):
                        B.act(Kw[k][:, h, :], psAb[:, h * 128:(h + 1) * 128], AF.Copy, scale=EAW[:, c, d * 4 + h:d * 4 + h + 1])
                    psB = B.psum(2)
                    for h in range(4):
                        B.mm(psB[:, h * 256:h * 256 + 129], PT[k][:, h, :], Vx[:, c, h, :], start=True, stop=False)
                        B.mm(psB[:, h * 256:h * 256 + 129], QT[:, h, tk], Cbf[:, d * 4 + h, :], start=False, stop=True)
                    B.tt("dve", dn[:, :], psB[:, 128::256], EB[:, c, d * 4:d * 4 + 4], ALU.mult)
                    B.act(dn[:, :], dn[:, :], AF.Abs)
                    B.ts("dve", dn[:, :], dn[:, :], 1.0, ALU.max)
                    B.recip(dn[:, :], dn[:, :])
                    B.tt("dve", r2[:, :], dn[:, :], EB[:, c, d * 4:d * 4 + 4], ALU.mult)
                    psD = B.psum(2)
                    for h in range(4):
                        B.mm(psD[:, h * 256:h * 256 + 129], Kw[k][:, h, :], Vx[:, c, h, :])
                    for h in range(4):
                        j = d * 4 + h
                        B.stt(C32[:, j, :], C32[:, j, :], EBT[:, c, j:j + 1], psD[:, h * 256:h * 256 + 129], ALU.mult, ALU.add)
                    B.cp("act", Cbf[:, d * 4:d * 4 + 4, :], C32[:, d * 4:d * 4 + 4, :])
                    return psB

                B.memset("pool", C32[:, :, :], 0.0)
                B.memset("pool", Cbf[:, :, :], 0.0)
                S.mark("L0 bwd")
                for c in [1, 0] + list(range(NT - 1, 1, -1)):
                    psB = scan_step(1, c)
                    y = yb[c % 2]
                    B.tt("dve", y[:, :].rearrange("p (h d) -> p h d", h=4), psB[:, :].rearrange("p (h d) -> p h d", d=256)[:, :, 0:128],
                         cur_r2[0][:, :].unsqueeze(2).to_broadcast([128, 4, 128]), ALU.mult)
                    B.stq(yb_s[par, c, :, :], y[:, :], wk=[("yb", par, c)])

                S.mark("L0 fwd+P4")
                with B.scope() as P4:
                    Wo = P4.sb("Wo", [128, 8, D], BF16)
                    B.ld(Wo[:, :, :], w_out0b.rearrange("(k p) n -> p k n", p=128), rk=[("w_out0b",)])
                    g2 = P4.sb("g2", [128, D], F32)
                    B.ld(g2[:, :], gsc[0, 0, NS, :].partition_broadcast(128), rk=[("gsc", 0, 0)])
                    hg = P4.sb("hg", [128, 512], F32)
                    B.ld(hg[:, :], ab_head_g[0, :].partition_broadcast(128))
                    ybl = [P4.sb("ybl%d" % i, [128, 512], F32) for i in range(2)]
                    ogl = [P4.sb("ogl%d" % i, [128, 512], F32) for i in range(2)]
                    ys_ = [P4.sb("ys", [128, 512], F32) for _i in range(2)]; sq4_ = [P4.sb("sq4", [128, 512], F32) for _i in range(2)]
                    st4_ = [P4.sb("st4", [128, 4], F32) for _i in range(2)]; st4b_ = [P4.sb("st4b", [128, 4], F32) for _i in range(2)]
                    st4c_ = [P4.sb("st4c", [128, 4], F32) for _i in range(2)]
                    cat = [P4.sb("cat%d" % i, [128, D], BF16) for i in range(2)]
                    catT_ = [P4.sb("catT", [128, 8, 128], BF16) for _i in range(2)]
                    for c in range(NT):
                        row = NS if c < 2 else b
                        if c == 2:
                            B.ld(g2[:, :], gsc[0, 0, b, :].partition_broadcast(128), rk=[("gsc", 0, 0)])
                        ct = cat[c % 2]
                        ys, sq, st4, st4b, st4c, catT = ys_[c % 2], sq4_[c % 2], st4_[c % 2], st4b_[c % 2], st4c_[c % 2], catT_[c % 2]
                        B.ld(ybl[c % 2][:, :], yb_s[par, c, :, :], rk=[("yb", par, c)])
                        B.ld(ogl[c % 2][:, :], og_s[par, c, :, :], rk=[("og", par, c)])
                        B.ld(ct[:, 0:512], sgu_s[par, c, :, :], rk=[("sgu", par, c)])
                        psB = scan_step(0, c)
                        B.tt("dve", ys[:, :].rearrange("p (h d) -> p h d", h=4), psB[:, :].rearrange("p (h d) -> p h d", d=256)[:, :, 0:128],
                             cur_r2[0][:, :].unsqueeze(2).to_broadcast([128, 4, 128]), ALU.mult)
                        B.tt("pool", ys[:, :], ys[:, :], ybl[c % 2][:, :], ALU.add)
                        B.tt("pool", ys[:, :], ys[:, :], ogl[c % 2][:, :], ALU.mult)
                        z3 = ys[:, :].rearrange("p (h d) -> p h d", h=4)
                        B.red(st4[:, :], z3)
                        B.ts("dve", st4[:, :], st4[:, :], 1.0 / 128, ALU.mult)
                        B.tt("dve", z3, z3, st4[:, :].unsqueeze(2).to_broadcast([128, 4, 128]), ALU.subtract)
                        B.tt("pool", sq[:, :], ys[:, :], ys[:, :], ALU.mult)
                        B.red(st4b[:, :], sq[:, :].rearrange("p (h d) -> p h d", h=4))
                        B.rstd(st4c[:, :], st4b[:, :], 128, st4b[:, :])
                        B.tt("dve", z3, z3, st4c[:, :].unsqueeze(2).to_broadcast([128, 4, 128]), ALU.mult)
                        B.tt("pool", ct[:, 512:1024], ys[:, :], hg[:, :], ALU.mult)
                        psT = B.psum(1)
                        psTb = psT[:, :].bitcast(BF16)
                        for kc in range(8):
                            B.tr(psTb[:, kc * 128:(kc + 1) * 128], ct[:, kc * 128:(kc + 1) * 128], ident_b[:, :])
                        B.cp("act", catT[:, :, :], psTb[:, 0:1024].rearrange("p (k t) -> p k t", t=128))
                        psY = B.psum(2)
                        for half in range(2):
                            for kc in range(8):
                                B.mm(psY[:, half * 512:(half + 1) * 512], catT[:, kc, :], Wo[:, kc, half * 512:(half + 1) * 512], start=(kc == 0), stop=(kc == 7))
                        xt = xts[c % NX]
                        sap, skey = src_tile(0, b, c)
                        B.ld(xt[:, :], sap, rk=[skey] if skey else [])
                        post_residual(Wks[c % 2], psY, xt, g2, 0, xm_s[b, c * 128:(c + 1) * 128, :], ("xm", b, c), xt)
                        prenorm_to_hT(Wks[c % 2], xt, 0, 1, row, c, None)
            ffn(0, b, list(range(NT)))

        EPS_t = G.sb("EPS_t", [128, 1], F32)
        B.memset("pool", EPS_t[:, :], EPS)
        B.eps_ap = EPS_t[:, 0:1]
        LNS_ap = G.sb("LNS_ap", [128, 1], F32)
        B.memset("pool", LNS_ap[:, :], math.log(128.0 ** -0.5))

        lg = G.sb("lg", [128, 8], F32); e1 = G.sb("e1", [128, 8], F32); pp1 = G.sb("pp1", [128, 1], F32)
        xi = G.sb("xi", [128, 8], F32); zeta = G.sb("zeta", [128, 8], F32); colA = G.sb("colA", [128, 8], F32); g128 = G.sb("g128", [128, 8], F32)
        rtmp = G.sb("rtmp", [128, 8], F32)
        DmT = G.sb("DmT", [128, 8, 128], BF16)
        if 1 in layers:
            LNK = math.log(256.0 ** -0.5)
            B.act(rtmp[:, :], lgd[:, :], AF.Exp, scale=-1.0)
            B.act(lg[:, :], rtmp[:, :], AF.Ln, bias=1.0)
            B.ts("dve", lg[:, :], lg[:, :], -1.0, ALU.mult)
            ps = B.psum()
            B.mm(ps[:, 0:1], triU[:, :], ones_f[:, 0:1])
            B.cp("dve", pp1[:, :], ps[:, 0:1])
            B.ts("dve", e1[:, :], lg[:, :], pp1[:, 0:1], ALU.mult)
            B.act(g128[:, :], lg[:, :], AF.Exp, scale=128.0)
            B.act(xi[:, 0:4], e1[:, 0:4], AF.Exp)
            B.ts("dve", rtmp[:, 0:4], e1[:, 0:4], -1.0, ALU.mult, LNK, ALU.add)
            B.act(colA[:, 0:4], rtmp[:, 0:4], AF.Exp)
            B.stt(rtmp[:, 0:4], lg[:, 0:4], 128.0, e1[:, 0:4], ALU.mult, ALU.subtract)
            B.ts("dve", rtmp[:, 0:4], rtmp[:, 0:4], LNK, ALU.add)
            B.act(zeta[:, 0:4], rtmp[:, 0:4], AF.Exp)
            B.stt(rtmp[:, 4:8], lg[:, 4:8], 129.0, e1[:, 4:8], ALU.mult, ALU.subtract)
            B.act(xi[:, 4:8], rtmp[:, 4:8], AF.Exp)
            B.ts("dve", rtmp[:, 4:8], rtmp[:, 4:8], -1.0, ALU.mult, LNK, ALU.add)
            B.act(colA[:, 4:8], rtmp[:, 4:8], AF.Exp)
            B.stt(rtmp[:, 4:8], lg[:, 4:8], -1.0, e1[:, 4:8], ALU.mult, ALU.add)
            B.ts("dve", rtmp[:, 4:8], rtmp[:, 4:8], LNK, ALU.add)
            B.act(zeta[:, 4:8], rtmp[:, 4:8], AF.Exp)
            for j in range(8):
                B.ts("dve", DmT[:, j, :], (triU if j < 4 else triL)[:, :], colA[:, j:j + 1], ALU.mult)

        def layer1(b):
            par = b % 2
            with B.scope() as M:
                Wks, xts = LW["Wks"], LW["xts"]
                S.mark("L1 P1")
                for i in range(NT):
                    xt = xts[i % len(xts)]
                    sap, skey = src_tile(1, b, i)
                    B.ld(xt[:, :], sap, rk=[skey])
                    prenorm_to_hT(Wks[i % 2], xt, 1, 0, NS if i < 2 else b, i, None)
                allh = [("hT", i) for i in range(NT)]
                for h in range(4):
                    with B.scope() as H:
                        Wh = H.sb("Wh", [128, 8, 1536], BF16)
                        for (d0, s0, n) in ((0, h * 256, 256), (256, 1024 + h * 512, 512), (768, 3072 + h * 256, 256), (1024, 4096 + h * 512, 512)):
                            B.ld(Wh[:, :, d0:d0 + n], w_in1b[:, s0:s0 + n].rearrange("(k p) n -> p k n", p=128), rk=[("w_in1b",)])
                        KTh = H.sb("KTh", [128, 2, T], BF16); QTh = H.sb("QTh", [128, 2, T], BF16)
                        Vh = H.sb("Vh", [128, NT, 512], BF16)
                        R32_ = [H.sb("R32", [128, 2, 512], F32) for _i in range(2)]; Rbf_ = [H.sb("Rbf", [128, 2, 512], BF16) for _i in range(2)]
                        PT = [H.sb("PT%d" % i, [128, 128], BF16) for i in range(2)]
                        Kz = [H.sb("Kz%d" % i, [128, 256], BF16) for i in range(2)]
                        hgh = H.sb("hgh", [128, 512], F32)
                        B.ld(hgh[:, :], ret_head_g[0, h * 512:(h + 1) * 512].partition_broadcast(128))
                        S.mark("L1 h%d proj" % h)
                        for fc in range(4):
                            col0 = fc * 128 if fc < 2 else 768 + (fc - 2) * 128
                            dst = KTh if fc < 2 else QTh
                            blocks = ([(0, 256)] if fc < 2 else []) + [(256 + 512 * k, 512) for k in range(4)]
                            for bi in range(0, len(blocks), 2):
                                ps = B.psum(2)
                                for q, (t0, n) in enumerate(blocks[bi:bi + 2]):
                                    for kc in range(8):
                                        B.mm(ps[:, q * 512:q * 512 + n], Wh[:, kc, col0:col0 + 128], hT[:, kc, t0:t0 + n],
                                             start=(kc == 0), stop=(kc == 7), rk=[Wh] + allh)
                                    B.cp("act" if q == 0 else "dve", dst[:, fc % 2, t0:t0 + n], ps[:, q * 512:q * 512 + n])
                        for i in range(NT):
                            ps = B.psum()
                            for kc in range(8):
                                B.mm(ps[:, 0:512], hT[:, kc, i * 128:(i + 1) * 128], Wh[:, kc, 256:768], start=(kc == 0), stop=(kc == 7), rk=[Wh, ("hT", i)])
                            B.cp("act" if i % 2 == 0 else "dve", Vh[:, i, :], ps[:, 0:512])
                        step = [0]

                        def scan_step(d, c, with_q):
                            k = step[0] % 2
                            step[0] += 1
                            j = d * 4 + h
                            R32, Rbf = R32_[d], Rbf_[d]
                            tk = slice(c * 128, (c + 1) * 128)
                            psA = B.psum(1); psAt = B.psum(1)
                            psAb = psAt[:, 0:512].bitcast(BF16)
                            if with_q:
                                for kc in range(2):
                                    B.mm(psA[:, 0:128], KTh[:, kc, tk], QTh[:, kc, tk], start=(kc == 0), stop=(kc == 1))
                            for kc in range(2):
                                B.tr(psAb[:, kc * 128:(kc + 1) * 128], KTh[:, kc, tk], ident_b[:, :])
                            psB = None
                            if with_q:
                                B.tt("dve", PT[k][:, :], psA[:, 0:128], DmT[:, j, :], ALU.mult)
                            B.act(Kz[k][:, :], psAb[:, 0:256], AF.Copy, scale=zeta[:, j:j + 1])
                            if with_q:
                                psB = B.psum(1)
                                B.mm(psB[:, 0:512], PT[k][:, :], Vh[:, c, :], start=True, stop=False)
                                B.mm(psB[:, 0:512], QTh[:, 0, tk], Rbf[:, 0, :], start=False, stop=False)
                                B.mm(psB[:, 0:512], QTh[:, 1, tk], Rbf[:, 1, :], start=False, stop=True)
                            psD = B.psum(2)
                            for kc in range(2):
                                B.mm(psD[:, kc * 512:(kc + 1) * 512], Kz[k][:, kc * 128:(kc + 1) * 128], Vh[:, c, :])
                            B.stt(R32[:, :, :], R32[:, :, :], g128[:, j:j + 1], psD[:, :].rearrange("p (k v) -> p k v", k=2), ALU.mult, ALU.add)
                            B.cp("act", Rbf[:, :, :], R32[:, :, :])
                            return psB

                        S.mark("L1 h%d scan" % h)
                        with B.scope() as PB:
                            yb = [PB.sb("yb%d" % i, [128, 512], F32) for i in range(4)]
                            for d in range(2):
                                B.memset("pool", R32_[d][:, :, :], 0.0); B.memset("pool", Rbf_[d][:, :, :], 0.0)
                            bw = [1, 0] + list(range(NT - 1, 1, -1))
                            for kk in range(NT):
                                for d, c in ((1, bw[kk]), (0, kk)):
                                    psB = scan_step(d, c, c >= 2)
                                    if c >= 2:
                                        y = yb[(2 * kk + d) % 4]
                                        B.act(y[:, :], psB[:, 0:512], AF.Copy, scale=xi[:, d * 4 + h:d * 4 + h + 1])
                                        if d == 1:
                                            B.stq(yb_s[par, c, :, :], y[:, :], wk=[("yb", par, c)])
                                        else:
                                            B.stq(yf_s[par, c, :, :], y[:, :], wk=[("yf", par, c)])
                        S.mark("L1 h%d merge" % h)
                        with B.scope() as PF:
                            ND = 3
                            ybl = [PF.sb("ybl%d" % i, [128, 512], F32) for i in range(ND)]
                            yfl = [PF.sb("yfl%d" % i, [128, 512], F32) for i in range(ND)]
                            ys_ = [PF.sb("ys", [128, 512], F32) for _i in range(ND)]; sqj_ = [PF.sb("sqj", [128, 512], BF16) for _i in range(ND)]
                            sg_ = [PF.sb("sg", [128, 512], F32) for _i in range(ND)]
                            zt = [PF.sb("zt%d" % i, [128, 512], BF16) for i in range(ND)]
                            s1_ = [PF.sb("s1r", [128, 4], F32) for _i in range(ND)]
                            for c in range(2, NT):
                                tk = slice(c * 128, (c + 1) * 128)
                                ys, sqj, sg, s1 = ys_[c % ND], sqj_[c % ND], sg_[c % ND], s1_[c % ND]
                                B.ld(ybl[c % ND][:, :], yb_s[par, c, :, :], rk=[("yb", par, c)])
                                B.ld(yfl[c % ND][:, :], yf_s[par, c, :, :], rk=[("yf", par, c)])
                                B.tt("dve", ys[:, :], yfl[c % ND][:, :], ybl[c % ND][:, :], ALU.add)
                                psG = B.psum()
                                for kc in range(8):
                                    B.mm(psG[:, 0:512], hT[:, kc, tk], Wh[:, kc, 1024:1536], start=(kc == 0), stop=(kc == 7), rk=[Wh, ("hT", c)])
                                B.act(sg[:, :], psG[:, 0:512], AF.Silu)
                                B.tt("pool", sg[:, :], sg[:, :], hgh[:, :], ALU.mult)
                                B.red(s1[:, 0:1], ys[:, :])
                                B.ts("dve", s1[:, 1:2], s1[:, 0:1], -1.0 / 512, ALU.mult)
                                B.act(sqj[:, :], ys[:, :], AF.Square, bias=s1[:, 1:2], accum=s1[:, 2:3])
                                B.rstd(s1[:, 3:4], s1[:, 2:3], 512, s1[:, 2:3])
                                B.ts("dve", ys[:, :], ys[:, :], s1[:, 1:2], ALU.add, s1[:, 3:4], ALU.mult)
                                B.tt("dve", zt[c % ND][:, :], ys[:, :], sg[:, :], ALU.mult)
                                B.stq(z_s[par, (c - 2) * 128:(c - 1) * 128, h * 512:(h + 1) * 512], zt[c % ND][:, :], wk=[("z", par, c)])
                S.mark("L1 P4")
                with B.scope() as P4:
                    Wo = P4.sb("Wo1", [128, 16, D], BF16)
                    B.ld(Wo[:, :, :], w_out1b.rearrange("(k p) n -> p k n", p=128), rk=[("w_out1b",)])
                    g2 = P4.sb("g2", [128, D], F32)
                    B.ld(g2[:, :], gsc[1, 0, b, :].partition_broadcast(128), rk=[("gsc", 1, 0)])
                    zl = [P4.sb("zl%d" % i, [128, 2048], BF16) for i in range(2)]
                    zT_ = [P4.sb("zT", [128, 16, 128], BF16) for _i in range(2)]
                    for c in range(2, NT):
                        z = zl[c % 2]
                        zT = zT_[c % 2]
                        B.ld(z[:, :], z_s[par, (c - 2) * 128:(c - 1) * 128, :], rk=[("z", par, c)])
                        psT = B.psum(2)
                        psTb = psT[:, :].bitcast(BF16)
                        for kc in range(16):
                            B.tr(psTb[:, kc * 128:(kc + 1) * 128], z[:, kc * 128:(kc + 1) * 128], ident_b[:, :])
                        B.cp("act", zT[:, :, :], psTb[:, :].rearrange("p (k t) -> p k t", t=128))
                        psY = B.psum(2)
                        for half in range(2):
                            for kc in range(16):
                                B.mm(psY[:, half * 512:(half + 1) * 512], zT[:, kc, :], Wo[:, kc, half * 512:(half + 1) * 512], start=(kc == 0), stop=(kc == 15))
                        xt = xts[c % len(xts)]
                        sap, skey = src_tile(1, b, c)
                        B.ld(xt[:, :], sap, rk=[skey])
                        post_residual(Wks[c % 2], psY, xt, g2, 1, xm_s[b, c * 128:(c + 1) * 128, :], ("xm", b, c), xt)
                        prenorm_to_hT(Wks[c % 2], xt, 1, 1, b, c, None)
            ffn(1, b, list(range(2, NT)))

        LW = {}

        def layer_ws(sc, nx):
            Wks = []
            for _i in range(2):
                _w = {"junk": sc.sb("junk", [128, D], BF16), "ss": sc.sb("ss", [128, 2], F32), "tmp1": sc.sb("tmp1", [128, 2], F32),
                      "rs": sc.sb("rs", [128, 2], F32), "xn": sc.sb("xn", [128, D], F32)}
                _w["t1"] = _w["xn"]
                Wks.append(_w)
            LW["Wks"] = Wks
            LW["xts"] = [sc.sb("xt%d" % i, [128, D], F32) for i in range(nx)]

        if 0 in layers:
            with B.scope() as LG:
                layer_ws(LG, 3)
                for b in range(NS):
                    layer0(b)
                    if b == 0:
                        S.mark("deferred casts")
                        deferred_casts(dcf, dcb, 1024, 8)
        if 1 in layers:
            with B.scope() as LG:
                layer_ws(LG, 3)
                for b in range(NS):
                    layer1(b)
        S.finish()
    return nc, S


_CACHE = {}


def make_in_map(inp, b0, NS):
    f = lambda a: np.ascontiguousarray(np.asarray(a, dtype=np.float32))
    return {
        "x": f(inp["x"][b0:b0 + NS]), "c": f(inp["c"][b0:b0 + NS]), "ctx": f(inp["ctx"][b0:b0 + NS]),
        "c_ctx": f(inp["c_ctx"]).reshape(1, D),
        "ada_w": f(inp["ada_w"]), "ada_b": f(inp["ada_b"]),
        "pre_g": f(inp["pre_g"]).reshape(4, D), "post_g": f(inp["post_g"]).reshape(4, D),
        "ffn_up": f(inp["ffn_up"]), "ffn_conv": f(inp["ffn_conv"]).reshape(2 * 9 * NJ, 128), "ffn_down": f(inp["ffn_down"]),
        "ab_w_in": f(inp["ab_w_in"]).reshape(D, AB_PROJ), "ab_qk_conv": f(inp["ab_qk_conv"]).reshape(24, 128),
        "ab_gate_b": f(inp["ab_gate_b"]).reshape(1, 16), "ab_sgu_w": f(inp["ab_sgu_w"]).reshape(4, 128, 128),
        "ab_sgu_b": f(inp["ab_sgu_b"]).reshape(4, 128), "ab_head_g": f(inp["ab_head_g"]).reshape(1, 512),
        "ab_w_out": f(inp["ab_w_out"]).reshape(D, D), "ret_w_in": f(inp["ret_w_in"]).reshape(D, C_PROJ),
        "ret_decay": f(inp["ret_decay"]).reshape(1, 8), "ret_head_g": f(inp["ret_head_g"]).reshape(1, 2048),
        "ret_w_out": f(inp["ret_w_out"]).reshape(2048, D),
    }


def kernel(**inputs):
    NS = inputs["x"].shape[0] // N_CORES
    if "nc" not in _CACHE:
        _CACHE["nc"] = build_program(NS)[0]
    nc = _CACHE["nc"]
    in_maps = [make_in_map(inputs, i * NS, NS) for i in range(N_CORES)]
    res = run_bass_kernel_spmd(nc, in_maps, core_ids=list(range(N_CORES)))
    return np.concatenate([np.asarray(r["out"], dtype=np.float32) for r in res.results], axis=0)
```

```python
from contextlib import ExitStack
import math
import numpy as np
import concourse.bass as bass
import concourse.mybir as mybir
from concourse.bass_utils import run_bass_kernel_spmd

F32 = mybir.dt.float32
BF16 = mybir.dt.bfloat16
AF = mybir.ActivationFunctionType
ALU = mybir.AluOpType
AX = mybir.AxisListType

D = 1024
SEQ = 2048
CTX = 256
T = SEQ + CTX
NT = T // 128
DFF = 2816
NJ = DFF // 128
EPS = 1e-6
AB_PROJ = 3088
C_PROJ = 6144
N_CORES = 8
REORDER = True
FOLD_WAITS = True
POOL_TAP = False


def _esize(dt):
    return 4 if dt == F32 else 2


def _free_elems(ap):
    n = 1
    for d in ap.shape[1:]:
        n *= d
    return n


class Sched:
    LAT = 120.0

    def __init__(self, nc, stack, n_ld=24, n_st=16):
        self.nc = nc
        self.eng = {"pe": nc.tensor, "act": nc.scalar, "dve": nc.vector, "pool": nc.gpsimd, "sp": nc.sync}
        self.sem = {}
        for e in ("pe", "act", "dve", "pool"):
            self.sem[e] = stack.enter_context(nc.semaphore("s_" + e))
        self.dq = {"sp": [], "pool": []}
        for i in range(n_ld):
            k = "ld%d" % i
            self.sem[k] = stack.enter_context(nc.semaphore(k))
            self.dq["sp"].append(k)
        for i in range(n_st):
            k = "st%d" % i
            self.sem[k] = stack.enter_context(nc.semaphore(k))
            self.dq["pool"].append(k)
        self.nodes = []
        self.last_w = {}
        self.readers = {}
        self.rel_node = None
        self.nwaits = 0
        self.ninst = 0
        self.cnt = {e: 0 for e in self.eng}

    @staticmethod
    def keys(x):
        if isinstance(x, (str, tuple)):
            return [x]
        if not hasattr(x, "tensor"):
            return [x.name]
        name = x.tensor.name
        if name != "psall":
            return [name]
        es = _esize(x.dtype)
        off = (x.offset * es) % 16384
        ext = es
        for (stp, cnt) in x.ap[1:]:
            ext += (cnt - 1) * abs(stp) * es
        return [("ps", k) for k in range(off // 2048, (off + ext - 1) // 2048 + 1)]

    def _flat(self, lst):
        out = []
        for x in lst:
            if x is None:
                continue
            for k in self.keys(x):
                if k not in out:
                    out.append(k)
        return out

    def issue(self, e, fn, reads=(), writes=(), dma=False, cost=100.0, lat=0.0, fold=False):
        reads = self._flat(reads)
        writes = self._flat(writes)
        nid = len(self.nodes)
        preds = {}
        for r in reads:
            w = self.last_w.get(r)
            if w is not None:
                preds[w] = True
        for r in reads:
            if isinstance(r, tuple) and r[0] == "ps" and r not in writes:
                writes.append(r)
        for w in writes:
            p = self.last_w.get(w)
            if p is not None and p not in preds:
                preds[p] = False
            for rd in self.readers.get(w, ()):
                if rd not in preds:
                    preds[rd] = False
        preds.pop(nid, None)
        self.nodes.append([e, fn, dma, float(cost), float(lat), list(preds.items()), bool(fold) and FOLD_WAITS])
        for w in writes:
            self.last_w[w] = nid
            self.readers[w] = []
        for r in reads:
            if r not in writes:
                self.readers.setdefault(r, []).append(nid)
        return nid

    def mark(self, name):
        if not hasattr(self, "marks"):
            self.marks = []
        self.marks.append((name, len(self.nodes)))

    def release(self, names):
        preds = {}
        if self.rel_node is not None:
            preds[self.rel_node] = False
        for n in names:
            for k in self.keys(n):
                w = self.last_w.pop(k, None)
                if w is not None:
                    preds[w] = False
                for rd in self.readers.pop(k, ()):
                    preds[rd] = False
        nid = len(self.nodes)
        self.nodes.append([None, None, False, 0.0, 0.0, list(preds.items()), False])
        self.rel_node = nid

    def adopt(self, names):
        if self.rel_node is None:
            return
        for n in names:
            for k in self.keys(n):
                self.readers[k] = [self.rel_node]

    def schedule(self):
        import heapq
        N = len(self.nodes)
        npred = [0] * N
        succ = [[] for _ in range(N)]
        for i, nd in enumerate(self.nodes):
            npred[i] = len(nd[5])
            for (p, _) in nd[5]:
                succ[p].append(i)
        ready_t = [0.0] * N
        start = [0.0] * N
        fin = [0.0] * N
        efree = {e: 0.0 for e in self.eng}
        rq = {e: [] for e in self.eng}
        pend = []
        done = 0

        def make_ready(i, t):
            nd = self.nodes[i]
            if nd[0] is None:
                finish(i, t, t)
            else:
                heapq.heappush(rq[nd[0]], i)
                ready_t[i] = t

        def finish(i, ts, tf):
            nonlocal done
            start[i] = ts
            fin[i] = tf
            done += 1
            for sc in succ[i]:
                npred[sc] -= 1
                if ready_t[sc] < tf + self.LAT:
                    ready_t[sc] = tf + self.LAT
                if npred[sc] == 0:
                    heapq.heappush(pend, (ready_t[sc], sc))

        for i in range(N):
            if npred[i] == 0:
                heapq.heappush(pend, (0.0, i))
        now = 0.0
        while done < N:
            while pend and pend[0][0] <= now:
                t, i = heapq.heappop(pend)
                make_ready(i, t)
            if done >= N:
                break
            progressed = False
            for e in self.eng:
                if efree[e] <= now and rq[e]:
                    i = heapq.heappop(rq[e])
                    nd = self.nodes[i]
                    ts = now
                    efree[e] = ts + nd[3]
                    finish(i, ts, ts + nd[3] + nd[4])
                    progressed = True
            if progressed:
                continue
            cand = [efree[e] for e in self.eng if rq[e] and efree[e] > now]
            if pend:
                cand.append(pend[0][0])
            if not cand:
                raise RuntimeError("scheduler stuck (cyclic dependencies?)")
            now = max(now, min(cand))
        self.sim_ns = max(fin) if N else 0.0
        self.sim_start, self.sim_fin = start, fin
        if not REORDER:
            return list(range(N))
        return sorted(range(N), key=lambda i: (start[i], i))

    def emit(self):
        order = self.schedule()
        known = {e: {} for e in self.eng}
        ev = [None] * len(self.nodes)
        dval = {k: 0 for q in self.dq.values() for k in q}
        dclk = {k: {} for q in self.dq.values() for k in q}
        dnext = {"sp": 0, "pool": 0}

        self.trace = {e: [] for e in self.eng}
        self.nfold = 0

        pend_w = []

        def wait(e, s, v, clk):
            kn = known[e]
            if s is not None and kn.get(s, 0) < v:
                pend_w.append((s, v))
                self.nwaits += 1
                kn[s] = v
            for k2, v2 in clk.items():
                if kn.get(k2, 0) < v2:
                    kn[k2] = v2

        def flush(e, keep_last):
            last = None
            if keep_last and pend_w:
                last = pend_w.pop()
            for (s_, v_) in pend_w:
                self.eng[e].wait_ge(self.sem[s_], v_)
                self.trace[e].append(("w", s_, v_))
            del pend_w[:]
            return last

        for i in order:
            e, fn, dma, cost, lat, preds, fold = self.nodes[i]
            if e is None:
                clk = {}
                for (p, _) in preds:
                    s, v, c2 = ev[p]
                    if s is not None and clk.get(s, 0) < v:
                        clk[s] = v
                    if s is None:
                        for k2, v2 in c2.items():
                            if clk.get(k2, 0) < v2:
                                clk[k2] = v2
                ev[i] = (None, 0, clk)
                continue
            own = None if dma else e
            for (p, raw) in preds:
                s, v, clk = ev[p]
                if s is None:
                    for k2, v2 in clk.items():
                        if k2 == own and e == "pe":
                            continue
                        if known[e].get(k2, 0) < v2:
                            pend_w.append((k2, v2))
                            self.nwaits += 1
                            known[e][k2] = v2
                    continue
                if s == own and (e == "pe" or not raw):
                    continue
                wait(e, s, v, clk)
            if dma:
                q = self.dq[e]
                k = q[dnext[e] % len(q)]
                dnext[e] += 1
                if dval[k] > 0:
                    wait(e, k, dval[k], dclk[k])
                flush(e, False)
                ins = fn()
                dval[k] += 16
                ins.then_inc(self.sem[k], 16)
                self.trace[e].append(("i", k, 16))
                clk = dict(known[e])
                dclk[k] = clk
                ev[i] = (k, dval[k], clk)
            else:
                last = flush(e, fold)
                ins = fn()
                if last is not None:
                    ins._wait_ge(self.sem[last[0]], last[1])
                    self.trace[e].append(("w", last[0], last[1]))
                    self.nfold += 1
                self.cnt[e] += 1
                ins.then_inc(self.sem[e], 1)
                self.trace[e].append(("i", e, 1))
                ev[i] = (e, self.cnt[e], dict(known[e]))
            self.ninst += 1
        for k, v in dval.items():
            if v > 0 and known["sp"].get(k, 0) < v:
                self.eng["sp"].wait_ge(self.sem[k], v)
        for e in ("pe", "act", "dve", "pool"):
            if self.cnt[e] > 0:
                self.eng["sp"].wait_ge(self.sem[e], self.cnt[e])

    def finish(self):
        self.emit()


class PSView:
    def __init__(self, arena, bank, n):
        self.arena = arena
        self.c0 = bank * 512
        self.n = n

    def __getitem__(self, idx):
        rows, cols = idx
        a = 0 if cols.start is None else cols.start
        b = self.n * 512 if cols.stop is None else cols.stop
        if cols.step is None:
            return self.arena[rows, self.c0 + a:self.c0 + b]
        return self.arena[rows, self.c0 + a:self.c0 + b:cols.step]


class Bld:
    def __init__(self, nc, st):
        self.nc = nc
        self.S = Sched(nc, st)
        self.arena = st.enter_context(nc.psum_tensor("psall", [128, 4096], F32))
        self.ps_i = 0
        self.uid = 0
        probe = nc.alloc_sbuf_tensor("sb_probe", [128, 8], F32)
        self.sb_top = (nc.lookup_mloc(probe).addr + 32 + 31) // 32 * 32
        self.sb_limit = nc.SBUF_PARTITION_SIZE_BYTES
        self.sb_peak = self.sb_top

    def psum(self, n=1):
        if n == 2 and self.ps_i % 2 == 1:
            self.ps_i += 1
        v = PSView(self.arena, self.ps_i % 8, n)
        self.ps_i += n
        return v

    class Scope:
        def __init__(self, b):
            self.b = b
            self.st = ExitStack()
            self.names = []

        def __enter__(self):
            self.top0 = self.b.sb_top
            return self

        def sb(self, name, shape, dt):
            self.b.uid += 1
            nbytes = _esize(dt)
            for d in shape[1:]:
                nbytes *= d
            nbytes = (nbytes + 31) // 32 * 32
            off = self.b.sb_top
            assert off + nbytes <= self.b.sb_limit, "SBUF overflow: %s needs %d at %d" % (name, nbytes, off)
            t = self.b.nc.alloc_sbuf_tensor_at("%s_%d" % (name, self.b.uid), list(shape), dt, offset=off)
            self.b.sb_top = off + nbytes
            self.b.sb_peak = max(self.b.sb_peak, self.b.sb_top)
            self.names.append(t)
            self.b.S.adopt([t])
            return t

        def __exit__(self, *a):
            self.b.S.release(self.names)
            self.b.sb_top = self.top0
            return False

    def scope(self):
        return Bld.Scope(self)

    def _k(self, aps, override):
        return list(override) if override is not None else [a for a in aps if a is not None and not isinstance(a, (int, float))]

    def mm(self, out, lhsT, rhs, start=True, stop=True, rk=None, wk=None):
        nc = self.nc
        n = _free_elems(rhs)
        c = (max(n, 64) / 2.4 * (4.0 if rhs.dtype == F32 else 1.0) + 8) * 1.2
        return self.S.issue("pe", lambda: nc.tensor.matmul(out, lhsT=lhsT, rhs=rhs, start=start, stop=stop),
                            reads=self._k([lhsT, rhs], rk), writes=self._k([out], wk), cost=c, lat=60)

    def tr(self, out, in_, ident, rk=None, wk=None):
        nc = self.nc
        c = 128 / 2.4 * (2.0 if in_.dtype == F32 else 1.0) + 8
        return self.S.issue("pe", lambda: nc.tensor.transpose(out, in_, ident),
                            reads=self._k([in_, ident], rk), writes=self._k([out], wk), cost=c, lat=60)

    def act(self, out, in_, func, bias=0.0, scale=1.0, accum=None, rk=None, wk=None):
        nc = self.nc
        kw = {}
        if accum is not None:
            kw["accum_out"] = accum

        def f():
            return nc.scalar.activation(out=out, in_=in_, func=func, bias=bias, scale=scale, **kw)
        c = (200 + _free_elems(in_) / 1.4 + (100 if accum is not None else 0)) * 1.2
        return self.S.issue("act", f, reads=self._k([in_, bias, scale], rk), writes=self._k([out, accum], wk), cost=c, lat=60, fold=(accum is None))

    def _vc(self, e, ap):
        n = _free_elems(ap)
        return (90 + n / 0.96) * 1.14 if e == "dve" else (150 + n * 1.65)

    def tt(self, e, out, a, b, op, rk=None, wk=None):
        eng = self.S.eng[e]
        return self.S.issue(e, lambda: eng.tensor_tensor(out=out, in0=a, in1=b, op=op),
                            reads=self._k([a, b], rk), writes=self._k([out], wk), cost=self._vc(e, out), lat=60, fold=True)

    def ts(self, e, out, a, s1, op0, s2=None, op1=None, rk=None, wk=None):
        eng = self.S.eng[e]

        def f():
            if op1 is None:
                return eng.tensor_scalar(out=out, in0=a, scalar1=s1, scalar2=None, op0=op0)
            return eng.tensor_scalar(out=out, in0=a, scalar1=s1, scalar2=s2, op0=op0, op1=op1)
        return self.S.issue(e, f, reads=self._k([a, s1, s2], rk), writes=self._k([out], wk), cost=self._vc(e, out), lat=60, fold=True)

    def stt(self, out, a, s, b, op0, op1, rk=None, wk=None):
        nc = self.nc
        return self.S.issue("dve", lambda: nc.vector.scalar_tensor_tensor(out=out, in0=a, scalar=s, in1=b, op0=op0, op1=op1),
                            reads=self._k([a, s, b], rk), writes=self._k([out], wk), cost=self._vc("dve", out), lat=60, fold=True)

    def cp(self, e, out, in_, rk=None, wk=None):
        nc = self.nc
        if e == "act":
            f = lambda: nc.scalar.copy(out=out, in_=in_)
            c = 200 + _free_elems(out) / 1.4
        else:
            eng = self.S.eng[e]
            f = lambda: eng.tensor_copy(out=out, in_=in_)
            c = self._vc(e, out)
        return self.S.issue(e, f, reads=self._k([in_], rk), writes=self._k([out], wk), cost=c, lat=60, fold=True)

    def red(self, out, in_, op=ALU.add, rk=None, wk=None):
        nc = self.nc
        return self.S.issue("dve", lambda: nc.vector.tensor_reduce(out=out, in_=in_, axis=AX.X, op=op),
                            reads=self._k([in_], rk), writes=self._k([out], wk), cost=self._vc("dve", in_), lat=60, fold=True)

    def recip(self, out, in_):
        nc = self.nc
        return self.S.issue("dve", lambda: nc.vector.reciprocal(out=out, in_=in_), reads=[in_], writes=[out], cost=self._vc("dve", out), lat=60, fold=True)

    def memset(self, e, ap, val):
        eng = self.S.eng[e]
        return self.S.issue(e, lambda: eng.memset(ap, val), writes=[ap], cost=self._vc(e, ap), lat=60, fold=True)

    def ld(self, out, in_, rk=None, wk=None):
        nc = self.nc
        nbytes = out.shape[0] * _free_elems(out) * _esize(out.dtype)
        return self.S.issue("sp", lambda: nc.sync.dma_start(out=out, in_=in_), reads=self._k([], rk), writes=self._k([out], wk), dma=True,
                            cost=70, lat=2200 + nbytes / 160.0)

    def stq(self, out, in_, rk=None, wk=None):
        nc = self.nc
        nbytes = in_.shape[0] * _free_elems(in_) * _esize(in_.dtype)
        return self.S.issue("pool", lambda: nc.gpsimd.dma_start(out=out, in_=in_), reads=self._k([in_], rk), writes=self._k([], wk), dma=True,
                            cost=600, lat=2500 + nbytes / 160.0)

    def rstd(self, out, ss, n, tmp):
        self.act(tmp, ss, AF.Sqrt, bias=self.eps_ap, scale=1.0 / n)
        self.recip(out, tmp)


def build_program(NS, dbg=False, layers=(0, 1)):
    nc = bass.Bass("TRN2", target_bir_lowering=False)
    R = NS + 1

    def din(name, shape):
        return nc.dram_tensor(name, list(shape), F32, kind="ExternalInput").ap()

    x_in = din("x", [NS, SEQ, D]); c_in = din("c", [NS, D]); ctx_in = din("ctx", [NS, CTX, D]); cctx_in = din("c_ctx", [1, D])
    ada_w = din("ada_w", [2, D, 6 * D]); ada_b = din("ada_b", [2, 6 * D]); pre_g = din("pre_g", [4, D]); post_g = din("post_g", [4, D])
    ffn_up = din("ffn_up", [2, D, 2 * DFF]); ffn_conv = din("ffn_conv", [2 * 9 * NJ, 128]); ffn_down = din("ffn_down", [2, DFF, D])
    ab_w_in = din("ab_w_in", [D, AB_PROJ]); ab_qk_conv = din("ab_qk_conv", [24, 128]); ab_gate_b = din("ab_gate_b", [1, 16])
    ab_sgu_w = din("ab_sgu_w", [4, 128, 128]); ab_sgu_b = din("ab_sgu_b", [4, 128]); ab_head_g = din("ab_head_g", [1, 512])
    ab_w_out = din("ab_w_out", [D, D]); ret_w_in = din("ret_w_in", [D, C_PROJ]); ret_decay = din("ret_decay", [1, 8])
    ret_head_g = din("ret_head_g", [1, 2048]); ret_w_out = din("ret_w_out", [2048, D])
    out_t = nc.dram_tensor("out", [NS, SEQ, D], F32, kind="ExternalOutput").ap()

    skind = "ExternalOutput" if dbg else "Internal"

    def dscr(name, shape, dt, k=None):
        return nc.dram_tensor(name, list(shape), dt, kind=k or "Internal").ap()

    w_in0b = dscr("w_in0b", [D, AB_PROJ], BF16); w_out0b = dscr("w_out0b", [D, D], BF16)
    w_in1b = dscr("w_in1b", [D, C_PROJ], BF16); w_out1b = dscr("w_out1b", [2048, D], BF16)
    up_r = dscr("up_r", [2, NJ, 128, 8, 2, 128], BF16); down_b = dscr("down_b", [2, DFF, D], BF16)
    gsc = dscr("gsc", [2, 2, R, D], F32)
    xm_s = dscr("xm_s", [NS, T, D], F32, skind)
    xo_s = dscr("xo_s", [NS, T, D], F32, skind)
    yb_s = dscr("yb_s", [2, NT, 128, 512], F32)
    og_s = dscr("og_s", [2, NT, 128, 512], F32)
    sgu_s = dscr("sgu_s", [2, NT, 128, 512], BF16)
    act_s = dscr("act_s", [2, NJ, 128, T], BF16)
    yf_s = dscr("yf_s", [2, NT, 128, 512], F32)
    z_s = dscr("z_s", [2, SEQ, 2048], BF16)

    with ExitStack() as st:
        B = Bld(nc, st)
        S = B.S
        G = B.scope()
        st.enter_context(G)
        ident_b = G.sb("ident_b", [128, 128], BF16); ident_f = G.sb("ident_f", [128, 128], F32)
        triU = G.sb("triU", [128, 128], F32); triL = G.sb("triL", [128, 128], F32); ones_f = G.sb("ones_f", [128, 128], F32)
        mskU = G.sb("mskU", [128, 128], BF16); mskL = G.sb("mskL", [128, 128], BF16)
        B.memset("pool", ones_f[:, :], 1.0)
        B.memset("pool", triU[:, :], 1.0); B.memset("pool", triL[:, :], 1.0); B.memset("pool", ident_f[:, :], 1.0)
        S.issue("pool", lambda: nc.gpsimd.affine_select(out=triU[:, :], in_=triU[:, :], pattern=[[1, 128]], compare_op=ALU.is_ge, fill=0.0, base=0, channel_multiplier=-1), reads=[triU], writes=[triU])
        S.issue("pool", lambda: nc.gpsimd.affine_select(out=triL[:, :], in_=triL[:, :], pattern=[[-1, 128]], compare_op=ALU.is_ge, fill=0.0, base=0, channel_multiplier=1), reads=[triL], writes=[triL])
        B.tt("pool", ident_f[:, :], triU[:, :], triL[:, :], ALU.mult)
        B.cp("pool", ident_b[:, :], ident_f[:, :]); B.cp("pool", mskU[:, :], triU[:, :]); B.cp("pool", mskL[:, :], triL[:, :])

        hT = G.sb("hT", [128, 8, T], BF16)
        sT = G.sb("sT", [128, 8, R], F32)
        modsT = G.sb("modsT", [128, 2, 48, R], F32)
        pre_gT = G.sb("pre_gT", [128, 32], F32)
        qkcT = G.sb("qkcT", [128, 24], F32)
        fcvT = G.sb("fcvT", [128, 2 * 9 * NJ], F32)
        sgbT = G.sb("sgbT", [128, 4], F32)
        hgT = G.sb("hgT", [128, 4], F32)
        g1c = G.sb("g1c", [128, 2, 2, R, 8], F32)
        gateb = G.sb("gateb", [128, 16], F32)
        lgd = G.sb("lgd", [128, 8], F32)
        wsT = G.sb("wsT", [128, 4, 128], BF16)

        def rowsT(dst, src_ap, nrows, P):
            with B.scope() as sc:
                stg = sc.sb("rt_stg", [128, 128], F32)
                for r0 in range(0, nrows, 128):
                    n = min(128, nrows - r0)
                    B.ld(stg[0:n, :], src_ap[r0:r0 + n, :])
                    ps = B.psum()
                    B.tr(ps[:, 0:n], stg[0:n, :], ident_f[0:n, 0:n])
                    B.cp("dve", dst[:, r0:r0 + n], ps[:, 0:n])

        cast_cnt = [0]
        cast_gate = [[]]

        def cast_mat(src, dst, key, stf, stb, W, rowscale=None):
            for r0 in range(0, src.shape[0], 128):
                for c0 in range(0, src.shape[1], W):
                    n = min(W, src.shape[1] - c0)
                    i = cast_cnt[0] % len(stf)
                    cast_cnt[0] += 1
                    B.ld(stf[i][:, 0:n], src[r0:r0 + 128, c0:c0 + n], rk=cast_gate[0])
                    col = rowscale(r0) if rowscale is not None else None
                    if col is not None:
                        B.ts("dve", stb[i][:, 0:n], stf[i][:, 0:n], col, ALU.mult)
                    else:
                        B.cp(("act", "dve", "pool")[cast_cnt[0] % 3], stb[i][:, 0:n], stf[i][:, 0:n])
                    B.stq(dst[r0:r0 + 128, c0:c0 + n], stb[i][:, 0:n], wk=[key])

        def cast_up(l, stf, stb, JB):
            for kc in range(8):
                for half in range(2):
                    for j0 in range(0, NJ, JB):
                        nj = min(JB, NJ - j0)
                        n = nj * 128
                        i = cast_cnt[0] % len(stf)
                        cast_cnt[0] += 1
                        B.ld(stf[i][:, 0:n], ffn_up[l, kc * 128:(kc + 1) * 128, half * DFF + j0 * 128:half * DFF + j0 * 128 + n], rk=cast_gate[0])
                        B.cp(("act", "dve", "pool")[cast_cnt[0] % 3], stb[i][:, 0:n], stf[i][:, 0:n])
                        B.stq(up_r[l, j0:j0 + nj, :, kc, half, :].rearrange("j p c -> p j c"),
                              stb[i][:, 0:n].rearrange("p (j c) -> p j c", c=128), wk=[("up_r", l)])

        def deferred_casts(stf, stb, W, JB):
            cast_gate[0] = [("defer_gate",)] if 0 in layers else []
            if 1 in layers:
                cast_mat(ret_w_in, w_in1b, ("w_in1b",), stf, stb, W)
                cast_mat(ret_w_out, w_out1b, ("w_out1b",), stf, stb, W)
                cast_up(1, stf, stb, JB)
                cast_mat(ffn_down[1], down_b[1], ("down_b", 1), stf, stb, W)

        dgate = G.sb("dgate", [128, 1], F32)
        dcf = [G.sb("dcf%d" % i, [128, 1024], F32) for i in range(2)]
        dcb = [G.sb("dcb%d" % i, [128, 1024], BF16) for i in range(2)]

        S.mark("prologue")
        with B.scope() as P:
            rowsT(pre_gT, pre_g.rearrange("a (c p) -> (a c) p", p=128), 32, P)
            rowsT(qkcT, ab_qk_conv, 24, P)
            rowsT(fcvT, ffn_conv, 2 * 9 * NJ, P)
            rowsT(sgbT, ab_sgu_b, 4, P)
            rowsT(hgT, ab_head_g.rearrange("a (c p) -> (a c) p", p=128), 4, P)
            B.ld(gateb[:, :], ab_gate_b[0, :].partition_broadcast(128))
            B.ld(lgd[:, :], ret_decay[0, :].partition_broadcast(128))
            wstg = P.sb("wstg", [128, 128], F32)
            for g in range(4):
                B.ld(wstg[:, :], ab_sgu_w[g, :, :])
                ps = B.psum()
                B.tr(ps[:, 0:128], wstg[:, :], ident_f[:, :])
                B.cp("act", wsT[:, g, :], ps[:, 0:128])
            crow = P.sb("crow", [R, D], F32)
            B.ld(crow[0:NS, :], c_in[:, :])
            B.ld(crow[NS:R, :], cctx_in[:, :])
            srow = P.sb("srow", [R, D], F32)
            B.act(srow[:, :], crow[:, :], AF.Silu)
            ps = B.psum()
            for kc in range(8):
                B.tr(ps[:, kc * 8:kc * 8 + R], srow[:, kc * 128:(kc + 1) * 128], ident_f[0:R, 0:R])
            B.cp("dve", sT[:, :, :], ps[:, 0:64].rearrange("p (k r) -> p k r", r=8)[:, :, 0:R])
            adbT = P.sb("adbT", [128, 96], F32)
            rowsT(adbT, ada_b.rearrange("l (c p) -> (l c) p", p=128), 96, P)
            adw = [P.sb("adw%d" % i, [128, 8, D], F32) for i in range(2)]
            brow = P.sb("brow", [R, D], F32); grow = P.sb("grow", [R, D], F32); prow = P.sb("prow", [R, D], F32)
            for l in range(2):
                for m in range(6):
                    w = adw[(l * 6 + m) % 2]
                    B.ld(w[:, :, :], ada_w[l, :, m * D:(m + 1) * D].rearrange("(k p) n -> p k n", p=128))
                    ps = B.psum()
                    for fc in range(8):
                        for kc in range(8):
                            B.mm(ps[:, fc * 8:fc * 8 + R], w[:, kc, fc * 128:(fc + 1) * 128], sT[:, kc, :], start=(kc == 0), stop=(kc == 7))
                    for fc in range(8):
                        B.ts("dve", modsT[:, l, m * 8 + fc, :], ps[:, fc * 8:fc * 8 + R], adbT[:, l * 48 + m * 8 + fc:l * 48 + m * 8 + fc + 1], ALU.add)
                    if m in (2, 5):
                        which = 0 if m == 2 else 1
                        ps2 = B.psum(2)
                        for half in range(2):
                            for kc in range(8):
                                B.mm(ps2[0:R, half * 512:(half + 1) * 512], sT[:, kc, :], w[:, kc, half * 512:(half + 1) * 512], start=(kc == 0), stop=(kc == 7))
                        B.ld(brow[:, :], ada_b[l, m * D:(m + 1) * D].partition_broadcast(R))
                        B.ld(prow[:, :], post_g[l * 2 + which, :].partition_broadcast(R))
                        B.tt("dve", grow[:, :], ps2[0:R, :], brow[:, :], ALU.add)
                        B.tt("dve", grow[:, :], grow[:, :], prow[:, :], ALU.mult)
                        B.stq(gsc[l, which, :, :], grow[:, :], wk=[("gsc", l, which)])
                for which in range(2):
                    msc = 1 if which == 0 else 4
                    for r in range(R):
                        B.stt(g1c[:, l, which, r, :], modsT[:, l, msc * 8:(msc + 1) * 8, r], 1.0,
                              pre_gT[:, (l * 2 + which) * 8:(l * 2 + which) * 8 + 8], ALU.add, ALU.mult)

            NSTG = 4
            cst_f = [P.sb("cst_f%d" % i, [128, 2048], F32) for i in range(NSTG)]
            cst_b = [P.sb("cst_b%d" % i, [128, 2048], BF16) for i in range(NSTG)]
            if 0 in layers:
                cast_mat(ab_w_in, w_in0b, ("w_in0b",), cst_f, cst_b, 2048)
                cast_mat(ab_w_out, w_out0b, ("w_out0b",), cst_f, cst_b, 2048,
                         rowscale=lambda r0: hgT[:, (r0 - 512) // 128:(r0 - 512) // 128 + 1] if r0 >= 512 else None)
                cast_up(0, cst_f, cst_b, 11)
                cast_mat(ffn_down[0], down_b[0], ("down_b", 0), cst_f, cst_b, 2048)
            if 0 not in layers:
                deferred_casts(cst_f, cst_b, 2048, 11)

        def src_tile(layer, b, i):
            if layer == 0:
                if i < 2:
                    return ctx_in[b, i * 128:(i + 1) * 128, :], None
                return x_in[b, (i - 2) * 128:(i - 1) * 128, :], None
            return xo_s[b, i * 128:(i + 1) * 128, :], ("xo", b, i)

        def prenorm_to_hT(W, xt, layer, which, row, i, tagk):
            junk, ss, tmp1, rs, xn = W["junk"], W["ss"], W["tmp1"], W["rs"], W["xn"]
            B.act(junk[:, :], xt[:, :], AF.Square, accum=ss[:, 0:1])
            B.rstd(rs[:, 0:1], ss[:, 0:1], D, tmp1[:, 0:1])
            B.ts("dve", xn[:, :], xt[:, :], rs[:, 0:1], ALU.mult)
            msh = 0 if which == 0 else 3
            ps = B.psum(2)
            for kc in range(8):
                B.tr(ps[:, kc * 128:(kc + 1) * 128], xn[:, kc * 128:(kc + 1) * 128], ident_f[:, :])
            for kc in range(8):
                B.act(hT[:, kc, i * 128:(i + 1) * 128], ps[:, kc * 128:(kc + 1) * 128], AF.Identity,
                      bias=modsT[:, layer, msh * 8 + kc, row:row + 1], scale=g1c[:, layer, which, row, kc:kc + 1],
                      wk=[("hT", i)])

        def post_residual(W, psY, xt, gt, layer, out_ap, out_key, xm):
            junk, ss, tmp1, rs, t1 = W["junk"], W["ss"], W["tmp1"], W["rs"], W["t1"]
            B.act(junk[:, :], psY[:, :], AF.Square, accum=ss[:, 1:2])
            B.rstd(rs[:, 1:2], ss[:, 1:2], D, tmp1[:, 1:2])
            B.stt(t1[:, :], psY[:, :], rs[:, 1:2], gt[:, :], ALU.mult, ALU.mult)
            B.tt("pool", xm[:, :], t1[:, :], xt[:, :], ALU.add)
            B.stq(out_ap, xm[:, :], wk=[out_key])

        def ffn(layer, b, tiles):
            par = b % 2
            has_ctx = 0 in tiles
            S.mark("L%d ffn_up" % layer)
            with B.scope() as F:
                gp = [F.sb("gp%d" % i, [128, 34, 66], F32) for i in range(2)]
                gc_ = [F.sb("gc", [128, 258], F32) for _i in range(2)]
                ptap_ = [F.sb("ptap", [128, SEQ], F32) for _i in range(2)] if POOL_TAP else [None, None]
                ab_ = [F.sb("ab%d" % i, [128, T], F32) for i in range(2)]
                acc_ = [F.sb("acc", [128, T], F32) for _i in range(2)]
                gl_ = [F.sb("gl", [128, T], F32) for _i in range(2)]
                ao = [F.sb("ao%d" % i, [128, T], BF16) for i in range(2)]
                wj = [F.sb("wj%d" % i, [128, 8, 256], BF16) for i in range(2)]
                for i in range(2):
                    B.memset("pool", gp[i][:, :, :], 0.0)
                for _i in range(2):
                    B.memset("pool", gc_[_i][:, :], 0.0)
                blocks = ([(0, 256)] if has_ctx else []) + [(256 + 512 * k, 512) for k in range(4)]
                for j in range(NJ):
                    w = wj[j % 2]
                    B.ld(w[:, :, :], up_r[layer, j, :, :, :, :].rearrange("p k h c -> p k (h c)"), rk=[("up_r", layer)])
                    gpj, abj, aoj = gp[j % 2], ab_[j % 2], ao[j % 2]
                    acc, gl, gc, ptap = acc_[j % 2], gl_[j % 2], gc_[j % 2], ptap_[j % 2]
                    for (t0, n) in blocks:
                        ps = B.psum(2)
                        rk = [w] + [("hT", t0 // 128 + q) for q in range(n // 128)]
                        for kc in range(8):
                            B.mm(ps[:, 0:n], w[:, kc, 128:256], hT[:, kc, t0:t0 + n], start=(kc == 0), stop=(kc == 7), rk=rk)
                        for kc in range(8):
                            B.mm(ps[:, 512:512 + n], w[:, kc, 0:128], hT[:, kc, t0:t0 + n], start=(kc == 0), stop=(kc == 7), rk=rk)
                        if t0 == 0:
                            B.cp("act", gc[:, 1:257], ps[:, 0:256])
                        else:
                            r0 = (t0 - 256) // 64
                            B.cp("act", gpj[:, 1 + r0:9 + r0, 1:65], ps[:, 0:512].rearrange("p (r c) -> p r c", c=64))
                        B.cp("act", abj[:, t0:t0 + n], ps[:, 512:512 + n])
                    accv = acc[:, 256:T].rearrange("p (r c) -> p r c", c=64)
                    first = True
                    for ty in range(3):
                        for tx in range(3):
                            wcol = fcvT[:, (layer * 9 + ty * 3 + tx) * NJ + j:(layer * 9 + ty * 3 + tx) * NJ + j + 1]
                            src = gpj[:, ty:ty + 32, tx:tx + 64]
                            if POOL_TAP and ty == 2 and tx == 2:
                                B.ts("pool", ptap[:, :].rearrange("p (r c) -> p r c", c=64), src, wcol, ALU.mult)
                                continue
                            if first:
                                B.act(accv, src, AF.Copy, scale=wcol)
                                first = False
                            else:
                                B.stt(accv, src, wcol, accv, ALU.mult, ALU.add)
                    if has_ctx:
                        for tx in range(3):
                            wcol = fcvT[:, (layer * 9 + 3 + tx) * NJ + j:(layer * 9 + 3 + tx) * NJ + j + 1]
                            if tx == 0:
                                B.act(acc[:, 0:256], gc[:, 0:256], AF.Copy, scale=wcol)
                            else:
                                B.stt(acc[:, 0:256], gc[:, tx:tx + 256], wcol, acc[:, 0:256], ALU.mult, ALU.add)
                    t00 = 0 if has_ctx else 256
                    if POOL_TAP:
                        B.tt("pool", acc[:, 256:T], acc[:, 256:T], ptap[:, :], ALU.add)
                    B.act(gl[:, t00:T], acc[:, t00:T], AF.Gelu_apprx_tanh)
                    B.tt("pool", aoj[:, t00:T], gl[:, t00:T], abj[:, t00:T], ALU.mult)
                    B.stq(act_s[par, j, :, t00:T], aoj[:, t00:T], wk=[("act_s", par, j)])
            S.mark("L%d ffn_down" % layer)
            with B.scope() as F:
                wd = F.sb("wd", [128, NJ, D], BF16)
                B.ld(wd[:, :, :], down_b[layer].rearrange("(j p) n -> p j n", p=128), rk=[("down_b", layer)])
                ablk = [F.sb("ablk%d" % i, [128, NJ, 256], BF16) for i in range(2)]
                g5 = F.sb("g5", [128, D], F32); g5c = F.sb("g5c", [128, D], F32)
                B.ld(g5[:, :], gsc[layer, 1, b, :].partition_broadcast(128), rk=[("gsc", layer, 1)])
                if has_ctx:
                    B.ld(g5c[:, :], gsc[layer, 1, NS, :].partition_broadcast(128), rk=[("gsc", layer, 1)])
                Wks = [{"junk": F.sb("f_junk", [128, D], BF16), "ss": F.sb("f_ss", [128, 2], F32), "tmp1": F.sb("f_tmp1", [128, 2], F32),
                        "rs": F.sb("f_rs", [128, 2], F32), "t1": F.sb("f_t1", [128, D], F32)} for _i in range(2)]
                xts = [F.sb("f_xt%d" % i, [128, D], F32) for i in range(2)]
                xos = [F.sb("f_xo%d" % i, [128, D], F32) for i in range(2)]
                nb = 0
                for t0 in range(0 if has_ctx else 256, T, 256):
                    blk = ablk[nb % 2]
                    nb += 1
                    B.ld(blk[:, :, :], act_s[par, :, :, t0:t0 + 256].rearrange("j p t -> p j t"), rk=[("act_s", par, j) for j in range(NJ)])
                    for q in range(2):
                        i = t0 // 128 + q
                        xt = xts[i % 2]; xo = xos[i % 2]
                        B.ld(xt[:, :], xm_s[b, i * 128:(i + 1) * 128, :], rk=[("xm", b, i)])
                        ps = B.psum(2)
                        for half in range(2):
                            for j in range(NJ):
                                B.mm(ps[:, half * 512:(half + 1) * 512], blk[:, j, q * 128:(q + 1) * 128], wd[:, j, half * 512:(half + 1) * 512],
                                     start=(j == 0), stop=(j == NJ - 1))
                        if layer == 1:
                            oap, okey = out_t[b, (i - 2) * 128:(i - 1) * 128, :], ("out", b, i)
                        else:
                            oap, okey = xo_s[b, i * 128:(i + 1) * 128, :], ("xo", b, i)
                        post_residual(Wks[i % 2], ps, xt, g5c if i < 2 else g5, layer, oap, okey, xo)

        def layer0(b):
            par = b % 2
            with B.scope() as M:
                KT = M.sb("KT", [128, 4, T], BF16); QT = M.sb("QT", [128, 4, T], BF16)
                Vx = M.sb("Vx", [128, NT, 4, 129], BF16)
                Gt = M.sb("Gt", [128, NT, 16], F32)
                EA = M.sb("EA", [128, NT, 8], F32); EAW = M.sb("EAW", [128, NT, 8], F32)
                EB = M.sb("EB", [128, NT, 8], F32); EBT = M.sb("EBT", [128, NT, 8], F32)
                C32 = M.sb("C32", [128, 8, 129], F32); Cbf = M.sb("Cbf", [128, 8, 129], BF16)
                Wks, xts = LW["Wks"], LW["xts"]
                NX = len(xts)
                yb = [M.sb("yb%d" % i, [128, 512], F32) for i in range(2)]
                PT = [M.sb("PT%d" % i, [128, 4, 128], BF16) for i in range(2)]
                Kw = [M.sb("Kw%d" % i, [128, 4, 128], BF16) for i in range(2)]
                dn_ = [M.sb("dn", [128, 4], F32) for _i in range(2)]; r2_ = [M.sb("r2", [128, 4], F32) for _i in range(2)]
                S.mark("L0 P1")
                for i in range(NT):
                    xt = xts[i % NX]
                    sap, skey = src_tile(0, b, i)
                    B.ld(xt[:, :], sap, rk=[skey] if skey else [])
                    prenorm_to_hT(Wks[i % 2], xt, 0, 0, NS if i < 2 else b, i, None)
                allh = [("hT", i) for i in range(NT)]
                S.mark("L0 P2a")
                with B.scope() as P2:
                    Wa = P2.sb("Wa", [128, 8, 1024], BF16)
                    B.ld(Wa[:, :, 0:512], w_in0b[:, 0:512].rearrange("(k p) n -> p k n", p=128), rk=[("w_in0b",)])
                    B.ld(Wa[:, :, 512:1024], w_in0b[:, 1040:1552].rearrange("(k p) n -> p k n", p=128), rk=[("w_in0b",)])
                    raw = [P2.sb("raw%d" % i, [128, T + 4], F32) for i in range(2)]
                    cacc = P2.sb("cacc", [128, T + 4], F32)
                    for i in range(2):
                        B.memset("pool", raw[i][:, :], 0.0)
                    for fc in range(8):
                        col0 = fc * 128
                        cch = 4 + fc if fc < 4 else fc - 4
                        rw = raw[fc % 2]
                        blocks = [(0, 256, 1)] + [(256 + 512 * k, 512, 259 + 512 * k) for k in range(4)]
                        for bi in range(0, len(blocks), 2):
                            ps = B.psum(2)
                            for q, (t0, n, off) in enumerate(blocks[bi:bi + 2]):
                                for kc in range(8):
                                    B.mm(ps[:, q * 512:q * 512 + n], Wa[:, kc, col0:col0 + 128], hT[:, kc, t0:t0 + n],
                                         start=(kc == 0), stop=(kc == 7), rk=[Wa] + allh)
                                B.cp("act", rw[:, off:off + n], ps[:, q * 512:q * 512 + n])
                        L = T + 4
                        B.ts("dve", cacc[:, 1:L - 1], rw[:, 0:L - 2], qkcT[:, cch:cch + 1], ALU.mult)
                        B.stt(cacc[:, 1:L - 1], rw[:, 1:L - 1], qkcT[:, 8 + cch:9 + cch], cacc[:, 1:L - 1], ALU.mult, ALU.add)
                        B.stt(cacc[:, 1:L - 1], rw[:, 2:L], qkcT[:, 16 + cch:17 + cch], cacc[:, 1:L - 1], ALU.mult, ALU.add)
                        dst = KT if fc < 4 else QT
                        B.act(dst[:, fc % 4, 0:256], cacc[:, 1:257], AF.Silu)
                        B.act(dst[:, fc % 4, 256:T], cacc[:, 259:259 + SEQ], AF.Silu)
                S.mark("L0 P2b")
                with B.scope() as P2:
                    Wr = P2.sb("Wr", [128, 8, 2064], BF16)
                    B.ld(Wr[:, :, 0:528], w_in0b[:, 512:1040].rearrange("(k p) n -> p k n", p=128), rk=[("w_in0b",)])
                    B.ld(Wr[:, :, 528:2064], w_in0b[:, 1552:3088].rearrange("(k p) n -> p k n", p=128), rk=[("w_in0b",)])
                    ogt = [P2.sb("ogt%d" % i, [128, 512], F32) for i in range(2)]
                    sgt = [P2.sb("sgt%d" % i, [128, 512], BF16) for i in range(2)]
                    u_ = [P2.sb("u", [128, 512], F32) for _i in range(2)]; va_ = [P2.sb("va", [128, 512], F32) for _i in range(2)]
                    sq_ = [P2.sb("sq", [128, 512], BF16) for _i in range(2)]
                    vc_ = [P2.sb("vc", [128, 512], BF16) for _i in range(2)]; s1_ = [P2.sb("s1", [128, 4], F32) for _i in range(2)]
                    def gates_and_prep():
                        S.mark("L0 gates")
                        psg = B.psum(1)
                        for i in range(NT):
                            for kc in range(8):
                                B.mm(psg[:, i * 16:(i + 1) * 16], hT[:, kc, i * 128:(i + 1) * 128], Wr[:, kc, 512:528], start=(kc == 0), stop=(kc == 7), rk=[Wr, ("hT", i)])
                        B.tt("dve", Gt[:, :, :], psg[:, 0:NT * 16].rearrange("p (t g) -> p t g", g=16), gateb[:, :].unsqueeze(1).to_broadcast([128, NT, 16]), ALU.add)
                        E1 = P2.sb("E1", [128, NT, 8], F32); Pn = P2.sb("Pn", [128, NT, 8], F32); A1 = P2.sb("A1", [128, NT, 8], F32); A2 = P2.sb("A2", [128, NT, 8], F32)
                        gv = Gt[:, :, :].rearrange("p t (g h) -> p t g h", h=4)
                        for d in range(2):
                            B.act(E1[:, :, d * 4:(d + 1) * 4], gv[:, :, 1 + 2 * d, :], AF.Exp, scale=-1.0)
                        B.act(Pn[:, :, :], E1[:, :, :], AF.Ln, bias=1.0)
                        ps = B.psum(1)
                        B.mm(ps[:, 0:72], triU[:, :], Pn[:, :, 0:4])
                        B.mm(ps[:, 72:144], triL[:, :], Pn[:, :, 4:8])
                        B.mm(ps[:, 144:288], ones_f[:, :], Pn[:, :, :])
                        for d in range(2):
                            B.tt("dve", A1[:, :, d * 4:(d + 1) * 4], gv[:, :, 2 * d, :], ps[:, d * 72:(d + 1) * 72].rearrange("p (t h) -> p t h", h=4), ALU.add)
                        B.tt("dve", A2[:, :, :], A1[:, :, :], ps[:, 144:288].rearrange("p (t j) -> p t j", j=8), ALU.subtract)
                        B.act(EA[:, :, :], A1[:, :, :], AF.Exp, bias=LNS_ap[:, 0:1])
                        B.act(EAW[:, :, :], A2[:, :, :], AF.Exp, bias=LNS_ap[:, 0:1])
                        for d in range(2):
                            B.act(EB[:, :, d * 4:(d + 1) * 4], ps[:, d * 72:(d + 1) * 72].rearrange("p (t h) -> p t h", h=4), AF.Exp, scale=-1.0)
                        B.act(EBT[:, :, :], ps[:, 144:288].rearrange("p (t j) -> p t j", j=8), AF.Exp, scale=-1.0)
                        if b == 0:
                            B.cp("pool", dgate[:, :], EBT[:, 0, 0:1], wk=[dgate, ("defer_gate",)])

                    B.memset("pool", Vx[:, :, :, :], 1.0)
                    for i in range(NT):
                        tk = slice(i * 128, (i + 1) * 128)
                        psV = B.psum(1)
                        for kc in range(8):
                            B.mm(psV[:, 0:512], hT[:, kc, tk], Wr[:, kc, 0:512], start=(kc == 0), stop=(kc == 7), rk=[Wr, ("hT", i)])
                        B.cp("act" if i % 2 == 0 else "dve", Vx[:, i, :, 0:128], psV[:, 0:512].rearrange("p (h d) -> p h d", h=4))
                    gates_and_prep()
                    for i in range(NT):
                        tk = slice(i * 128, (i + 1) * 128)
                        rk = [Wr, ("hT", i)]
                        u, va, sq, vc, s1 = u_[i % 2], va_[i % 2], sq_[i % 2], vc_[i % 2], s1_[i % 2]
                        psO = B.psum(2); psU = B.psum(2)
                        for (dst, c0, n) in ((psO[:, 0:512], 528, 512),
                                             (psO[:, 512:1024], 1040, 512), (psU[:, 0:512], 1552, 512)):
                            for kc in range(8):
                                B.mm(dst, hT[:, kc, tk], Wr[:, kc, c0:c0 + n], start=(kc == 0), stop=(kc == 7), rk=rk)
                        B.act(ogt[i % 2][:, :], psO[:, 0:512], AF.Sigmoid)
                        B.stq(og_s[par, i, :, :], ogt[i % 2][:, :], wk=[("og", par, i)])
                        B.act(u[:, :], psO[:, 512:1024], AF.Gelu_apprx_tanh)
                        B.act(va[:, :], psU[:, 0:512], AF.Gelu_apprx_tanh, accum=s1[:, 0:1])
                        B.ts("dve", s1[:, 1:2], s1[:, 0:1], -1.0 / 512, ALU.mult)
                        B.act(sq[:, :], va[:, :], AF.Square, bias=s1[:, 1:2], accum=s1[:, 2:3])
                        B.rstd(s1[:, 3:4], s1[:, 2:3], 512, s1[:, 2:3])
                        B.ts("dve", vc[:, :], va[:, :], s1[:, 1:2], ALU.add, s1[:, 3:4], ALU.mult)
                        for g in range(4):
                            B.mm(psU[:, 512 + g * 128:512 + (g + 1) * 128], wsT[:, g, :], vc[:, g * 128:(g + 1) * 128])
                        for g in range(4):
                            B.stt(sgt[i % 2][:, g * 128:(g + 1) * 128], psU[:, 512 + g * 128:512 + (g + 1) * 128], sgbT[:, g:g + 1],
                                  u[:, g * 128:(g + 1) * 128], ALU.add, ALU.mult)
                        B.stq(sgu_s[par, i, :, :], sgt[i % 2][:, :], wk=[("sgu", par, i)])
                step = [0]
                cur_r2 = [None]

                def scan_step(d, c):
                    k = step[0] % 2
                    step[0] += 1
                    dn, r2 = dn_[k], r2_[k]
                    cur_r2[0] = r2
                    msk = mskU if d == 0 else mskL
                    tk = slice(c * 128, (c + 1) * 128)
                    psA = B.psum(1); psAt = B.psum(1)
                    psAb = psAt[:, 0:512].bitcast(BF16)
                    for h in range(4):
                        B.mm(psA[:, h * 128:(h + 1) * 128], KT[:, h, tk], QT[:, h, tk])
                    for h in range(4):
                        B.tr(psAb[:, h * 128:(h + 1) * 128], KT[:, h, tk], ident_b[:, :])
                    for h in range(4):
                        B.stt(PT[k][:, h, :], psA[:, h * 128:(h + 1) * 128], EA[:, c, d * 4 + h:d * 4 + h + 1], msk[:, :], ALU.mult, ALU.mult)
                    for h in range(4):
                        B.act(Kw[k][:, h, :], psAb[:, h * 128:(h + 1) * 128], AF.Copy, scale=EAW[:, c, d * 4 + h:d * 4 + h + 1])
                    psB = B.psum(2)
                    for h in range(4):
                        B.mm(psB[:, h * 256:h * 256 + 129], PT[k][:, h, :], Vx[:, c, h, :], start=True, stop=False)
                        B.mm(psB[:, h * 256:h * 256 + 129], QT[:, h, tk], Cbf[:, d * 4 + h, :], start=False, stop=True)
                    B.tt("dve", dn[:, :], psB[:, 128::256], EB[:, c, d * 4:d * 4 + 4], ALU.mult)
                    B.act(dn[:, :], dn[:, :], AF.Abs)
                    B.ts("dve", dn[:, :], dn[:, :], 1.0, ALU.max)
                    B.recip(dn[:, :], dn[:, :])
                    B.tt("dve", r2[:, :], dn[:, :], EB[:, c, d * 4:d * 4 + 4], ALU.mult)
                    psD = B.psum(2)
                    for h in range(4):
                        B.mm(psD[:, h * 256:h * 256 + 129], Kw[k][:, h, :], Vx[:, c, h, :])
                    for h in range(4):
                        j = d * 4 + h
                        B.stt(C32[:, j, :], C32[:, j, :], EBT[:, c, j:j + 1], psD[:, h * 256:h * 256 + 129], ALU.mult, ALU.add)
                    B.cp("act", Cbf[:, d * 4:d * 4 + 4, :], C32[:, d * 4:d * 4 + 4, :])
                    return psB

                B.memset("pool", C32[:, :, :], 0.0)
                B.memset("pool", Cbf[:, :, :], 0.0)
                S.mark("L0 bwd")
                for c in [1, 0] + list(range(NT - 1, 1, -1)):
                    psB = scan_step(1, c)
                    y = yb[c % 2]
                    B.tt("dve", y[:, :].rearrange("p (h d) -> p h d", h=4), psB[:, :].rearrange("p (h d) -> p h d", d=256)[:, :, 0:128],
                         cur_r2[0][:, :].unsqueeze(2).to_broadcast([128, 4, 128]), ALU.mult)
                    B.stq(yb_s[par, c, :, :], y[:, :], wk=[("yb", par, c)])

                S.mark("L0 fwd+P4")
                with B.scope() as P4:
                    Wo = P4.sb("Wo", [128, 8, D], BF16)
                    B.ld(Wo[:, :, :], w_out0b.rearrange("(k p) n -> p k n", p=128), rk=[("w_out0b",)])
                    g2 = P4.sb("g2", [128, D], F32)
                    B.ld(g2[:, :], gsc[0, 0, NS, :].partition_broadcast(128), rk=[("gsc", 0, 0)])
                    ybl = [P4.sb("ybl%d" % i, [128, 512], F32) for i in range(2)]
                    ogl = [P4.sb("ogl%d" % i, [128, 512], F32) for i in range(2)]
                    ys_ = [P4.sb("ys", [128, 512], F32) for _i in range(2)]; sq4_ = [P4.sb("sq4", [128, 512], F32) for _i in range(2)]
                    st4_ = [P4.sb("st4", [128, 4], F32) for _i in range(2)]; st4b_ = [P4.sb("st4b", [128, 4], F32) for _i in range(2)]
                    st4c_ = [P4.sb("st4c", [128, 4], F32) for _i in range(2)]
                    cat = [P4.sb("cat%d" % i, [128, D], BF16) for i in range(2)]
                    catT_ = [P4.sb("catT", [128, 8, 128], BF16) for _i in range(2)]
                    for c in range(NT):
                        row = NS if c < 2 else b
                        if c == 2:
                            B.ld(g2[:, :], gsc[0, 0, b, :].partition_broadcast(128), rk=[("gsc", 0, 0)])
                        ct = cat[c % 2]
                        ys, sq, st4, st4b, st4c, catT = ys_[c % 2], sq4_[c % 2], st4_[c % 2], st4b_[c % 2], st4c_[c % 2], catT_[c % 2]
                        B.ld(ybl[c % 2][:, :], yb_s[par, c, :, :], rk=[("yb", par, c)])
                        B.ld(ogl[c % 2][:, :], og_s[par, c, :, :], rk=[("og", par, c)])
                        B.ld(ct[:, 0:512], sgu_s[par, c, :, :], rk=[("sgu", par, c)])
                        psB = scan_step(0, c)
                        B.tt("dve", ys[:, :].rearrange("p (h d) -> p h d", h=4), psB[:, :].rearrange("p (h d) -> p h d", d=256)[:, :, 0:128],
                             cur_r2[0][:, :].unsqueeze(2).to_broadcast([128, 4, 128]), ALU.mult)
                        B.tt("pool", ys[:, :], ys[:, :], ybl[c % 2][:, :], ALU.add)
                        B.tt("pool", ys[:, :], ys[:, :], ogl[c % 2][:, :], ALU.mult)
                        z3 = ys[:, :].rearrange("p (h d) -> p h d", h=4)
                        B.red(st4[:, :], z3)
                        B.ts("dve", st4[:, :], st4[:, :], 1.0 / 128, ALU.mult)
                        B.tt("dve", z3, z3, st4[:, :].unsqueeze(2).to_broadcast([128, 4, 128]), ALU.subtract)
                        B.tt("pool", sq[:, :], ys[:, :], ys[:, :], ALU.mult)
                        B.red(st4b[:, :], sq[:, :].rearrange("p (h d) -> p h d", h=4))
                        B.rstd(st4c[:, :], st4b[:, :], 128, st4b[:, :])
                        B.tt("dve", ct[:, 512:1024].rearrange("p (h d) -> p h d", h=4), z3, st4c[:, :].unsqueeze(2).to_broadcast([128, 4, 128]), ALU.mult)
                        psT = B.psum(1)
                        psTb = psT[:, :].bitcast(BF16)
                        for kc in range(8):
                            B.tr(psTb[:, kc * 128:(kc + 1) * 128], ct[:, kc * 128:(kc + 1) * 128], ident_b[:, :])
                        B.cp("act", catT[:, :, :], psTb[:, 0:1024].rearrange("p (k t) -> p k t", t=128))
                        psY = B.psum(2)
                        for half in range(2):
                            for kc in range(8):
                                B.mm(psY[:, half * 512:(half + 1) * 512], catT[:, kc, :], Wo[:, kc, half * 512:(half + 1) * 512], start=(kc == 0), stop=(kc == 7))
                        xt = xts[c % NX]
                        sap, skey = src_tile(0, b, c)
                        B.ld(xt[:, :], sap, rk=[skey] if skey else [])
                        post_residual(Wks[c % 2], psY, xt, g2, 0, xm_s[b, c * 128:(c + 1) * 128, :], ("xm", b, c), xt)
                        prenorm_to_hT(Wks[c % 2], xt, 0, 1, row, c, None)
            ffn(0, b, list(range(NT)))

        EPS_t = G.sb("EPS_t", [128, 1], F32)
        B.memset("pool", EPS_t[:, :], EPS)
        B.eps_ap = EPS_t[:, 0:1]
        LNS_ap = G.sb("LNS_ap", [128, 1], F32)
        B.memset("pool", LNS_ap[:, :], math.log(128.0 ** -0.5))

        lg = G.sb("lg", [128, 8], F32); e1 = G.sb("e1", [128, 8], F32); pp1 = G.sb("pp1", [128, 1], F32)
        xi = G.sb("xi", [128, 8], F32); zeta = G.sb("zeta", [128, 8], F32); colA = G.sb("colA", [128, 8], F32); g128 = G.sb("g128", [128, 8], F32)
        rtmp = G.sb("rtmp", [128, 8], F32)
        DmT = G.sb("DmT", [128, 8, 128], BF16)
        if 1 in layers:
            LNK = math.log(256.0 ** -0.5)
            B.act(rtmp[:, :], lgd[:, :], AF.Exp, scale=-1.0)
            B.act(lg[:, :], rtmp[:, :], AF.Ln, bias=1.0)
            B.ts("dve", lg[:, :], lg[:, :], -1.0, ALU.mult)
            ps = B.psum()
            B.mm(ps[:, 0:1], triU[:, :], ones_f[:, 0:1])
            B.cp("dve", pp1[:, :], ps[:, 0:1])
            B.ts("dve", e1[:, :], lg[:, :], pp1[:, 0:1], ALU.mult)
            B.act(g128[:, :], lg[:, :], AF.Exp, scale=128.0)
            B.act(xi[:, 0:4], e1[:, 0:4], AF.Exp)
            B.ts("dve", rtmp[:, 0:4], e1[:, 0:4], -1.0, ALU.mult, LNK, ALU.add)
            B.act(colA[:, 0:4], rtmp[:, 0:4], AF.Exp)
            B.stt(rtmp[:, 0:4], lg[:, 0:4], 128.0, e1[:, 0:4], ALU.mult, ALU.subtract)
            B.ts("dve", rtmp[:, 0:4], rtmp[:, 0:4], LNK, ALU.add)
            B.act(zeta[:, 0:4], rtmp[:, 0:4], AF.Exp)
            B.stt(rtmp[:, 4:8], lg[:, 4:8], 129.0, e1[:, 4:8], ALU.mult, ALU.subtract)
            B.act(xi[:, 4:8], rtmp[:, 4:8], AF.Exp)
            B.ts("dve", rtmp[:, 4:8], rtmp[:, 4:8], -1.0, ALU.mult, LNK, ALU.add)
            B.act(colA[:, 4:8], rtmp[:, 4:8], AF.Exp)
            B.stt(rtmp[:, 4:8], lg[:, 4:8], -1.0, e1[:, 4:8], ALU.mult, ALU.add)
            B.ts("dve", rtmp[:, 4:8], rtmp[:, 4:8], LNK, ALU.add)
            B.act(zeta[:, 4:8], rtmp[:, 4:8], AF.Exp)
            for j in range(8):
                B.ts("dve", DmT[:, j, :], (triU if j < 4 else triL)[:, :], colA[:, j:j + 1], ALU.mult)

        def layer1(b):
            par = b % 2
            with B.scope() as M:
                Wks, xts = LW["Wks"], LW["xts"]
                S.mark("L1 P1")
                for i in range(NT):
                    xt = xts[i % len(xts)]
                    sap, skey = src_tile(1, b, i)
                    B.ld(xt[:, :], sap, rk=[skey])
                    prenorm_to_hT(Wks[i % 2], xt, 1, 0, NS if i < 2 else b, i, None)
                allh = [("hT", i) for i in range(NT)]
                for h in range(4):
                    with B.scope() as H:
                        Wh = H.sb("Wh", [128, 8, 1536], BF16)
                        for (d0, s0, n) in ((0, h * 256, 256), (256, 1024 + h * 512, 512), (768, 3072 + h * 256, 256), (1024, 4096 + h * 512, 512)):
                            B.ld(Wh[:, :, d0:d0 + n], w_in1b[:, s0:s0 + n].rearrange("(k p) n -> p k n", p=128), rk=[("w_in1b",)])
                        KTh = H.sb("KTh", [128, 2, T], BF16); QTh = H.sb("QTh", [128, 2, T], BF16)
                        Vh = H.sb("Vh", [128, NT, 512], BF16)
                        R32_ = [H.sb("R32", [128, 2, 512], F32) for _i in range(2)]; Rbf_ = [H.sb("Rbf", [128, 2, 512], BF16) for _i in range(2)]
                        PT = [H.sb("PT%d" % i, [128, 128], BF16) for i in range(2)]
                        Kz = [H.sb("Kz%d" % i, [128, 256], BF16) for i in range(2)]
                        hgh = H.sb("hgh", [128, 512], F32)
                        B.ld(hgh[:, :], ret_head_g[0, h * 512:(h + 1) * 512].partition_broadcast(128))
                        S.mark("L1 h%d proj" % h)
                        for fc in range(4):
                            col0 = fc * 128 if fc < 2 else 768 + (fc - 2) * 128
                            dst = KTh if fc < 2 else QTh
                            blocks = ([(0, 256)] if fc < 2 else []) + [(256 + 512 * k, 512) for k in range(4)]
                            for bi in range(0, len(blocks), 2):
                                ps = B.psum(2)
                                for q, (t0, n) in enumerate(blocks[bi:bi + 2]):
                                    for kc in range(8):
                                        B.mm(ps[:, q * 512:q * 512 + n], Wh[:, kc, col0:col0 + 128], hT[:, kc, t0:t0 + n],
                                             start=(kc == 0), stop=(kc == 7), rk=[Wh] + allh)
                                    B.cp("act" if q == 0 else "dve", dst[:, fc % 2, t0:t0 + n], ps[:, q * 512:q * 512 + n])
                        for i in range(NT):
                            ps = B.psum()
                            for kc in range(8):
                                B.mm(ps[:, 0:512], hT[:, kc, i * 128:(i + 1) * 128], Wh[:, kc, 256:768], start=(kc == 0), stop=(kc == 7), rk=[Wh, ("hT", i)])
                            B.cp("act" if i % 2 == 0 else "dve", Vh[:, i, :], ps[:, 0:512])
                        step = [0]

                        def scan_step(d, c, with_q):
                            k = step[0] % 2
                            step[0] += 1
                            j = d * 4 + h
                            R32, Rbf = R32_[d], Rbf_[d]
                            tk = slice(c * 128, (c + 1) * 128)
                            psA = B.psum(1); psAt = B.psum(1)
                            psAb = psAt[:, 0:512].bitcast(BF16)
                            if with_q:
                                for kc in range(2):
                                    B.mm(psA[:, 0:128], KTh[:, kc, tk], QTh[:, kc, tk], start=(kc == 0), stop=(kc == 1))
                            for kc in range(2):
                                B.tr(psAb[:, kc * 128:(kc + 1) * 128], KTh[:, kc, tk], ident_b[:, :])
                            psB = None
                            if with_q:
                                B.tt("dve", PT[k][:, :], psA[:, 0:128], DmT[:, j, :], ALU.mult)
                            B.act(Kz[k][:, :], psAb[:, 0:256], AF.Copy, scale=zeta[:, j:j + 1])
                            if with_q:
                                psB = B.psum(1)
                                B.mm(psB[:, 0:512], PT[k][:, :], Vh[:, c, :], start=True, stop=False)
                                B.mm(psB[:, 0:512], QTh[:, 0, tk], Rbf[:, 0, :], start=False, stop=False)
                                B.mm(psB[:, 0:512], QTh[:, 1, tk], Rbf[:, 1, :], start=False, stop=True)
                            psD = B.psum(2)
                            for kc in range(2):
                                B.mm(psD[:, kc * 512:(kc + 1) * 512], Kz[k][:, kc * 128:(kc + 1) * 128], Vh[:, c, :])
                            B.stt(R32[:, :, :], R32[:, :, :], g128[:, j:j + 1], psD[:, :].rearrange("p (k v) -> p k v", k=2), ALU.mult, ALU.add)
                            B.cp("act", Rbf[:, :, :], R32[:, :, :])
                            return psB

                        S.mark("L1 h%d scan" % h)
                        with B.scope() as PB:
                            yb = [PB.sb("yb%d" % i, [128, 512], F32) for i in range(4)]
                            for d in range(2):
                                B.memset("pool", R32_[d][:, :, :], 0.0); B.memset("pool", Rbf_[d][:, :, :], 0.0)
                            bw = [1, 0] + list(range(NT - 1, 1, -1))
                            for kk in range(NT):
                                for d, c in ((1, bw[kk]), (0, kk)):
                                    psB = scan_step(d, c, c >= 2)
                                    if c >= 2:
                                        y = yb[(2 * kk + d) % 4]
                                        B.act(y[:, :], psB[:, 0:512], AF.Copy, scale=xi[:, d * 4 + h:d * 4 + h + 1])
                                        if d == 1:
                                            B.stq(yb_s[par, c, :, :], y[:, :], wk=[("yb", par, c)])
                                        else:
                                            B.stq(yf_s[par, c, :, :], y[:, :], wk=[("yf", par, c)])
                        S.mark("L1 h%d merge" % h)
                        with B.scope() as PF:
                            ND = 3
                            ybl = [PF.sb("ybl%d" % i, [128, 512], F32) for i in range(ND)]
                            yfl = [PF.sb("yfl%d" % i, [128, 512], F32) for i in range(ND)]
                            ys_ = [PF.sb("ys", [128, 512], F32) for _i in range(ND)]; sqj_ = [PF.sb("sqj", [128, 512], BF16) for _i in range(ND)]
                            sg_ = [PF.sb("sg", [128, 512], F32) for _i in range(ND)]
                            zt = [PF.sb("zt%d" % i, [128, 512], BF16) for i in range(ND)]
                            s1_ = [PF.sb("s1r", [128, 4], F32) for _i in range(ND)]
                            for c in range(2, NT):
                                tk = slice(c * 128, (c + 1) * 128)
                                ys, sqj, sg, s1 = ys_[c % ND], sqj_[c % ND], sg_[c % ND], s1_[c % ND]
                                B.ld(ybl[c % ND][:, :], yb_s[par, c, :, :], rk=[("yb", par, c)])
                                B.ld(yfl[c % ND][:, :], yf_s[par, c, :, :], rk=[("yf", par, c)])
                                B.tt("dve", ys[:, :], yfl[c % ND][:, :], ybl[c % ND][:, :], ALU.add)
                                psG = B.psum()
                                for kc in range(8):
                                    B.mm(psG[:, 0:512], hT[:, kc, tk], Wh[:, kc, 1024:1536], start=(kc == 0), stop=(kc == 7), rk=[Wh, ("hT", c)])
                                B.act(sg[:, :], psG[:, 0:512], AF.Silu)
                                B.tt("pool", sg[:, :], sg[:, :], hgh[:, :], ALU.mult)
                                B.red(s1[:, 0:1], ys[:, :])
                                B.ts("dve", s1[:, 1:2], s1[:, 0:1], -1.0 / 512, ALU.mult)
                                B.act(sqj[:, :], ys[:, :], AF.Square, bias=s1[:, 1:2], accum=s1[:, 2:3])
                                B.rstd(s1[:, 3:4], s1[:, 2:3], 512, s1[:, 2:3])
                                B.ts("dve", ys[:, :], ys[:, :], s1[:, 1:2], ALU.add, s1[:, 3:4], ALU.mult)
                                B.tt("dve", zt[c % ND][:, :], ys[:, :], sg[:, :], ALU.mult)
                                B.stq(z_s[par, (c - 2) * 128:(c - 1) * 128, h * 512:(h + 1) * 512], zt[c % ND][:, :], wk=[("z", par, c)])
                S.mark("L1 P4")
                with B.scope() as P4:
                    Wo = P4.sb("Wo1", [128, 16, D], BF16)
                    B.ld(Wo[:, :, :], w_out1b.rearrange("(k p) n -> p k n", p=128), rk=[("w_out1b",)])
                    g2 = P4.sb("g2", [128, D], F32)
                    B.ld(g2[:, :], gsc[1, 0, b, :].partition_broadcast(128), rk=[("gsc", 1, 0)])
                    zl = [P4.sb("zl%d" % i, [128, 2048], BF16) for i in range(2)]
                    zT_ = [P4.sb("zT", [128, 16, 128], BF16) for _i in range(2)]
                    for c in range(2, NT):
                        z = zl[c % 2]
                        zT = zT_[c % 2]
                        B.ld(z[:, :], z_s[par, (c - 2) * 128:(c - 1) * 128, :], rk=[("z", par, c)])
                        psT = B.psum(2)
                        psTb = psT[:, :].bitcast(BF16)
                        for kc in range(16):
                            B.tr(psTb[:, kc * 128:(kc + 1) * 128], z[:, kc * 128:(kc + 1) * 128], ident_b[:, :])
                        B.cp("act", zT[:, :, :], psTb[:, :].rearrange("p (k t) -> p k t", t=128))
                        psY = B.psum(2)
                        for half in range(2):
                            for kc in range(16):
                                B.mm(psY[:, half * 512:(half + 1) * 512], zT[:, kc, :], Wo[:, kc, half * 512:(half + 1) * 512], start=(kc == 0), stop=(kc == 15))
                        xt = xts[c % len(xts)]
                        sap, skey = src_tile(1, b, c)
                        B.ld(xt[:, :], sap, rk=[skey])
                        post_residual(Wks[c % 2], psY, xt, g2, 1, xm_s[b, c * 128:(c + 1) * 128, :], ("xm", b, c), xt)
                        prenorm_to_hT(Wks[c % 2], xt, 1, 1, b, c, None)
            ffn(1, b, list(range(2, NT)))

        LW = {}

        def layer_ws(sc, nx):
            Wks = []
            for _i in range(2):
                _w = {"junk": sc.sb("junk", [128, D], BF16), "ss": sc.sb("ss", [128, 2], F32), "tmp1": sc.sb("tmp1", [128, 2], F32),
                      "rs": sc.sb("rs", [128, 2], F32), "xn": sc.sb("xn", [128, D], F32)}
                _w["t1"] = _w["xn"]
                Wks.append(_w)
            LW["Wks"] = Wks
            LW["xts"] = [sc.sb("xt%d" % i, [128, D], F32) for i in range(nx)]

        if 0 in layers:
            with B.scope() as LG:
                layer_ws(LG, 3)
                for b in range(NS):
                    layer0(b)
                    if b == 0:
                        S.mark("deferred casts")
                        deferred_casts(dcf, dcb, 1024, 8)
        if 1 in layers:
            with B.scope() as LG:
                layer_ws(LG, 3)
                for b in range(NS):
                    layer1(b)
        S.finish()
    return nc, S


_CACHE = {}


def make_in_map(inp, b0, NS):
    f = lambda a: np.ascontiguousarray(np.asarray(a, dtype=np.float32))
    return {
        "x": f(inp["x"][b0:b0 + NS]), "c": f(inp["c"][b0:b0 + NS]), "ctx": f(inp["ctx"][b0:b0 + NS]),
        "c_ctx": f(inp["c_ctx"]).reshape(1, D),
        "ada_w": f(inp["ada_w"]), "ada_b": f(inp["ada_b"]),
        "pre_g": f(inp["pre_g"]).reshape(4, D), "post_g": f(inp["post_g"]).reshape(4, D),
        "ffn_up": f(inp["ffn_up"]), "ffn_conv": f(inp["ffn_conv"]).reshape(2 * 9 * NJ, 128), "ffn_down": f(inp["ffn_down"]),
        "ab_w_in": f(inp["ab_w_in"]).reshape(D, AB_PROJ), "ab_qk_conv": f(inp["ab_qk_conv"]).reshape(24, 128),
        "ab_gate_b": f(inp["ab_gate_b"]).reshape(1, 16), "ab_sgu_w": f(inp["ab_sgu_w"]).reshape(4, 128, 128),
        "ab_sgu_b": f(inp["ab_sgu_b"]).reshape(4, 128), "ab_head_g": f(inp["ab_head_g"]).reshape(1, 512),
        "ab_w_out": f(inp["ab_w_out"]).reshape(D, D), "ret_w_in": f(inp["ret_w_in"]).reshape(D, C_PROJ),
        "ret_decay": f(inp["ret_decay"]).reshape(1, 8), "ret_head_g": f(inp["ret_head_g"]).reshape(1, 2048),
        "ret_w_out": f(inp["ret_w_out"]).reshape(2048, D),
    }


def kernel(**inputs):
    NS = inputs["x"].shape[0] // N_CORES
    if "nc" not in _CACHE:
        _CACHE["nc"] = build_program(NS)[0]
    nc = _CACHE["nc"]
    in_maps = [make_in_map(inputs, i * NS, NS) for i in range(N_CORES)]
    res = run_bass_kernel_spmd(nc, in_maps, core_ids=list(range(N_CORES)))
    return np.concatenate([np.asarray(r["out"], dtype=np.float32) for r in res.results], axis=0)
```

```python
from contextlib import ExitStack
import math
import numpy as np
import concourse.bass as bass
import concourse.mybir as mybir
from concourse.bass_utils import run_bass_kernel_spmd

F32 = mybir.dt.float32
BF16 = mybir.dt.bfloat16
AF = mybir.ActivationFunctionType
ALU = mybir.AluOpType
AX = mybir.AxisListType

D = 1024
SEQ = 2048
CTX = 256
T = SEQ + CTX
NT = T // 128
DFF = 2816
NJ = DFF // 128
EPS = 1e-6
AB_PROJ = 3088
C_PROJ = 6144
N_CORES = 8
REORDER = True
FOLD_WAITS = True
POOL_TAP = False


def _esize(dt):
    return 4 if dt == F32 else 2


def _free_elems(ap):
    n = 1
    for d in ap.shape[1:]:
        n *= d
    return n


class Sched:
    LAT = 120.0

    def __init__(self, nc, stack, n_ld=48, n_st=4):
        self.nc = nc
        self.eng = {"pe": nc.tensor, "act": nc.scalar, "dve": nc.vector, "pool": nc.gpsimd, "sp": nc.sync}
        self.sem = {}
        for e in ("pe", "act", "dve", "pool"):
            self.sem[e] = stack.enter_context(nc.semaphore("s_" + e))
        self.dq = {"sp": [], "pool": []}
        for i in range(n_ld):
            k = "ld%d" % i
            self.sem[k] = stack.enter_context(nc.semaphore(k))
            self.dq["sp"].append(k)
        for i in range(n_st):
            k = "st%d" % i
            self.sem[k] = stack.enter_context(nc.semaphore(k))
            self.dq["pool"].append(k)
        self.nodes = []
        self.last_w = {}
        self.readers = {}
        self.rel_node = None
        self.nwaits = 0
        self.ninst = 0
        self.cnt = {e: 0 for e in self.eng}

    @staticmethod
    def keys(x):
        if isinstance(x, (str, tuple)):
            return [x]
        if not hasattr(x, "tensor"):
            return [x.name]
        name = x.tensor.name
        if name != "psall":
            return [name]
        es = _esize(x.dtype)
        off = (x.offset * es) % 16384
        ext = es
        for (stp, cnt) in x.ap[1:]:
            ext += (cnt - 1) * abs(stp) * es
        return [("ps", k) for k in range(off // 2048, (off + ext - 1) // 2048 + 1)]

    def _flat(self, lst):
        out = []
        for x in lst:
            if x is None:
                continue
            for k in self.keys(x):
                if k not in out:
                    out.append(k)
        return out

    def issue(self, e, fn, reads=(), writes=(), dma=False, cost=100.0, lat=0.0, fold=False):
        reads = self._flat(reads)
        writes = self._flat(writes)
        nid = len(self.nodes)
        preds = {}
        for r in reads:
            w = self.last_w.get(r)
            if w is not None:
                preds[w] = True
        for r in reads:
            if isinstance(r, tuple) and r[0] == "ps" and r not in writes:
                writes.append(r)
        for w in writes:
            p = self.last_w.get(w)
            if p is not None and p not in preds:
                preds[p] = False
            for rd in self.readers.get(w, ()):
                if rd not in preds:
                    preds[rd] = False
        preds.pop(nid, None)
        self.nodes.append([e, fn, dma, float(cost), float(lat), list(preds.items()), bool(fold) and FOLD_WAITS])
        for w in writes:
            self.last_w[w] = nid
            self.readers[w] = []
        for r in reads:
            if r not in writes:
                self.readers.setdefault(r, []).append(nid)
        return nid

    def mark(self, name):
        if not hasattr(self, "marks"):
            self.marks = []
        self.marks.append((name, len(self.nodes)))

    def release(self, names):
        preds = {}
        if self.rel_node is not None:
            preds[self.rel_node] = False
        for n in names:
            for k in self.keys(n):
                w = self.last_w.pop(k, None)
                if w is not None:
                    preds[w] = False
                for rd in self.readers.pop(k, ()):
                    preds[rd] = False
        nid = len(self.nodes)
        self.nodes.append([None, None, False, 0.0, 0.0, list(preds.items()), False])
        self.rel_node = nid

    def adopt(self, names):
        if self.rel_node is None:
            return
        for n in names:
            for k in self.keys(n):
                self.readers[k] = [self.rel_node]

    def schedule(self):
        import heapq
        N = len(self.nodes)
        npred = [0] * N
        succ = [[] for _ in range(N)]
        for i, nd in enumerate(self.nodes):
            npred[i] = len(nd[5])
            for (p, _) in nd[5]:
                succ[p].append(i)
        ready_t = [0.0] * N
        start = [0.0] * N
        fin = [0.0] * N
        efree = {e: 0.0 for e in self.eng}
        rq = {e: [] for e in self.eng}
        pend = []
        done = 0

        def make_ready(i, t):
            nd = self.nodes[i]
            if nd[0] is None:
                finish(i, t, t)
            else:
                heapq.heappush(rq[nd[0]], i)
                ready_t[i] = t

        def finish(i, ts, tf):
            nonlocal done
            start[i] = ts
            fin[i] = tf
            done += 1
            for sc in succ[i]:
                npred[sc] -= 1
                if ready_t[sc] < tf + self.LAT:
                    ready_t[sc] = tf + self.LAT
                if npred[sc] == 0:
                    heapq.heappush(pend, (ready_t[sc], sc))

        for i in range(N):
            if npred[i] == 0:
                heapq.heappush(pend, (0.0, i))
        now = 0.0
        while done < N:
            while pend and pend[0][0] <= now:
                t, i = heapq.heappop(pend)
                make_ready(i, t)
            if done >= N:
                break
            progressed = False
            for e in self.eng:
                if efree[e] <= now and rq[e]:
                    i = heapq.heappop(rq[e])
                    nd = self.nodes[i]
                    ts = now
                    efree[e] = ts + nd[3]
                    finish(i, ts, ts + nd[3] + nd[4])
                    progressed = True
            if progressed:
                continue
            cand = [efree[e] for e in self.eng if rq[e] and efree[e] > now]
            if pend:
                cand.append(pend[0][0])
            if not cand:
                raise RuntimeError("scheduler stuck (cyclic dependencies?)")
            now = max(now, min(cand))
        self.sim_ns = max(fin) if N else 0.0
        self.sim_start, self.sim_fin = start, fin
        if not REORDER:
            return list(range(N))
        return sorted(range(N), key=lambda i: (start[i], i))

    def emit(self):
        order = self.schedule()
        known = {e: {} for e in self.eng}
        ev = [None] * len(self.nodes)
        dval = {k: 0 for q in self.dq.values() for k in q}
        dclk = {k: {} for q in self.dq.values() for k in q}
        dnext = {"sp": 0, "pool": 0}

        self.trace = {e: [] for e in self.eng}
        self.nfold = 0

        pend_w = []

        def wait(e, s, v, clk):
            kn = known[e]
            if s is not None and kn.get(s, 0) < v:
                pend_w.append((s, v))
                self.nwaits += 1
                kn[s] = v
            for k2, v2 in clk.items():
                if kn.get(k2, 0) < v2:
                    kn[k2] = v2

        def flush(e, keep_last):
            last = None
            if keep_last and pend_w:
                last = pend_w.pop()
            for (s_, v_) in pend_w:
                self.eng[e].wait_ge(self.sem[s_], v_)
                self.trace[e].append(("w", s_, v_))
            del pend_w[:]
            return last

        for i in order:
            e, fn, dma, cost, lat, preds, fold = self.nodes[i]
            if e is None:
                clk = {}
                for (p, _) in preds:
                    s, v, c2 = ev[p]
                    if s is not None and clk.get(s, 0) < v:
                        clk[s] = v
                    if s is None:
                        for k2, v2 in c2.items():
                            if clk.get(k2, 0) < v2:
                                clk[k2] = v2
                ev[i] = (None, 0, clk)
                continue
            own = None if dma else e
            for (p, raw) in preds:
                s, v, clk = ev[p]
                if s is None:
                    for k2, v2 in clk.items():
                        if k2 == own and e == "pe":
                            continue
                        if known[e].get(k2, 0) < v2:
                            pend_w.append((k2, v2))
                            self.nwaits += 1
                            known[e][k2] = v2
                    continue
                if s == own and (e == "pe" or not raw):
                    continue
                wait(e, s, v, clk)
            if dma:
                q = self.dq[e]
                k = q[dnext[e] % len(q)]
                dnext[e] += 1
                if dval[k] > 0:
                    wait(e, k, dval[k], dclk[k])
                flush(e, False)
                ins = fn()
                dval[k] += 16
                ins.then_inc(self.sem[k], 16)
                self.trace[e].append(("i", k, 16))
                clk = dict(known[e])
                dclk[k] = clk
                ev[i] = (k, dval[k], clk)
            else:
                last = flush(e, fold)
                ins = fn()
                if last is not None:
                    ins._wait_ge(self.sem[last[0]], last[1])
                    self.trace[e].append(("w", last[0], last[1]))
                    self.nfold += 1
                self.cnt[e] += 1
                ins.then_inc(self.sem[e], 1)
                self.trace[e].append(("i", e, 1))
                ev[i] = (e, self.cnt[e], dict(known[e]))
            self.ninst += 1
        for k, v in dval.items():
            if v > 0 and known["sp"].get(k, 0) < v:
                self.eng["sp"].wait_ge(self.sem[k], v)
        for e in ("pe", "act", "dve", "pool"):
            if self.cnt[e] > 0:
                self.eng["sp"].wait_ge(self.sem[e], self.cnt[e])

    def finish(self):
        self.emit()


class PSView:
    def __init__(self, arena, bank, n):
        self.arena = arena
        self.c0 = bank * 512
        self.n = n

    def __getitem__(self, idx):
        rows, cols = idx
        a = 0 if cols.start is None else cols.start
        b = self.n * 512 if cols.stop is None else cols.stop
        if cols.step is None:
            return self.arena[rows, self.c0 + a:self.c0 + b]
        return self.arena[rows, self.c0 + a:self.c0 + b:cols.step]


class Bld:
    def __init__(self, nc, st):
        self.nc = nc
        self.S = Sched(nc, st)
        self.arena = st.enter_context(nc.psum_tensor("psall", [128, 4096], F32))
        self.ps_i = 0
        self.uid = 0
        probe = nc.alloc_sbuf_tensor("sb_probe", [128, 8], F32)
        self.sb_top = (nc.lookup_mloc(probe).addr + 32 + 31) // 32 * 32
        self.sb_limit = nc.SBUF_PARTITION_SIZE_BYTES
        self.sb_peak = self.sb_top

    def psum(self, n=1):
        if n == 2 and self.ps_i % 2 == 1:
            self.ps_i += 1
        v = PSView(self.arena, self.ps_i % 8, n)
        self.ps_i += n
        return v

    class Scope:
        def __init__(self, b):
            self.b = b
            self.st = ExitStack()
            self.names = []

        def __enter__(self):
            self.top0 = self.b.sb_top
            return self

        def sb(self, name, shape, dt):
            self.b.uid += 1
            nbytes = _esize(dt)
            for d in shape[1:]:
                nbytes *= d
            nbytes = (nbytes + 31) // 32 * 32
            off = self.b.sb_top
            assert off + nbytes <= self.b.sb_limit, "SBUF overflow: %s needs %d at %d" % (name, nbytes, off)
            t = self.b.nc.alloc_sbuf_tensor_at("%s_%d" % (name, self.b.uid), list(shape), dt, offset=off)
            self.b.sb_top = off + nbytes
            self.b.sb_peak = max(self.b.sb_peak, self.b.sb_top)
            self.names.append(t)
            self.b.S.adopt([t])
            return t

        def __exit__(self, *a):
            self.b.S.release(self.names)
            self.b.sb_top = self.top0
            return False

    def scope(self):
        return Bld.Scope(self)

    def _k(self, aps, override):
        return list(override) if override is not None else [a for a in aps if a is not None and not isinstance(a, (int, float))]

    def mm(self, out, lhsT, rhs, start=True, stop=True, rk=None, wk=None):
        nc = self.nc
        n = _free_elems(rhs)
        c = (max(n, 64) / 2.4 * (4.0 if rhs.dtype == F32 else 1.0) + 8) * 1.2
        return self.S.issue("pe", lambda: nc.tensor.matmul(out, lhsT=lhsT, rhs=rhs, start=start, stop=stop),
                            reads=self._k([lhsT, rhs], rk), writes=self._k([out], wk), cost=c, lat=60)

    def tr(self, out, in_, ident, rk=None, wk=None):
        nc = self.nc
        c = 128 / 2.4 * (2.0 if in_.dtype == F32 else 1.0) + 8
        return self.S.issue("pe", lambda: nc.tensor.transpose(out, in_, ident),
                            reads=self._k([in_, ident], rk), writes=self._k([out], wk), cost=c, lat=60)

    def act(self, out, in_, func, bias=0.0, scale=1.0, accum=None, rk=None, wk=None):
        nc = self.nc
        kw = {}
        if accum is not None:
            kw["accum_out"] = accum

        def f():
            return nc.scalar.activation(out=out, in_=in_, func=func, bias=bias, scale=scale, **kw)
        c = (200 + _free_elems(in_) / 1.4 + (100 if accum is not None else 0)) * 1.2
        return self.S.issue("act", f, reads=self._k([in_, bias, scale], rk), writes=self._k([out, accum], wk), cost=c, lat=60, fold=(accum is None))

    def _vc(self, e, ap):
        n = _free_elems(ap)
        return (90 + n / 0.96) * 1.14 if e == "dve" else (150 + n * 1.65)

    def tt(self, e, out, a, b, op, rk=None, wk=None):
        eng = self.S.eng[e]
        return self.S.issue(e, lambda: eng.tensor_tensor(out=out, in0=a, in1=b, op=op),
                            reads=self._k([a, b], rk), writes=self._k([out], wk), cost=self._vc(e, out), lat=60, fold=True)

    def ts(self, e, out, a, s1, op0, s2=None, op1=None, rk=None, wk=None):
        eng = self.S.eng[e]

        def f():
            if op1 is None:
                return eng.tensor_scalar(out=out, in0=a, scalar1=s1, scalar2=None, op0=op0)
            return eng.tensor_scalar(out=out, in0=a, scalar1=s1, scalar2=s2, op0=op0, op1=op1)
        return self.S.issue(e, f, reads=self._k([a, s1, s2], rk), writes=self._k([out], wk), cost=self._vc(e, out), lat=60, fold=True)

    def stt(self, out, a, s, b, op0, op1, rk=None, wk=None):
        nc = self.nc
        return self.S.issue("dve", lambda: nc.vector.scalar_tensor_tensor(out=out, in0=a, scalar=s, in1=b, op0=op0, op1=op1),
                            reads=self._k([a, s, b], rk), writes=self._k([out], wk), cost=self._vc("dve", out), lat=60, fold=True)

    def cp(self, e, out, in_, rk=None, wk=None):
        nc = self.nc
        if e == "act":
            f = lambda: nc.scalar.copy(out=out, in_=in_)
            c = 200 + _free_elems(out) / 1.4
        else:
            eng = self.S.eng[e]
            f = lambda: eng.tensor_copy(out=out, in_=in_)
            c = self._vc(e, out)
        return self.S.issue(e, f, reads=self._k([in_], rk), writes=self._k([out], wk), cost=c, lat=60, fold=True)

    def red(self, out, in_, op=ALU.add, rk=None, wk=None):
        nc = self.nc
        return self.S.issue("dve", lambda: nc.vector.tensor_reduce(out=out, in_=in_, axis=AX.X, op=op),
                            reads=self._k([in_], rk), writes=self._k([out], wk), cost=self._vc("dve", in_), lat=60, fold=True)

    def recip(self, out, in_):
        nc = self.nc
        return self.S.issue("dve", lambda: nc.vector.reciprocal(out=out, in_=in_), reads=[in_], writes=[out], cost=self._vc("dve", out), lat=60, fold=True)

    def memset(self, e, ap, val):
        eng = self.S.eng[e]
        return self.S.issue(e, lambda: eng.memset(ap, val), writes=[ap], cost=self._vc(e, ap), lat=60, fold=True)

    def ld(self, out, in_, rk=None, wk=None):
        nc = self.nc
        nbytes = out.shape[0] * _free_elems(out) * _esize(out.dtype)
        return self.S.issue("sp", lambda: nc.sync.dma_start(out=out, in_=in_), reads=self._k([], rk), writes=self._k([out], wk), dma=True,
                            cost=70, lat=2200 + nbytes / 160.0)

    def stq(self, out, in_, rk=None, wk=None):
        nc = self.nc
        nbytes = in_.shape[0] * _free_elems(in_) * _esize(in_.dtype)
        return self.S.issue("sp", lambda: nc.sync.dma_start(out=out, in_=in_), reads=self._k([in_], rk), writes=self._k([], wk), dma=True,
                            cost=70, lat=2200 + nbytes / 160.0)

    def rstd(self, out, ss, n, tmp):
        self.act(tmp, ss, AF.Sqrt, bias=self.eps_ap, scale=1.0 / n)
        self.recip(out, tmp)


def build_program(NS, dbg=False, layers=(0, 1)):
    nc = bass.Bass("TRN2", target_bir_lowering=False)
    R = NS + 1

    def din(name, shape):
        return nc.dram_tensor(name, list(shape), F32, kind="ExternalInput").ap()

    x_in = din("x", [NS, SEQ, D]); c_in = din("c", [NS, D]); ctx_in = din("ctx", [NS, CTX, D]); cctx_in = din("c_ctx", [1, D])
    ada_w = din("ada_w", [2, D, 6 * D]); ada_b = din("ada_b", [2, 6 * D]); pre_g = din("pre_g", [4, D]); post_g = din("post_g", [4, D])
    ffn_up = din("ffn_up", [2, D, 2 * DFF]); ffn_conv = din("ffn_conv", [2 * 9 * NJ, 128]); ffn_down = din("ffn_down", [2, DFF, D])
    ab_w_in = din("ab_w_in", [D, AB_PROJ]); ab_qk_conv = din("ab_qk_conv", [24, 128]); ab_gate_b = din("ab_gate_b", [1, 16])
    ab_sgu_w = din("ab_sgu_w", [4, 128, 128]); ab_sgu_b = din("ab_sgu_b", [4, 128]); ab_head_g = din("ab_head_g", [1, 512])
    ab_w_out = din("ab_w_out", [D, D]); ret_w_in = din("ret_w_in", [D, C_PROJ]); ret_decay = din("ret_decay", [1, 8])
    ret_head_g = din("ret_head_g", [1, 2048]); ret_w_out = din("ret_w_out", [2048, D])
    out_t = nc.dram_tensor("out", [NS, SEQ, D], F32, kind="ExternalOutput").ap()

    skind = "ExternalOutput" if dbg else "Internal"

    def dscr(name, shape, dt, k=None):
        return nc.dram_tensor(name, list(shape), dt, kind=k or "Internal").ap()

    w_in0b = dscr("w_in0b", [D, AB_PROJ], BF16); w_out0b = dscr("w_out0b", [D, D], BF16)
    w_in1b = dscr("w_in1b", [D, C_PROJ], BF16); w_out1b = dscr("w_out1b", [2048, D], BF16)
    up_r = dscr("up_r", [2, NJ, 128, 8, 2, 128], BF16); down_b = dscr("down_b", [2, DFF, D], BF16)
    gsc = dscr("gsc", [2, 2, R, D], F32)
    xm_s = dscr("xm_s", [NS, T, D], F32, skind)
    xo_s = dscr("xo_s", [NS, T, D], F32, skind)
    yb_s = dscr("yb_s", [2, NT, 128, 512], F32)
    og_s = dscr("og_s", [2, NT, 128, 512], F32)
    sgu_s = dscr("sgu_s", [2, NT, 128, 512], BF16)
    act_s = dscr("act_s", [2, NJ, 128, T], BF16)
    yf_s = dscr("yf_s", [2, NT, 128, 512], F32)
    z_s = dscr("z_s", [2, SEQ, 2048], BF16)

    with ExitStack() as st:
        B = Bld(nc, st)
        S = B.S
        G = B.scope()
        st.enter_context(G)
        ident_b = G.sb("ident_b", [128, 128], BF16); ident_f = G.sb("ident_f", [128, 128], F32)
        triU = G.sb("triU", [128, 128], F32); triL = G.sb("triL", [128, 128], F32); ones_f = G.sb("ones_f", [128, 128], F32)
        mskU = G.sb("mskU", [128, 128], BF16); mskL = G.sb("mskL", [128, 128], BF16)
        B.memset("pool", ones_f[:, :], 1.0)
        B.memset("pool", triU[:, :], 1.0); B.memset("pool", triL[:, :], 1.0); B.memset("pool", ident_f[:, :], 1.0)
        S.issue("pool", lambda: nc.gpsimd.affine_select(out=triU[:, :], in_=triU[:, :], pattern=[[1, 128]], compare_op=ALU.is_ge, fill=0.0, base=0, channel_multiplier=-1), reads=[triU], writes=[triU])
        S.issue("pool", lambda: nc.gpsimd.affine_select(out=triL[:, :], in_=triL[:, :], pattern=[[-1, 128]], compare_op=ALU.is_ge, fill=0.0, base=0, channel_multiplier=1), reads=[triL], writes=[triL])
        B.tt("pool", ident_f[:, :], triU[:, :], triL[:, :], ALU.mult)
        B.cp("pool", ident_b[:, :], ident_f[:, :]); B.cp("pool", mskU[:, :], triU[:, :]); B.cp("pool", mskL[:, :], triL[:, :])

        hT = G.sb("hT", [128, 8, T], BF16)
        sT = G.sb("sT", [128, 8, R], F32)
        modsT = G.sb("modsT", [128, 2, 48, R], F32)
        pre_gT = G.sb("pre_gT", [128, 32], F32)
        qkcT = G.sb("qkcT", [128, 24], F32)
        fcvT = G.sb("fcvT", [128, 2 * 9 * NJ], F32)
        sgbT = G.sb("sgbT", [128, 4], F32)
        g1c = G.sb("g1c", [128, 2, 2, R, 8], F32)
        gateb = G.sb("gateb", [128, 16], F32)
        lgd = G.sb("lgd", [128, 8], F32)
        wsT = G.sb("wsT", [128, 4, 128], BF16)

        def rowsT(dst, src_ap, nrows, P):
            with B.scope() as sc:
                stg = sc.sb("rt_stg", [128, 128], F32)
                for r0 in range(0, nrows, 128):
                    n = min(128, nrows - r0)
                    B.ld(stg[0:n, :], src_ap[r0:r0 + n, :])
                    ps = B.psum()
                    B.tr(ps[:, 0:n], stg[0:n, :], ident_f[0:n, 0:n])
                    B.cp("dve", dst[:, r0:r0 + n], ps[:, 0:n])

        cast_cnt = [0]
        cast_gate = [[]]

        def cast_mat(src, dst, key, stf, stb, W):
            for r0 in range(0, src.shape[0], 128):
                for c0 in range(0, src.shape[1], W):
                    n = min(W, src.shape[1] - c0)
                    i = cast_cnt[0] % len(stf)
                    cast_cnt[0] += 1
                    B.ld(stf[i][:, 0:n], src[r0:r0 + 128, c0:c0 + n], rk=cast_gate[0])
                    B.cp(("act", "dve", "pool")[cast_cnt[0] % 3], stb[i][:, 0:n], stf[i][:, 0:n])
                    B.stq(dst[r0:r0 + 128, c0:c0 + n], stb[i][:, 0:n], wk=[key])

        def cast_up(l, stf, stb, JB):
            for kc in range(8):
                for half in range(2):
                    for j0 in range(0, NJ, JB):
                        nj = min(JB, NJ - j0)
                        n = nj * 128
                        i = cast_cnt[0] % len(stf)
                        cast_cnt[0] += 1
                        B.ld(stf[i][:, 0:n], ffn_up[l, kc * 128:(kc + 1) * 128, half * DFF + j0 * 128:half * DFF + j0 * 128 + n], rk=cast_gate[0])
                        B.cp(("act", "dve", "pool")[cast_cnt[0] % 3], stb[i][:, 0:n], stf[i][:, 0:n])
                        B.stq(up_r[l, j0:j0 + nj, :, kc, half, :].rearrange("j p c -> p j c"),
                              stb[i][:, 0:n].rearrange("p (j c) -> p j c", c=128), wk=[("up_r", l)])

        def deferred_casts(stf, stb, W, JB):
            cast_gate[0] = [("defer_gate",)] if 0 in layers else []
            if 1 in layers:
                cast_mat(ret_w_in, w_in1b, ("w_in1b",), stf, stb, W)
                cast_mat(ret_w_out, w_out1b, ("w_out1b",), stf, stb, W)
                cast_up(1, stf, stb, JB)
                cast_mat(ffn_down[1], down_b[1], ("down_b", 1), stf, stb, W)

        dgate = G.sb("dgate", [128, 1], F32)
        dcf = [G.sb("dcf%d" % i, [128, 1024], F32) for i in range(2)]
        dcb = [G.sb("dcb%d" % i, [128, 1024], BF16) for i in range(2)]

        S.mark("prologue")
        with B.scope() as P:
            rowsT(pre_gT, pre_g.rearrange("a (c p) -> (a c) p", p=128), 32, P)
            rowsT(qkcT, ab_qk_conv, 24, P)
            rowsT(fcvT, ffn_conv, 2 * 9 * NJ, P)
            rowsT(sgbT, ab_sgu_b, 4, P)
            B.ld(gateb[:, :], ab_gate_b[0, :].partition_broadcast(128))
            B.ld(lgd[:, :], ret_decay[0, :].partition_broadcast(128))
            wstg = P.sb("wstg", [128, 128], F32)
            for g in range(4):
                B.ld(wstg[:, :], ab_sgu_w[g, :, :])
                ps = B.psum()
                B.tr(ps[:, 0:128], wstg[:, :], ident_f[:, :])
                B.cp("act", wsT[:, g, :], ps[:, 0:128])
            crow = P.sb("crow", [R, D], F32)
            B.ld(crow[0:NS, :], c_in[:, :])
            B.ld(crow[NS:R, :], cctx_in[:, :])
            srow = P.sb("srow", [R, D], F32)
            B.act(srow[:, :], crow[:, :], AF.Silu)
            ps = B.psum()
            for kc in range(8):
                B.tr(ps[:, kc * 8:kc * 8 + R], srow[:, kc * 128:(kc + 1) * 128], ident_f[0:R, 0:R])
            B.cp("dve", sT[:, :, :], ps[:, 0:64].rearrange("p (k r) -> p k r", r=8)[:, :, 0:R])
            adbT = P.sb("adbT", [128, 96], F32)
            rowsT(adbT, ada_b.rearrange("l (c p) -> (l c) p", p=128), 96, P)
            adw = [P.sb("adw%d" % i, [128, 8, D], F32) for i in range(2)]
            brow = P.sb("brow", [R, D], F32); grow = P.sb("grow", [R, D], F32); prow = P.sb("prow", [R, D], F32)
            for l in range(2):
                for m in range(6):
                    w = adw[(l * 6 + m) % 2]
                    B.ld(w[:, :, :], ada_w[l, :, m * D:(m + 1) * D].rearrange("(k p) n -> p k n", p=128))
                    ps = B.psum()
                    for fc in range(8):
                        for kc in range(8):
                            B.mm(ps[:, fc * 8:fc * 8 + R], w[:, kc, fc * 128:(fc + 1) * 128], sT[:, kc, :], start=(kc == 0), stop=(kc == 7))
                    for fc in range(8):
                        B.ts("dve", modsT[:, l, m * 8 + fc, :], ps[:, fc * 8:fc * 8 + R], adbT[:, l * 48 + m * 8 + fc:l * 48 + m * 8 + fc + 1], ALU.add)
                    if m in (2, 5):
                        which = 0 if m == 2 else 1
                        ps2 = B.psum(2)
                        for half in range(2):
                            for kc in range(8):
                                B.mm(ps2[0:R, half * 512:(half + 1) * 512], sT[:, kc, :], w[:, kc, half * 512:(half + 1) * 512], start=(kc == 0), stop=(kc == 7))
                        B.ld(brow[:, :], ada_b[l, m * D:(m + 1) * D].partition_broadcast(R))
                        B.ld(prow[:, :], post_g[l * 2 + which, :].partition_broadcast(R))
                        B.tt("dve", grow[:, :], ps2[0:R, :], brow[:, :], ALU.add)
                        B.tt("dve", grow[:, :], grow[:, :], prow[:, :], ALU.mult)
                        B.stq(gsc[l, which, :, :], grow[:, :], wk=[("gsc", l, which)])
                for which in range(2):
                    msc = 1 if which == 0 else 4
                    for r in range(R):
                        B.stt(g1c[:, l, which, r, :], modsT[:, l, msc * 8:(msc + 1) * 8, r], 1.0,
                              pre_gT[:, (l * 2 + which) * 8:(l * 2 + which) * 8 + 8], ALU.add, ALU.mult)

            NSTG = 4
            cst_f = [P.sb("cst_f%d" % i, [128, 2048], F32) for i in range(NSTG)]
            cst_b = [P.sb("cst_b%d" % i, [128, 2048], BF16) for i in range(NSTG)]
            if 0 in layers:
                cast_mat(ab_w_in, w_in0b, ("w_in0b",), cst_f, cst_b, 2048)
                cast_mat(ab_w_out, w_out0b, ("w_out0b",), cst_f, cst_b, 2048)
                cast_up(0, cst_f, cst_b, 11)
                cast_mat(ffn_down[0], down_b[0], ("down_b", 0), cst_f, cst_b, 2048)
            if 0 not in layers:
                deferred_casts(cst_f, cst_b, 2048, 11)

        def src_tile(layer, b, i):
            if layer == 0:
                if i < 2:
                    return ctx_in[b, i * 128:(i + 1) * 128, :], None
                return x_in[b, (i - 2) * 128:(i - 1) * 128, :], None
            return xo_s[b, i * 128:(i + 1) * 128, :], ("xo", b, i)

        def prenorm_to_hT(W, xt, layer, which, row, i, tagk):
            junk, ss, tmp1, rs, xn = W["junk"], W["ss"], W["tmp1"], W["rs"], W["xn"]
            B.act(junk[:, :], xt[:, :], AF.Square, accum=ss[:, 0:1])
            B.rstd(rs[:, 0:1], ss[:, 0:1], D, tmp1[:, 0:1])
            B.ts("dve", xn[:, :], xt[:, :], rs[:, 0:1], ALU.mult)
            msh = 0 if which == 0 else 3
            ps = B.psum(2)
            for kc in range(8):
                B.tr(ps[:, kc * 128:(kc + 1) * 128], xn[:, kc * 128:(kc + 1) * 128], ident_f[:, :])
            for kc in range(8):
                B.act(hT[:, kc, i * 128:(i + 1) * 128], ps[:, kc * 128:(kc + 1) * 128], AF.Identity,
                      bias=modsT[:, layer, msh * 8 + kc, row:row + 1], scale=g1c[:, layer, which, row, kc:kc + 1],
                      wk=[("hT", i)])

        def post_residual(W, psY, xt, gt, layer, out_ap, out_key, xm):
            junk, ss, tmp1, rs, t1 = W["junk"], W["ss"], W["tmp1"], W["rs"], W["t1"]
            B.act(junk[:, :], psY[:, :], AF.Square, accum=ss[:, 1:2])
            B.rstd(rs[:, 1:2], ss[:, 1:2], D, tmp1[:, 1:2])
            B.stt(t1[:, :], psY[:, :], rs[:, 1:2], gt[:, :], ALU.mult, ALU.mult)
            B.tt("pool", xm[:, :], t1[:, :], xt[:, :], ALU.add)
            B.stq(out_ap, xm[:, :], wk=[out_key])

        def ffn(layer, b, tiles):
            par = b % 2
            has_ctx = 0 in tiles
            S.mark("L%d ffn_up" % layer)
            with B.scope() as F:
                gp = [F.sb("gp%d" % i, [128, 34, 66], F32) for i in range(2)]
                gc_ = [F.sb("gc", [128, 258], F32) for _i in range(2)]
                ptap_ = [F.sb("ptap", [128, SEQ], F32) for _i in range(2)] if POOL_TAP else [None, None]
                ab_ = [F.sb("ab%d" % i, [128, T], F32) for i in range(2)]
                acc_ = [F.sb("acc", [128, T], F32) for _i in range(2)]
                gl_ = [F.sb("gl", [128, T], F32) for _i in range(2)]
                ao = [F.sb("ao%d" % i, [128, T], BF16) for i in range(2)]
                wj = [F.sb("wj%d" % i, [128, 8, 256], BF16) for i in range(2)]
                for i in range(2):
                    B.memset("pool", gp[i][:, :, :], 0.0)
                for _i in range(2):
                    B.memset("pool", gc_[_i][:, :], 0.0)
                blocks = ([(0, 256)] if has_ctx else []) + [(256 + 512 * k, 512) for k in range(4)]
                for j in range(NJ):
                    w = wj[j % 2]
                    B.ld(w[:, :, :], up_r[layer, j, :, :, :, :].rearrange("p k h c -> p k (h c)"), rk=[("up_r", layer)])
                    gpj, abj, aoj = gp[j % 2], ab_[j % 2], ao[j % 2]
                    acc, gl, gc, ptap = acc_[j % 2], gl_[j % 2], gc_[j % 2], ptap_[j % 2]
                    for (t0, n) in blocks:
                        ps = B.psum(2)
                        rk = [w] + [("hT", t0 // 128 + q) for q in range(n // 128)]
                        for kc in range(8):
                            B.mm(ps[:, 0:n], w[:, kc, 128:256], hT[:, kc, t0:t0 + n], start=(kc == 0), stop=(kc == 7), rk=rk)
                        for kc in range(8):
                            B.mm(ps[:, 512:512 + n], w[:, kc, 0:128], hT[:, kc, t0:t0 + n], start=(kc == 0), stop=(kc == 7), rk=rk)
                        if t0 == 0:
                            B.cp("act", gc[:, 1:257], ps[:, 0:256])
                        else:
                            r0 = (t0 - 256) // 64
                            B.cp("act", gpj[:, 1 + r0:9 + r0, 1:65], ps[:, 0:512].rearrange("p (r c) -> p r c", c=64))
                        B.cp("act", abj[:, t0:t0 + n], ps[:, 512:512 + n])
                    accv = acc[:, 256:T].rearrange("p (r c) -> p r c", c=64)
                    first = True
                    for ty in range(3):
                        for tx in range(3):
                            wcol = fcvT[:, (layer * 9 + ty * 3 + tx) * NJ + j:(layer * 9 + ty * 3 + tx) * NJ + j + 1]
                            src = gpj[:, ty:ty + 32, tx:tx + 64]
                            if POOL_TAP and ty == 2 and tx == 2:
                                B.ts("pool", ptap[:, :].rearrange("p (r c) -> p r c", c=64), src, wcol, ALU.mult)
                                continue
                            if first:
                                B.act(accv, src, AF.Copy, scale=wcol)
                                first = False
                            else:
                                B.stt(accv, src, wcol, accv, ALU.mult, ALU.add)
                    if has_ctx:
                        for tx in range(3):
                            wcol = fcvT[:, (layer * 9 + 3 + tx) * NJ + j:(layer * 9 + 3 + tx) * NJ + j + 1]
                            if tx == 0:
                                B.act(acc[:, 0:256], gc[:, 0:256], AF.Copy, scale=wcol)
                            else:
                                B.stt(acc[:, 0:256], gc[:, tx:tx + 256], wcol, acc[:, 0:256], ALU.mult, ALU.add)
                    t00 = 0 if has_ctx else 256
                    if POOL_TAP:
                        B.tt("pool", acc[:, 256:T], acc[:, 256:T], ptap[:, :], ALU.add)
                    B.act(gl[:, t00:T], acc[:, t00:T], AF.Gelu_apprx_tanh)
                    B.tt("pool", aoj[:, t00:T], gl[:, t00:T], abj[:, t00:T], ALU.mult)
                    B.stq(act_s[par, j, :, t00:T], aoj[:, t00:T], wk=[("act_s", par, j)])
            S.mark("L%d ffn_down" % layer)
            with B.scope() as F:
                wd = F.sb("wd", [128, NJ, D], BF16)
                B.ld(wd[:, :, :], down_b[layer].rearrange("(j p) n -> p j n", p=128), rk=[("down_b", layer)])
                ablk = [F.sb("ablk%d" % i, [128, NJ, 256], BF16) for i in range(2)]
                g5 = F.sb("g5", [128, D], F32); g5c = F.sb("g5c", [128, D], F32)
                B.ld(g5[:, :], gsc[layer, 1, b, :].partition_broadcast(128), rk=[("gsc", layer, 1)])
                if has_ctx:
                    B.ld(g5c[:, :], gsc[layer, 1, NS, :].partition_broadcast(128), rk=[("gsc", layer, 1)])
                Wks = [{"junk": F.sb("f_junk", [128, D], BF16), "ss": F.sb("f_ss", [128, 2], F32), "tmp1": F.sb("f_tmp1", [128, 2], F32),
                        "rs": F.sb("f_rs", [128, 2], F32), "t1": F.sb("f_t1", [128, D], F32)} for _i in range(2)]
                xts = [F.sb("f_xt%d" % i, [128, D], F32) for i in range(2)]
                xos = [F.sb("f_xo%d" % i, [128, D], F32) for i in range(2)]
                nb = 0
                for t0 in range(0 if has_ctx else 256, T, 256):
                    blk = ablk[nb % 2]
                    nb += 1
                    B.ld(blk[:, :, :], act_s[par, :, :, t0:t0 + 256].rearrange("j p t -> p j t"), rk=[("act_s", par, j) for j in range(NJ)])
                    for q in range(2):
                        i = t0 // 128 + q
                        xt = xts[i % 2]; xo = xos[i % 2]
                        B.ld(xt[:, :], xm_s[b, i * 128:(i + 1) * 128, :], rk=[("xm", b, i)])
                        ps = B.psum(2)
                        for half in range(2):
                            for j in range(NJ):
                                B.mm(ps[:, half * 512:(half + 1) * 512], blk[:, j, q * 128:(q + 1) * 128], wd[:, j, half * 512:(half + 1) * 512],
                                     start=(j == 0), stop=(j == NJ - 1))
                        if layer == 1:
                            oap, okey = out_t[b, (i - 2) * 128:(i - 1) * 128, :], ("out", b, i)
                        else:
                            oap, okey = xo_s[b, i * 128:(i + 1) * 128, :], ("xo", b, i)
                        post_residual(Wks[i % 2], ps, xt, g5c if i < 2 else g5, layer, oap, okey, xo)

        def layer0(b):
            par = b % 2
            with B.scope() as M:
                KT = M.sb("KT", [128, 4, T], BF16); QT = M.sb("QT", [128, 4, T], BF16)
                Vx = M.sb("Vx", [128, NT, 4, 129], BF16)
                Gt = M.sb("Gt", [128, NT, 16], F32)
                EA = M.sb("EA", [128, NT, 8], F32); EAW = M.sb("EAW", [128, NT, 8], F32)
                EB = M.sb("EB", [128, NT, 8], F32); EBT = M.sb("EBT", [128, NT, 8], F32)
                C32 = M.sb("C32", [128, 8, 129], F32); Cbf = M.sb("Cbf", [128, 8, 129], BF16)
                Wks, xts = LW["Wks"], LW["xts"]
                NX = len(xts)
                yb = [M.sb("yb%d" % i, [128, 512], F32) for i in range(2)]
                PT = [M.sb("PT%d" % i, [128, 4, 128], BF16) for i in range(2)]
                Kw = [M.sb("Kw%d" % i, [128, 4, 128], BF16) for i in range(2)]
                dn_ = [M.sb("dn", [128, 4], F32) for _i in range(2)]; r2_ = [M.sb("r2", [128, 4], F32) for _i in range(2)]
                S.mark("L0 P1")
                for i in range(NT):
                    xt = xts[i % NX]
                    sap, skey = src_tile(0, b, i)
                    B.ld(xt[:, :], sap, rk=[skey] if skey else [])
                    prenorm_to_hT(Wks[i % 2], xt, 0, 0, NS if i < 2 else b, i, None)
                allh = [("hT", i) for i in range(NT)]
                S.mark("L0 P2a")
                with B.scope() as P2:
                    Wa = P2.sb("Wa", [128, 8, 1024], BF16)
                    B.ld(Wa[:, :, 0:512], w_in0b[:, 0:512].rearrange("(k p) n -> p k n", p=128), rk=[("w_in0b",)])
                    B.ld(Wa[:, :, 512:1024], w_in0b[:, 1040:1552].rearrange("(k p) n -> p k n", p=128), rk=[("w_in0b",)])
                    raw = [P2.sb("raw%d" % i, [128, T + 4], F32) for i in range(2)]
                    cacc = P2.sb("cacc", [128, T + 4], F32)
                    for i in range(2):
                        B.memset("pool", raw[i][:, :], 0.0)
                    for fc in range(8):
                        col0 = fc * 128
                        cch = 4 + fc if fc < 4 else fc - 4
                        rw = raw[fc % 2]
                        blocks = [(0, 256, 1)] + [(256 + 512 * k, 512, 259 + 512 * k) for k in range(4)]
                        for bi in range(0, len(blocks), 2):
                            ps = B.psum(2)
                            for q, (t0, n, off) in enumerate(blocks[bi:bi + 2]):
                                for kc in range(8):
                                    B.mm(ps[:, q * 512:q * 512 + n], Wa[:, kc, col0:col0 + 128], hT[:, kc, t0:t0 + n],
                                         start=(kc == 0), stop=(kc == 7), rk=[Wa] + allh)
                                B.cp("act", rw[:, off:off + n], ps[:, q * 512:q * 512 + n])
                        L = T + 4
                        B.ts("dve", cacc[:, 1:L - 1], rw[:, 0:L - 2], qkcT[:, cch:cch + 1], ALU.mult)
                        B.stt(cacc[:, 1:L - 1], rw[:, 1:L - 1], qkcT[:, 8 + cch:9 + cch], cacc[:, 1:L - 1], ALU.mult, ALU.add)
                        B.stt(cacc[:, 1:L - 1], rw[:, 2:L], qkcT[:, 16 + cch:17 + cch], cacc[:, 1:L - 1], ALU.mult, ALU.add)
                        dst = KT if fc < 4 else QT
                        B.act(dst[:, fc % 4, 0:256], cacc[:, 1:257], AF.Silu)
                        B.act(dst[:, fc % 4, 256:T], cacc[:, 259:259 + SEQ], AF.Silu)
                S.mark("L0 P2b")
                with B.scope() as P2:
                    Wr = P2.sb("Wr", [128, 8, 2064], BF16)
                    B.ld(Wr[:, :, 0:528], w_in0b[:, 512:1040].rearrange("(k p) n -> p k n", p=128), rk=[("w_in0b",)])
                    B.ld(Wr[:, :, 528:2064], w_in0b[:, 1552:3088].rearrange("(k p) n -> p k n", p=128), rk=[("w_in0b",)])
                    ogt = [P2.sb("ogt%d" % i, [128, 512], F32) for i in range(2)]
                    sgt = [P2.sb("sgt%d" % i, [128, 512], BF16) for i in range(2)]
                    u_ = [P2.sb("u", [128, 512], F32) for _i in range(2)]; va_ = [P2.sb("va", [128, 512], F32) for _i in range(2)]
                    sq_ = [P2.sb("sq", [128, 512], BF16) for _i in range(2)]
                    vc_ = [P2.sb("vc", [128, 512], BF16) for _i in range(2)]; s1_ = [P2.sb("s1", [128, 4], F32) for _i in range(2)]
                    def gates_and_prep():
                        S.mark("L0 gates")
                        psg = B.psum(1)
                        for i in range(NT):
                            for kc in range(8):
                                B.mm(psg[:, i * 16:(i + 1) * 16], hT[:, kc, i * 128:(i + 1) * 128], Wr[:, kc, 512:528], start=(kc == 0), stop=(kc == 7), rk=[Wr, ("hT", i)])
                        B.tt("dve", Gt[:, :, :], psg[:, 0:NT * 16].rearrange("p (t g) -> p t g", g=16), gateb[:, :].unsqueeze(1).to_broadcast([128, NT, 16]), ALU.add)
                        E1 = P2.sb("E1", [128, NT, 8], F32); Pn = P2.sb("Pn", [128, NT, 8], F32); A1 = P2.sb("A1", [128, NT, 8], F32); A2 = P2.sb("A2", [128, NT, 8], F32)
                        gv = Gt[:, :, :].rearrange("p t (g h) -> p t g h", h=4)
                        for d in range(2):
                            B.act(E1[:, :, d * 4:(d + 1) * 4], gv[:, :, 1 + 2 * d, :], AF.Exp, scale=-1.0)
                        B.act(Pn[:, :, :], E1[:, :, :], AF.Ln, bias=1.0)
                        ps = B.psum(1)
                        B.mm(ps[:, 0:72], triU[:, :], Pn[:, :, 0:4])
                        B.mm(ps[:, 72:144], triL[:, :], Pn[:, :, 4:8])
                        B.mm(ps[:, 144:288], ones_f[:, :], Pn[:, :, :])
                        for d in range(2):
                            B.tt("dve", A1[:, :, d * 4:(d + 1) * 4], gv[:, :, 2 * d, :], ps[:, d * 72:(d + 1) * 72].rearrange("p (t h) -> p t h", h=4), ALU.add)
                        B.tt("dve", A2[:, :, :], A1[:, :, :], ps[:, 144:288].rearrange("p (t j) -> p t j", j=8), ALU.subtract)
                        B.act(EA[:, :, :], A1[:, :, :], AF.Exp, bias=LNS_ap[:, 0:1])
                        B.act(EAW[:, :, :], A2[:, :, :], AF.Exp, bias=LNS_ap[:, 0:1])
                        for d in range(2):
                            B.act(EB[:, :, d * 4:(d + 1) * 4], ps[:, d * 72:(d + 1) * 72].rearrange("p (t h) -> p t h", h=4), AF.Exp, scale=-1.0)
                        B.act(EBT[:, :, :], ps[:, 144:288].rearrange("p (t j) -> p t j", j=8), AF.Exp, scale=-1.0)
                        if b == 0:
                            B.cp("pool", dgate[:, :], EBT[:, 0, 0:1], wk=[dgate, ("defer_gate",)])

                    B.memset("pool", Vx[:, :, :, :], 1.0)
                    for i in range(NT):
                        tk = slice(i * 128, (i + 1) * 128)
                        psV = B.psum(1)
                        for kc in range(8):
                            B.mm(psV[:, 0:512], hT[:, kc, tk], Wr[:, kc, 0:512], start=(kc == 0), stop=(kc == 7), rk=[Wr, ("hT", i)])
                        B.cp("act" if i % 2 == 0 else "dve", Vx[:, i, :, 0:128], psV[:, 0:512].rearrange("p (h d) -> p h d", h=4))
                    gates_and_prep()
                    for i in range(NT):
                        tk = slice(i * 128, (i + 1) * 128)
                        rk = [Wr, ("hT", i)]
                        u, va, sq, vc, s1 = u_[i % 2], va_[i % 2], sq_[i % 2], vc_[i % 2], s1_[i % 2]
                        psO = B.psum(2); psU = B.psum(2)
                        for (dst, c0, n) in ((psO[:, 0:512], 528, 512),
                                             (psO[:, 512:1024], 1040, 512), (psU[:, 0:512], 1552, 512)):
                            for kc in range(8):
                                B.mm(dst, hT[:, kc, tk], Wr[:, kc, c0:c0 + n], start=(kc == 0), stop=(kc == 7), rk=rk)
                        B.act(ogt[i % 2][:, :], psO[:, 0:512], AF.Sigmoid)
                        B.stq(og_s[par, i, :, :], ogt[i % 2][:, :], wk=[("og", par, i)])
                        B.act(u[:, :], psO[:, 512:1024], AF.Gelu_apprx_tanh)
                        B.act(va[:, :], psU[:, 0:512], AF.Gelu_apprx_tanh, accum=s1[:, 0:1])
                        B.ts("dve", s1[:, 1:2], s1[:, 0:1], -1.0 / 512, ALU.mult)
                        B.act(sq[:, :], va[:, :], AF.Square, bias=s1[:, 1:2], accum=s1[:, 2:3])
                        B.rstd(s1[:, 3:4], s1[:, 2:3], 512, s1[:, 2:3])
                        B.ts("dve", vc[:, :], va[:, :], s1[:, 1:2], ALU.add, s1[:, 3:4], ALU.mult)
                        for g in range(4):
                            B.mm(psU[:, 512 + g * 128:512 + (g + 1) * 128], wsT[:, g, :], vc[:, g * 128:(g + 1) * 128])
                        for g in range(4):
                            B.stt(sgt[i % 2][:, g * 128:(g + 1) * 128], psU[:, 512 + g * 128:512 + (g + 1) * 128], sgbT[:, g:g + 1],
                                  u[:, g * 128:(g + 1) * 128], ALU.add, ALU.mult)
                        B.stq(sgu_s[par, i, :, :], sgt[i % 2][:, :], wk=[("sgu", par, i)])
                step = [0]
                cur_r2 = [None]

                def scan_step(d, c):
                    k = step[0] % 2
                    step[0] += 1
                    dn, r2 = dn_[k], r2_[k]
                    cur_r2[0] = r2
                    msk = mskU if d == 0 else mskL
                    tk = slice(c * 128, (c + 1) * 128)
                    psA = B.psum(1); psAt = B.psum(1)
                    psAb = psAt[:, 0:512].bitcast(BF16)
                    for h in range(4):
                        B.mm(psA[:, h * 128:(h + 1) * 128], KT[:, h, tk], QT[:, h, tk])
                    for h in range(4):
                        B.tr(psAb[:, h * 128:(h + 1) * 128], KT[:, h, tk], ident_b[:, :])
                    for h in range(4):
                        B.stt(PT[k][:, h, :], psA[:, h * 128:(h + 1) * 128], EA[:, c, d * 4 + h:d * 4 + h + 1], msk[:, :], ALU.mult, ALU.mult)
                    for h in range(4):
                        B.act(Kw[k][:, h, :], psAb[:, h * 128:(h + 1) * 128], AF.Copy, scale=EAW[:, c, d * 4 + h:d * 4 + h + 1])
                    psB = B.psum(2)
                    for h in range(4):
                        B.mm(psB[:, h * 256:h * 256 + 129], PT[k][:, h, :], Vx[:, c, h, :], start=True, stop=False)
                        B.mm(psB[:, h * 256:h * 256 + 129], QT[:, h, tk], Cbf[:, d * 4 + h, :], start=False, stop=True)
                    B.tt("dve", dn[:, :], psB[:, 128::256], EB[:, c, d * 4:d * 4 + 4], ALU.mult)
                    B.act(dn[:, :], dn[:, :], AF.Abs)
                    B.ts("dve", dn[:, :], dn[:, :], 1.0, ALU.max)
                    B.recip(dn[:, :], dn[:, :])
                    B.tt("dve", r2[:, :], dn[:, :], EB[:, c, d * 4:d * 4 + 4], ALU.mult)
                    psD = B.psum(2)
                    for h in range(4):
                        B.mm(psD[:, h * 256:h * 256 + 129], Kw[k][:, h, :], Vx[:, c, h, :])
                    for h in range(4):
                        j = d * 4 + h
                        B.stt(C32[:, j, :], C32[:, j, :], EBT[:, c, j:j + 1], psD[:, h * 256:h * 256 + 129], ALU.mult, ALU.add)
                    B.cp("act", Cbf[:, d * 4:d * 4 + 4, :], C32[:, d * 4:d * 4 + 4, :])
                    return psB

                B.memset("pool", C32[:, :, :], 0.0)
                B.memset("pool", Cbf[:, :, :], 0.0)
                S.mark("L0 bwd")
                for c in [1, 0] + list(range(NT - 1, 1, -1)):
                    psB = scan_step(1, c)
                    y = yb[c % 2]
                    B.tt("dve", y[:, :].rearrange("p (h d) -> p h d", h=4), psB[:, :].rearrange("p (h d) -> p h d", d=256)[:, :, 0:128],
                         cur_r2[0][:, :].unsqueeze(2).to_broadcast([128, 4, 128]), ALU.mult)
                    B.stq(yb_s[par, c, :, :], y[:, :], wk=[("yb", par, c)])

                S.mark("L0 fwd+P4")
                with B.scope() as P4:
                    Wo = P4.sb("Wo", [128, 8, D], BF16)
                    B.ld(Wo[:, :, :], w_out0b.rearrange("(k p) n -> p k n", p=128), rk=[("w_out0b",)])
                    g2 = P4.sb("g2", [128, D], F32)
                    B.ld(g2[:, :], gsc[0, 0, NS, :].partition_broadcast(128), rk=[("gsc", 0, 0)])
                    hg = P4.sb("hg", [128, 512], F32)
                    B.ld(hg[:, :], ab_head_g[0, :].partition_broadcast(128))
                    ybl = [P4.sb("ybl%d" % i, [128, 512], F32) for i in range(2)]
                    ogl = [P4.sb("ogl%d" % i, [128, 512], F32) for i in range(2)]
                    ys_ = [P4.sb("ys", [128, 512], F32) for _i in range(2)]; sq4_ = [P4.sb("sq4", [128, 512], F32) for _i in range(2)]
                    st4_ = [P4.sb("st4", [128, 4], F32) for _i in range(2)]; st4b_ = [P4.sb("st4b", [128, 4], F32) for _i in range(2)]
                    st4c_ = [P4.sb("st4c", [128, 4], F32) for _i in range(2)]
                    cat = [P4.sb("cat%d" % i, [128, D], BF16) for i in range(2)]
                    catT_ = [P4.sb("catT", [128, 8, 128], BF16) for _i in range(2)]
                    for c in range(NT):
                        row = NS if c < 2 else b
                        if c == 2:
                            B.ld(g2[:, :], gsc[0, 0, b, :].partition_broadcast(128), rk=[("gsc", 0, 0)])
                        ct = cat[c % 2]
                        ys, sq, st4, st4b, st4c, catT = ys_[c % 2], sq4_[c % 2], st4_[c % 2], st4b_[c % 2], st4c_[c % 2], catT_[c % 2]
                        B.ld(ybl[c % 2][:, :], yb_s[par, c, :, :], rk=[("yb", par, c)])
                        B.ld(ogl[c % 2][:, :], og_s[par, c, :, :], rk=[("og", par, c)])
                        B.ld(ct[:, 0:512], sgu_s[par, c, :, :], rk=[("sgu", par, c)])
                        psB = scan_step(0, c)
                        B.tt("dve", ys[:, :].rearrange("p (h d) -> p h d", h=4), psB[:, :].rearrange("p (h d) -> p h d", d=256)[:, :, 0:128],
                             cur_r2[0][:, :].unsqueeze(2).to_broadcast([128, 4, 128]), ALU.mult)
                        B.tt("pool", ys[:, :], ys[:, :], ybl[c % 2][:, :], ALU.add)
                        B.tt("pool", ys[:, :], ys[:, :], ogl[c % 2][:, :], ALU.mult)
                        z3 = ys[:, :].rearrange("p (h d) -> p h d", h=4)
                        B.red(st4[:, :], z3)
                        B.ts("dve", st4[:, :], st4[:, :], 1.0 / 128, ALU.mult)
                        B.tt("dve", z3, z3, st4[:, :].unsqueeze(2).to_broadcast([128, 4, 128]), ALU.subtract)
                        B.tt("pool", sq[:, :], ys[:, :], ys[:, :], ALU.mult)
                        B.red(st4b[:, :], sq[:, :].rearrange("p (h d) -> p h d", h=4))
                        B.rstd(st4c[:, :], st4b[:, :], 128, st4b[:, :])
                        B.tt("dve", z3, z3, st4c[:, :].unsqueeze(2).to_broadcast([128, 4, 128]), ALU.mult)
                        B.tt("pool", ct[:, 512:1024], ys[:, :], hg[:, :], ALU.mult)
                        psT = B.psum(1)
                        psTb = psT[:, :].bitcast(BF16)
                        for kc in range(8):
                            B.tr(psTb[:, kc * 128:(kc + 1) * 128], ct[:, kc * 128:(kc + 1) * 128], ident_b[:, :])
                        B.cp("act", catT[:, :, :], psTb[:, 0:1024].rearrange("p (k t) -> p k t", t=128))
                        psY = B.psum(2)
                        for half in range(2):
                            for kc in range(8):
                                B.mm(psY[:, half * 512:(half + 1) * 512], catT[:, kc, :], Wo[:, kc, half * 512:(half + 1) * 512], start=(kc == 0), stop=(kc == 7))
                        xt = xts[c % NX]
                        sap, skey = src_tile(0, b, c)
                        B.ld(xt[:, :], sap, rk=[skey] if skey else [])
                        post_residual(Wks[c % 2], psY, xt, g2, 0, xm_s[b, c * 128:(c + 1) * 128, :], ("xm", b, c), xt)
                        prenorm_to_hT(Wks[c % 2], xt, 0, 1, row, c, None)
            ffn(0, b, list(range(NT)))

        EPS_t = G.sb("EPS_t", [128, 1], F32)
        B.memset("pool", EPS_t[:, :], EPS)
        B.eps_ap = EPS_t[:, 0:1]
        LNS_ap = G.sb("LNS_ap", [128, 1], F32)
        B.memset("pool", LNS_ap[:, :], math.log(128.0 ** -0.5))

        lg = G.sb("lg", [128, 8], F32); e1 = G.sb("e1", [128, 8], F32); pp1 = G.sb("pp1", [128, 1], F32)
        xi = G.sb("xi", [128, 8], F32); zeta = G.sb("zeta", [128, 8], F32); colA = G.sb("colA", [128, 8], F32); g128 = G.sb("g128", [128, 8], F32)
        rtmp = G.sb("rtmp", [128, 8], F32)
        DmT = G.sb("DmT", [128, 8, 128], BF16)
        if 1 in layers:
            LNK = math.log(256.0 ** -0.5)
            B.act(rtmp[:, :], lgd[:, :], AF.Exp, scale=-1.0)
            B.act(lg[:, :], rtmp[:, :], AF.Ln, bias=1.0)
            B.ts("dve", lg[:, :], lg[:, :], -1.0, ALU.mult)
            ps = B.psum()
            B.mm(ps[:, 0:1], triU[:, :], ones_f[:, 0:1])
            B.cp("dve", pp1[:, :], ps[:, 0:1])
            B.ts("dve", e1[:, :], lg[:, :], pp1[:, 0:1], ALU.mult)
            B.act(g128[:, :], lg[:, :], AF.Exp, scale=128.0)
            B.act(xi[:, 0:4], e1[:, 0:4], AF.Exp)
            B.ts("dve", rtmp[:, 0:4], e1[:, 0:4], -1.0, ALU.mult, LNK, ALU.add)
            B.act(colA[:, 0:4], rtmp[:, 0:4], AF.Exp)
            B.stt(rtmp[:, 0:4], lg[:, 0:4], 128.0, e1[:, 0:4], ALU.mult, ALU.subtract)
            B.ts("dve", rtmp[:, 0:4], rtmp[:, 0:4], LNK, ALU.add)
            B.act(zeta[:, 0:4], rtmp[:, 0:4], AF.Exp)
            B.stt(rtmp[:, 4:8], lg[:, 4:8], 129.0, e1[:, 4:8], ALU.mult, ALU.subtract)
            B.act(xi[:, 4:8], rtmp[:, 4:8], AF.Exp)
            B.ts("dve", rtmp[:, 4:8], rtmp[:, 4:8], -1.0, ALU.mult, LNK, ALU.add)
            B.act(colA[:, 4:8], rtmp[:, 4:8], AF.Exp)
            B.stt(rtmp[:, 4:8], lg[:, 4:8], -1.0, e1[:, 4:8], ALU.mult, ALU.add)
            B.ts("dve", rtmp[:, 4:8], rtmp[:, 4:8], LNK, ALU.add)
            B.act(zeta[:, 4:8], rtmp[:, 4:8], AF.Exp)
            for j in range(8):
                B.ts("dve", DmT[:, j, :], (triU if j < 4 else triL)[:, :], colA[:, j:j + 1], ALU.mult)

        def layer1(b):
            par = b % 2
            with B.scope() as M:
                Wks, xts = LW["Wks"], LW["xts"]
                S.mark("L1 P1")
                for i in range(NT):
                    xt = xts[i % len(xts)]
                    sap, skey = src_tile(1, b, i)
                    B.ld(xt[:, :], sap, rk=[skey])
                    prenorm_to_hT(Wks[i % 2], xt, 1, 0, NS if i < 2 else b, i, None)
                allh = [("hT", i) for i in range(NT)]
                for h in range(4):
                    with B.scope() as H:
                        Wh = H.sb("Wh", [128, 8, 1536], BF16)
                        for (d0, s0, n) in ((0, h * 256, 256), (256, 1024 + h * 512, 512), (768, 3072 + h * 256, 256), (1024, 4096 + h * 512, 512)):
                            B.ld(Wh[:, :, d0:d0 + n], w_in1b[:, s0:s0 + n].rearrange("(k p) n -> p k n", p=128), rk=[("w_in1b",)])
                        KTh = H.sb("KTh", [128, 2, T], BF16); QTh = H.sb("QTh", [128, 2, T], BF16)
                        Vh = H.sb("Vh", [128, NT, 512], BF16)
                        R32_ = [H.sb("R32", [128, 2, 512], F32) for _i in range(2)]; Rbf_ = [H.sb("Rbf", [128, 2, 512], BF16) for _i in range(2)]
                        PT = [H.sb("PT%d" % i, [128, 128], BF16) for i in range(2)]
                        Kz = [H.sb("Kz%d" % i, [128, 256], BF16) for i in range(2)]
                        hgh = H.sb("hgh", [128, 512], F32)
                        B.ld(hgh[:, :], ret_head_g[0, h * 512:(h + 1) * 512].partition_broadcast(128))
                        S.mark("L1 h%d proj" % h)
                        for fc in range(4):
                            col0 = fc * 128 if fc < 2 else 768 + (fc - 2) * 128
                            dst = KTh if fc < 2 else QTh
                            blocks = ([(0, 256)] if fc < 2 else []) + [(256 + 512 * k, 512) for k in range(4)]
                            for bi in range(0, len(blocks), 2):
                                ps = B.psum(2)
                                for q, (t0, n) in enumerate(blocks[bi:bi + 2]):
                                    for kc in range(8):
                                        B.mm(ps[:, q * 512:q * 512 + n], Wh[:, kc, col0:col0 + 128], hT[:, kc, t0:t0 + n],
                                             start=(kc == 0), stop=(kc == 7), rk=[Wh] + allh)
                                    B.cp("act" if q == 0 else "dve", dst[:, fc % 2, t0:t0 + n], ps[:, q * 512:q * 512 + n])
                        for i in range(NT):
                            ps = B.psum()
                            for kc in range(8):
                                B.mm(ps[:, 0:512], hT[:, kc, i * 128:(i + 1) * 128], Wh[:, kc, 256:768], start=(kc == 0), stop=(kc == 7), rk=[Wh, ("hT", i)])
                            B.cp("act" if i % 2 == 0 else "dve", Vh[:, i, :], ps[:, 0:512])
                        step = [0]

                        def scan_step(d, c, with_q):
                            k = step[0] % 2
                            step[0] += 1
                            j = d * 4 + h
                            R32, Rbf = R32_[d], Rbf_[d]
                            tk = slice(c * 128, (c + 1) * 128)
                            psA = B.psum(1); psAt = B.psum(1)
                            psAb = psAt[:, 0:512].bitcast(BF16)
                            if with_q:
                                for kc in range(2):
                                    B.mm(psA[:, 0:128], KTh[:, kc, tk], QTh[:, kc, tk], start=(kc == 0), stop=(kc == 1))
                            for kc in range(2):
                                B.tr(psAb[:, kc * 128:(kc + 1) * 128], KTh[:, kc, tk], ident_b[:, :])
                            psB = None
                            if with_q:
                                B.tt("dve", PT[k][:, :], psA[:, 0:128], DmT[:, j, :], ALU.mult)
                            B.act(Kz[k][:, :], psAb[:, 0:256], AF.Copy, scale=zeta[:, j:j + 1])
                            if with_q:
                                psB = B.psum(1)
                                B.mm(psB[:, 0:512], PT[k][:, :], Vh[:, c, :], start=True, stop=False)
                                B.mm(psB[:, 0:512], QTh[:, 0, tk], Rbf[:, 0, :], start=False, stop=False)
                                B.mm(psB[:, 0:512], QTh[:, 1, tk], Rbf[:, 1, :], start=False, stop=True)
                            psD = B.psum(2)
                            for kc in range(2):
                                B.mm(psD[:, kc * 512:(kc + 1) * 512], Kz[k][:, kc * 128:(kc + 1) * 128], Vh[:, c, :])
                            B.stt(R32[:, :, :], R32[:, :, :], g128[:, j:j + 1], psD[:, :].rearrange("p (k v) -> p k v", k=2), ALU.mult, ALU.add)
                            B.cp("act", Rbf[:, :, :], R32[:, :, :])
                            return psB

                        S.mark("L1 h%d scan" % h)
                        with B.scope() as PB:
                            yb = [PB.sb("yb%d" % i, [128, 512], F32) for i in range(4)]
                            for d in range(2):
                                B.memset("pool", R32_[d][:, :, :], 0.0); B.memset("pool", Rbf_[d][:, :, :], 0.0)
                            bw = [1, 0] + list(range(NT - 1, 1, -1))
                            for kk in range(NT):
                                for d, c in ((1, bw[kk]), (0, kk)):
                                    psB = scan_step(d, c, c >= 2)
                                    if c >= 2:
                                        y = yb[(2 * kk + d) % 4]
                                        B.act(y[:, :], psB[:, 0:512], AF.Copy, scale=xi[:, d * 4 + h:d * 4 + h + 1])
                                        if d == 1:
                                            B.stq(yb_s[par, c, :, :], y[:, :], wk=[("yb", par, c)])
                                        else:
                                            B.stq(yf_s[par, c, :, :], y[:, :], wk=[("yf", par, c)])
                        S.mark("L1 h%d merge" % h)
                        with B.scope() as PF:
                            ND = 3
                            ybl = [PF.sb("ybl%d" % i, [128, 512], F32) for i in range(ND)]
                            yfl = [PF.sb("yfl%d" % i, [128, 512], F32) for i in range(ND)]
                            ys_ = [PF.sb("ys", [128, 512], F32) for _i in range(ND)]; sqj_ = [PF.sb("sqj", [128, 512], BF16) for _i in range(ND)]
                            sg_ = [PF.sb("sg", [128, 512], F32) for _i in range(ND)]
                            zt = [PF.sb("zt%d" % i, [128, 512], BF16) for i in range(ND)]
                            s1_ = [PF.sb("s1r", [128, 4], F32) for _i in range(ND)]
                            for c in range(2, NT):
                                tk = slice(c * 128, (c + 1) * 128)
                                ys, sqj, sg, s1 = ys_[c % ND], sqj_[c % ND], sg_[c % ND], s1_[c % ND]
                                B.ld(ybl[c % ND][:, :], yb_s[par, c, :, :], rk=[("yb", par, c)])
                                B.ld(yfl[c % ND][:, :], yf_s[par, c, :, :], rk=[("yf", par, c)])
                                B.tt("dve", ys[:, :], yfl[c % ND][:, :], ybl[c % ND][:, :], ALU.add)
                                psG = B.psum()
                                for kc in range(8):
                                    B.mm(psG[:, 0:512], hT[:, kc, tk], Wh[:, kc, 1024:1536], start=(kc == 0), stop=(kc == 7), rk=[Wh, ("hT", c)])
                                B.act(sg[:, :], psG[:, 0:512], AF.Silu)
                                B.tt("pool", sg[:, :], sg[:, :], hgh[:, :], ALU.mult)
                                B.red(s1[:, 0:1], ys[:, :])
                                B.ts("dve", s1[:, 1:2], s1[:, 0:1], -1.0 / 512, ALU.mult)
                                B.act(sqj[:, :], ys[:, :], AF.Square, bias=s1[:, 1:2], accum=s1[:, 2:3])
                                B.rstd(s1[:, 3:4], s1[:, 2:3], 512, s1[:, 2:3])
                                B.ts("dve", ys[:, :], ys[:, :], s1[:, 1:2], ALU.add, s1[:, 3:4], ALU.mult)
                                B.tt("dve", zt[c % ND][:, :], ys[:, :], sg[:, :], ALU.mult)
                                B.stq(z_s[par, (c - 2) * 128:(c - 1) * 128, h * 512:(h + 1) * 512], zt[c % ND][:, :], wk=[("z", par, c)])
                S.mark("L1 P4")
                with B.scope() as P4:
                    Wo = P4.sb("Wo1", [128, 16, D], BF16)
                    B.ld(Wo[:, :, :], w_out1b.rearrange("(k p) n -> p k n", p=128), rk=[("w_out1b",)])
                    g2 = P4.sb("g2", [128, D], F32)
                    B.ld(g2[:, :], gsc[1, 0, b, :].partition_broadcast(128), rk=[("gsc", 1, 0)])
                    zl = [P4.sb("zl%d" % i, [128, 2048], BF16) for i in range(2)]
                    zT_ = [P4.sb("zT", [128, 16, 128], BF16) for _i in range(2)]
                    for c in range(2, NT):
                        z = zl[c % 2]
                        zT = zT_[c % 2]
                        B.ld(z[:, :], z_s[par, (c - 2) * 128:(c - 1) * 128, :], rk=[("z", par, c)])
                        psT = B.psum(2)
                        psTb = psT[:, :].bitcast(BF16)
                        for kc in range(16):
                            B.tr(psTb[:, kc * 128:(kc + 1) * 128], z[:, kc * 128:(kc + 1) * 128], ident_b[:, :])
                        B.cp("act", zT[:, :, :], psTb[:, :].rearrange("p (k t) -> p k t", t=128))
                        psY = B.psum(2)
                        for half in range(2):
                            for kc in range(16):
                                B.mm(psY[:, half * 512:(half + 1) * 512], zT[:, kc, :], Wo[:, kc, half * 512:(half + 1) * 512], start=(kc == 0), stop=(kc == 15))
                        xt = xts[c % len(xts)]
                        sap, skey = src_tile(1, b, c)
                        B.ld(xt[:, :], sap, rk=[skey])
                        post_residual(Wks[c % 2], psY, xt, g2, 1, xm_s[b, c * 128:(c + 1) * 128, :], ("xm", b, c), xt)
                        prenorm_to_hT(Wks[c % 2], xt, 1, 1, b, c, None)
            ffn(1, b, list(range(2, NT)))

        LW = {}

        def layer_ws(sc, nx):
            Wks = []
            for _i in range(2):
                _w = {"junk": sc.sb("junk", [128, D], BF16), "ss": sc.sb("ss", [128, 2], F32), "tmp1": sc.sb("tmp1", [128, 2], F32),
                      "rs": sc.sb("rs", [128, 2], F32), "xn": sc.sb("xn", [128, D], F32)}
                _w["t1"] = _w["xn"]
                Wks.append(_w)
            LW["Wks"] = Wks
            LW["xts"] = [sc.sb("xt%d" % i, [128, D], F32) for i in range(nx)]

        if 0 in layers:
            with B.scope() as LG:
                layer_ws(LG, 3)
                for b in range(NS):
                    layer0(b)
                    if b == 0:
                        S.mark("deferred casts")
                        deferred_casts(dcf, dcb, 1024, 8)
        if 1 in layers:
            with B.scope() as LG:
                layer_ws(LG, 3)
                for b in range(NS):
                    layer1(b)
        S.finish()
    return nc, S


_CACHE = {}


def make_in_map(inp, b0, NS):
    f = lambda a: np.ascontiguousarray(np.asarray(a, dtype=np.float32))
    return {
        "x": f(inp["x"][b0:b0 + NS]), "c": f(inp["c"][b0:b0 + NS]), "ctx": f(inp["ctx"][b0:b0 + NS]),
        "c_ctx": f(inp["c_ctx"]).reshape(1, D),
        "ada_w": f(inp["ada_w"]), "ada_b": f(inp["ada_b"]),
        "pre_g": f(inp["pre_g"]).reshape(4, D), "post_g": f(inp["post_g"]).reshape(4, D),
        "ffn_up": f(inp["ffn_up"]), "ffn_conv": f(inp["ffn_conv"]).reshape(2 * 9 * NJ, 128), "ffn_down": f(inp["ffn_down"]),
        "ab_w_in": f(inp["ab_w_in"]).reshape(D, AB_PROJ), "ab_qk_conv": f(inp["ab_qk_conv"]).reshape(24, 128),
        "ab_gate_b": f(inp["ab_gate_b"]).reshape(1, 16), "ab_sgu_w": f(inp["ab_sgu_w"]).reshape(4, 128, 128),
        "ab_sgu_b": f(inp["ab_sgu_b"]).reshape(4, 128), "ab_head_g": f(inp["ab_head_g"]).reshape(1, 512),
        "ab_w_out": f(inp["ab_w_out"]).reshape(D, D), "ret_w_in": f(inp["ret_w_in"]).reshape(D, C_PROJ),
        "ret_decay": f(inp["ret_decay"]).reshape(1, 8), "ret_head_g": f(inp["ret_head_g"]).reshape(1, 2048),
        "ret_w_out": f(inp["ret_w_out"]).reshape(2048, D),
    }


def kernel(**inputs):
    NS = inputs["x"].shape[0] // N_CORES
    if "nc" not in _CACHE:
        _CACHE["nc"] = build_program(NS)[0]
    nc = _CACHE["nc"]
    in_maps = [make_in_map(inputs, i * NS, NS) for i in range(N_CORES)]
    res = run_bass_kernel_spmd(nc, in_maps, core_ids=list(range(N_CORES)))
    return np.concatenate([np.asarray(r["out"], dtype=np.float32) for r in res.results], axis=0)
```

```python
from contextlib import ExitStack
import math
import numpy as np
import concourse.bass as bass
import concourse.mybir as mybir
from concourse.bass_utils import run_bass_kernel_spmd

F32 = mybir.dt.float32
BF16 = mybir.dt.bfloat16
AF = mybir.ActivationFunctionType
ALU = mybir.AluOpType
AX = mybir.AxisListType

D = 1024
SEQ = 2048
CTX = 256
T = SEQ + CTX
NT = T // 128
DFF = 2816
NJ = DFF // 128
EPS = 1e-6
AB_PROJ = 3088
C_PROJ = 6144
N_CORES = 8
REORDER = True
FOLD_WAITS = True
POOL_TAP = False


def _esize(dt):
    return 4 if dt == F32 else 2


def _free_elems(ap):
    n = 1
    for d in ap.shape[1:]:
        n *= d
    return n


class Sched:
    LAT = 120.0

    def __init__(self, nc, stack, n_ld=48, n_st=4):
        self.nc = nc
        self.eng = {"pe": nc.tensor, "act": nc.scalar, "dve": nc.vector, "pool": nc.gpsimd, "sp": nc.sync}
        self.sem = {}
        for e in ("pe", "act", "dve", "pool"):
            self.sem[e] = stack.enter_context(nc.semaphore("s_" + e))
        self.dq = {"sp": [], "pool": []}
        for i in range(n_ld):
            k = "ld%d" % i
            self.sem[k] = stack.enter_context(nc.semaphore(k))
            self.dq["sp"].append(k)
        for i in range(n_st):
            k = "st%d" % i
            self.sem[k] = stack.enter_context(nc.semaphore(k))
            self.dq["pool"].append(k)
        self.nodes = []
        self.last_w = {}
        self.readers = {}
        self.rel_node = None
        self.nwaits = 0
        self.ninst = 0
        self.cnt = {e: 0 for e in self.eng}

    @staticmethod
    def keys(x):
        if isinstance(x, (str, tuple)):
            return [x]
        if not hasattr(x, "tensor"):
            return [x.name]
        name = x.tensor.name
        if name != "psall":
            return [name]
        es = _esize(x.dtype)
        off = (x.offset * es) % 16384
        ext = es
        for (stp, cnt) in x.ap[1:]:
            ext += (cnt - 1) * abs(stp) * es
        return [("ps", k) for k in range(off // 2048, (off + ext - 1) // 2048 + 1)]

    def _flat(self, lst):
        out = []
        for x in lst:
            if x is None:
                continue
            for k in self.keys(x):
                if k not in out:
                    out.append(k)
        return out

    def issue(self, e, fn, reads=(), writes=(), dma=False, cost=100.0, lat=0.0, fold=False):
        reads = self._flat(reads)
        writes = self._flat(writes)
        nid = len(self.nodes)
        preds = {}
        for r in reads:
            w = self.last_w.get(r)
            if w is not None:
                preds[w] = True
        for r in reads:
            if isinstance(r, tuple) and r[0] == "ps" and r not in writes:
                writes.append(r)
        for w in writes:
            p = self.last_w.get(w)
            if p is not None and p not in preds:
                preds[p] = False
            for rd in self.readers.get(w, ()):
                if rd not in preds:
                    preds[rd] = False
        preds.pop(nid, None)
        self.nodes.append([e, fn, dma, float(cost), float(lat), list(preds.items()), bool(fold) and FOLD_WAITS])
        for w in writes:
            self.last_w[w] = nid
            self.readers[w] = []
        for r in reads:
            if r not in writes:
                self.readers.setdefault(r, []).append(nid)
        return nid

    def mark(self, name):
        if not hasattr(self, "marks"):
            self.marks = []
        self.marks.append((name, len(self.nodes)))

    def release(self, names):
        preds = {}
        if self.rel_node is not None:
            preds[self.rel_node] = False
        for n in names:
            for k in self.keys(n):
                w = self.last_w.pop(k, None)
                if w is not None:
                    preds[w] = False
                for rd in self.readers.pop(k, ()):
                    preds[rd] = False
        nid = len(self.nodes)
        self.nodes.append([None, None, False, 0.0, 0.0, list(preds.items()), False])
        self.rel_node = nid

    def adopt(self, names):
        if self.rel_node is None:
            return
        for n in names:
            for k in self.keys(n):
                self.readers[k] = [self.rel_node]

    def schedule(self):
        import heapq
        N = len(self.nodes)
        npred = [0] * N
        succ = [[] for _ in range(N)]
        for i, nd in enumerate(self.nodes):
            npred[i] = len(nd[5])
            for (p, _) in nd[5]:
                succ[p].append(i)
        ready_t = [0.0] * N
        start = [0.0] * N
        fin = [0.0] * N
        efree = {e: 0.0 for e in self.eng}
        rq = {e: [] for e in self.eng}
        pend = []
        done = 0

        def make_ready(i, t):
            nd = self.nodes[i]
            if nd[0] is None:
                finish(i, t, t)
            else:
                heapq.heappush(rq[nd[0]], i)
                ready_t[i] = t

        def finish(i, ts, tf):
            nonlocal done
            start[i] = ts
            fin[i] = tf
            done += 1
            for sc in succ[i]:
                npred[sc] -= 1
                if ready_t[sc] < tf + self.LAT:
                    ready_t[sc] = tf + self.LAT
                if npred[sc] == 0:
                    heapq.heappush(pend, (ready_t[sc], sc))

        for i in range(N):
            if npred[i] == 0:
                heapq.heappush(pend, (0.0, i))
        now = 0.0
        while done < N:
            while pend and pend[0][0] <= now:
                t, i = heapq.heappop(pend)
                make_ready(i, t)
            if done >= N:
                break
            progressed = False
            for e in self.eng:
                if efree[e] <= now and rq[e]:
                    i = heapq.heappop(rq[e])
                    nd = self.nodes[i]
                    ts = now
                    efree[e] = ts + nd[3]
                    finish(i, ts, ts + nd[3] + nd[4])
                    progressed = True
            if progressed:
                continue
            cand = [efree[e] for e in self.eng if rq[e] and efree[e] > now]
            if pend:
                cand.append(pend[0][0])
            if not cand:
                raise RuntimeError("scheduler stuck (cyclic dependencies?)")
            now = max(now, min(cand))
        self.sim_ns = max(fin) if N else 0.0
        self.sim_start, self.sim_fin = start, fin
        if not REORDER:
            return list(range(N))
        return sorted(range(N), key=lambda i: (start[i], i))

    def emit(self):
        order = self.schedule()
        known = {e: {} for e in self.eng}
        ev = [None] * len(self.nodes)
        dval = {k: 0 for q in self.dq.values() for k in q}
        dclk = {k: {} for q in self.dq.values() for k in q}
        dnext = {"sp": 0, "pool": 0}

        self.trace = {e: [] for e in self.eng}
        self.nfold = 0

        pend_w = []

        def wait(e, s, v, clk):
            kn = known[e]
            if s is not None and kn.get(s, 0) < v:
                pend_w.append((s, v))
                self.nwaits += 1
                kn[s] = v
            for k2, v2 in clk.items():
                if kn.get(k2, 0) < v2:
                    kn[k2] = v2

        def flush(e, keep_last):
            last = None
            if keep_last and pend_w:
                last = pend_w.pop()
            for (s_, v_) in pend_w:
                self.eng[e].wait_ge(self.sem[s_], v_)
                self.trace[e].append(("w", s_, v_))
            del pend_w[:]
            return last

        for i in order:
            e, fn, dma, cost, lat, preds, fold = self.nodes[i]
            if e is None:
                clk = {}
                for (p, _) in preds:
                    s, v, c2 = ev[p]
                    if s is not None and clk.get(s, 0) < v:
                        clk[s] = v
                    if s is None:
                        for k2, v2 in c2.items():
                            if clk.get(k2, 0) < v2:
                                clk[k2] = v2
                ev[i] = (None, 0, clk)
                continue
            own = None if dma else e
            for (p, raw) in preds:
                s, v, clk = ev[p]
                if s is None:
                    for k2, v2 in clk.items():
                        if k2 == own and e == "pe":
                            continue
                        if known[e].get(k2, 0) < v2:
                            pend_w.append((k2, v2))
                            self.nwaits += 1
                            known[e][k2] = v2
                    continue
                if s == own and (e == "pe" or not raw):
                    continue
                wait(e, s, v, clk)
            if dma:
                q = self.dq[e]
                k = q[dnext[e] % len(q)]
                dnext[e] += 1
                if dval[k] > 0:
                    wait(e, k, dval[k], dclk[k])
                flush(e, False)
                ins = fn()
                dval[k] += 16
                ins.then_inc(self.sem[k], 16)
                self.trace[e].append(("i", k, 16))
                clk = dict(known[e])
                dclk[k] = clk
                ev[i] = (k, dval[k], clk)
            else:
                last = flush(e, fold)
                ins = fn()
                if last is not None:
                    ins._wait_ge(self.sem[last[0]], last[1])
                    self.trace[e].append(("w", last[0], last[1]))
                    self.nfold += 1
                self.cnt[e] += 1
                ins.then_inc(self.sem[e], 1)
                self.trace[e].append(("i", e, 1))
                ev[i] = (e, self.cnt[e], dict(known[e]))
            self.ninst += 1
        for k, v in dval.items():
            if v > 0 and known["sp"].get(k, 0) < v:
                self.eng["sp"].wait_ge(self.sem[k], v)
        for e in ("pe", "act", "dve", "pool"):
            if self.cnt[e] > 0:
                self.eng["sp"].wait_ge(self.sem[e], self.cnt[e])

    def finish(self):
        self.emit()


class PSView:
    def __init__(self, arena, bank, n):
        self.arena = arena
        self.c0 = bank * 512
        self.n = n

    def __getitem__(self, idx):
        rows, cols = idx
        a = 0 if cols.start is None else cols.start
        b = self.n * 512 if cols.stop is None else cols.stop
        if cols.step is None:
            return self.arena[rows, self.c0 + a:self.c0 + b]
        return self.arena[rows, self.c0 + a:self.c0 + b:cols.step]


class Bld:
    def __init__(self, nc, st):
        self.nc = nc
        self.S = Sched(nc, st)
        self.arena = st.enter_context(nc.psum_tensor("psall", [128, 4096], F32))
        self.ps_i = 0
        self.uid = 0
        probe = nc.alloc_sbuf_tensor("sb_probe", [128, 8], F32)
        self.sb_top = (nc.lookup_mloc(probe).addr + 32 + 31) // 32 * 32
        self.sb_limit = nc.SBUF_PARTITION_SIZE_BYTES
        self.sb_peak = self.sb_top

    def psum(self, n=1):
        if n == 2 and self.ps_i % 2 == 1:
            self.ps_i += 1
        v = PSView(self.arena, self.ps_i % 8, n)
        self.ps_i += n
        return v

    class Scope:
        def __init__(self, b):
            self.b = b
            self.st = ExitStack()
            self.names = []

        def __enter__(self):
            self.top0 = self.b.sb_top
            return self

        def sb(self, name, shape, dt):
            self.b.uid += 1
            nbytes = _esize(dt)
            for d in shape[1:]:
                nbytes *= d
            nbytes = (nbytes + 31) // 32 * 32
            off = self.b.sb_top
            assert off + nbytes <= self.b.sb_limit, "SBUF overflow: %s needs %d at %d" % (name, nbytes, off)
            t = self.b.nc.alloc_sbuf_tensor_at("%s_%d" % (name, self.b.uid), list(shape), dt, offset=off)
            self.b.sb_top = off + nbytes
            self.b.sb_peak = max(self.b.sb_peak, self.b.sb_top)
            self.names.append(t)
            self.b.S.adopt([t])
            return t

        def __exit__(self, *a):
            self.b.S.release(self.names)
            self.b.sb_top = self.top0
            return False

    def scope(self):
        return Bld.Scope(self)

    def _k(self, aps, override):
        return list(override) if override is not None else [a for a in aps if a is not None and not isinstance(a, (int, float))]

    def mm(self, out, lhsT, rhs, start=True, stop=True, rk=None, wk=None):
        nc = self.nc
        n = _free_elems(rhs)
        c = (max(n, 64) / 2.4 * (4.0 if rhs.dtype == F32 else 1.0) + 8) * 1.2
        return self.S.issue("pe", lambda: nc.tensor.matmul(out, lhsT=lhsT, rhs=rhs, start=start, stop=stop),
                            reads=self._k([lhsT, rhs], rk), writes=self._k([out], wk), cost=c, lat=60)

    def tr(self, out, in_, ident, rk=None, wk=None):
        nc = self.nc
        c = 128 / 2.4 * (2.0 if in_.dtype == F32 else 1.0) + 8
        return self.S.issue("pe", lambda: nc.tensor.transpose(out, in_, ident),
                            reads=self._k([in_, ident], rk), writes=self._k([out], wk), cost=c, lat=60)

    def act(self, out, in_, func, bias=0.0, scale=1.0, accum=None, rk=None, wk=None):
        nc = self.nc
        kw = {}
        if accum is not None:
            kw["accum_out"] = accum

        def f():
            return nc.scalar.activation(out=out, in_=in_, func=func, bias=bias, scale=scale, **kw)
        c = (200 + _free_elems(in_) / 1.4 + (100 if accum is not None else 0)) * 1.2
        return self.S.issue("act", f, reads=self._k([in_, bias, scale], rk), writes=self._k([out, accum], wk), cost=c, lat=60, fold=(accum is None))

    def _vc(self, e, ap):
        n = _free_elems(ap)
        return (90 + n / 0.96) * 1.14 if e == "dve" else (150 + n * 1.65)

    def tt(self, e, out, a, b, op, rk=None, wk=None):
        eng = self.S.eng[e]
        return self.S.issue(e, lambda: eng.tensor_tensor(out=out, in0=a, in1=b, op=op),
                            reads=self._k([a, b], rk), writes=self._k([out], wk), cost=self._vc(e, out), lat=60, fold=True)

    def ts(self, e, out, a, s1, op0, s2=None, op1=None, rk=None, wk=None):
        eng = self.S.eng[e]

        def f():
            if op1 is None:
                return eng.tensor_scalar(out=out, in0=a, scalar1=s1, scalar2=None, op0=op0)
            return eng.tensor_scalar(out=out, in0=a, scalar1=s1, scalar2=s2, op0=op0, op1=op1)
        return self.S.issue(e, f, reads=self._k([a, s1, s2], rk), writes=self._k([out], wk), cost=self._vc(e, out), lat=60, fold=True)

    def stt(self, out, a, s, b, op0, op1, rk=None, wk=None):
        nc = self.nc
        return self.S.issue("dve", lambda: nc.vector.scalar_tensor_tensor(out=out, in0=a, scalar=s, in1=b, op0=op0, op1=op1),
                            reads=self._k([a, s, b], rk), writes=self._k([out], wk), cost=self._vc("dve", out), lat=60, fold=True)

    def cp(self, e, out, in_, rk=None, wk=None):
        nc = self.nc
        if e == "act":
            f = lambda: nc.scalar.copy(out=out, in_=in_)
            c = 200 + _free_elems(out) / 1.4
        else:
            eng = self.S.eng[e]
            f = lambda: eng.tensor_copy(out=out, in_=in_)
            c = self._vc(e, out)
        return self.S.issue(e, f, reads=self._k([in_], rk), writes=self._k([out], wk), cost=c, lat=60, fold=True)

    def red(self, out, in_, op=ALU.add, rk=None, wk=None):
        nc = self.nc
        return self.S.issue("dve", lambda: nc.vector.tensor_reduce(out=out, in_=in_, axis=AX.X, op=op),
                            reads=self._k([in_], rk), writes=self._k([out], wk), cost=self._vc("dve", in_), lat=60, fold=True)

    def recip(self, out, in_):
        nc = self.nc
        return self.S.issue("dve", lambda: nc.vector.reciprocal(out=out, in_=in_), reads=[in_], writes=[out], cost=self._vc("dve", out), lat=60, fold=True)

    def memset(self, e, ap, val):
        eng = self.S.eng[e]
        return self.S.issue(e, lambda: eng.memset(ap, val), writes=[ap], cost=self._vc(e, ap), lat=60, fold=True)

    def ld(self, out, in_, rk=None, wk=None):
        nc = self.nc
        nbytes = out.shape[0] * _free_elems(out) * _esize(out.dtype)
        return self.S.issue("sp", lambda: nc.sync.dma_start(out=out, in_=in_), reads=self._k([], rk), writes=self._k([out], wk), dma=True,
                            cost=70, lat=2200 + nbytes / 160.0)

    def stq(self, out, in_, rk=None, wk=None):
        nc = self.nc
        nbytes = in_.shape[0] * _free_elems(in_) * _esize(in_.dtype)
        return self.S.issue("sp", lambda: nc.sync.dma_start(out=out, in_=in_), reads=self._k([in_], rk), writes=self._k([], wk), dma=True,
                            cost=70, lat=2200 + nbytes / 160.0)

    def rstd(self, out, ss, n, tmp):
        self.act(tmp, ss, AF.Sqrt, bias=self.eps_ap, scale=1.0 / n)
        self.recip(out, tmp)


def build_program(NS, dbg=False, layers=(0, 1)):
    nc = bass.Bass("TRN2", target_bir_lowering=False)
    R = NS + 1

    def din(name, shape):
        return nc.dram_tensor(name, list(shape), F32, kind="ExternalInput").ap()

    x_in = din("x", [NS, SEQ, D]); c_in = din("c", [NS, D]); ctx_in = din("ctx", [NS, CTX, D]); cctx_in = din("c_ctx", [1, D])
    ada_w = din("ada_w", [2, D, 6 * D]); ada_b = din("ada_b", [2, 6 * D]); pre_g = din("pre_g", [4, D]); post_g = din("post_g", [4, D])
    ffn_up = din("ffn_up", [2, D, 2 * DFF]); ffn_conv = din("ffn_conv", [2 * 9 * NJ, 128]); ffn_down = din("ffn_down", [2, DFF, D])
    ab_w_in = din("ab_w_in", [D, AB_PROJ]); ab_qk_conv = din("ab_qk_conv", [24, 128]); ab_gate_b = din("ab_gate_b", [1, 16])
    ab_sgu_w = din("ab_sgu_w", [4, 128, 128]); ab_sgu_b = din("ab_sgu_b", [4, 128]); ab_head_g = din("ab_head_g", [1, 512])
    ab_w_out = din("ab_w_out", [D, D]); ret_w_in = din("ret_w_in", [D, C_PROJ]); ret_decay = din("ret_decay", [1, 8])
    ret_head_g = din("ret_head_g", [1, 2048]); ret_w_out = din("ret_w_out", [2048, D])
    out_t = nc.dram_tensor("out", [NS, SEQ, D], F32, kind="ExternalOutput").ap()

    skind = "ExternalOutput" if dbg else "Internal"

    def dscr(name, shape, dt, k=None):
        return nc.dram_tensor(name, list(shape), dt, kind=k or "Internal").ap()

    w_in0b = dscr("w_in0b", [D, AB_PROJ], BF16); w_out0b = dscr("w_out0b", [D, D], BF16)
    w_in1b = dscr("w_in1b", [D, C_PROJ], BF16); w_out1b = dscr("w_out1b", [2048, D], BF16)
    up_r = dscr("up_r", [2, NJ, 128, 8, 2, 128], BF16); down_b = dscr("down_b", [2, DFF, D], BF16)
    gsc = dscr("gsc", [2, 2, R, D], F32)
    xm_s = dscr("xm_s", [NS, T, D], F32, skind)
    xo_s = dscr("xo_s", [NS, T, D], F32, skind)
    yb_s = dscr("yb_s", [2, NT, 128, 512], F32)
    og_s = dscr("og_s", [2, NT, 128, 512], F32)
    sgu_s = dscr("sgu_s", [2, NT, 128, 512], BF16)
    act_s = dscr("act_s", [2, NJ, 128, T], BF16)
    yf_s = dscr("yf_s", [2, NT, 128, 512], F32)
    z_s = dscr("z_s", [2, SEQ, 2048], BF16)

    with ExitStack() as st:
        B = Bld(nc, st)
        S = B.S
        G = B.scope()
        st.enter_context(G)
        ident_b = G.sb("ident_b", [128, 128], BF16); ident_f = G.sb("ident_f", [128, 128], F32)
        triU = G.sb("triU", [128, 128], F32); triL = G.sb("triL", [128, 128], F32); ones_f = G.sb("ones_f", [128, 128], F32)
        mskU = G.sb("mskU", [128, 128], BF16); mskL = G.sb("mskL", [128, 128], BF16)
        B.memset("pool", ones_f[:, :], 1.0)
        B.memset("pool", triU[:, :], 1.0); B.memset("pool", triL[:, :], 1.0); B.memset("pool", ident_f[:, :], 1.0)
        S.issue("pool", lambda: nc.gpsimd.affine_select(out=triU[:, :], in_=triU[:, :], pattern=[[1, 128]], compare_op=ALU.is_ge, fill=0.0, base=0, channel_multiplier=-1), reads=[triU], writes=[triU])
        S.issue("pool", lambda: nc.gpsimd.affine_select(out=triL[:, :], in_=triL[:, :], pattern=[[-1, 128]], compare_op=ALU.is_ge, fill=0.0, base=0, channel_multiplier=1), reads=[triL], writes=[triL])
        B.tt("pool", ident_f[:, :], triU[:, :], triL[:, :], ALU.mult)
        B.cp("pool", ident_b[:, :], ident_f[:, :]); B.cp("pool", mskU[:, :], triU[:, :]); B.cp("pool", mskL[:, :], triL[:, :])

        hT = G.sb("hT", [128, 8, T], BF16)
        sT = G.sb("sT", [128, 8, R], F32)
        modsT = G.sb("modsT", [128, 2, 48, R], F32)
        pre_gT = G.sb("pre_gT", [128, 32], F32)
        qkcT = G.sb("qkcT", [128, 24], F32)
        fcvT = G.sb("fcvT", [128, 2 * 9 * NJ], F32)
        sgbT = G.sb("sgbT", [128, 4], F32)
        hgT = G.sb("hgT", [128, 4], F32)
        g1c = G.sb("g1c", [128, 2, 2, R, 8], F32)
        gateb = G.sb("gateb", [128, 16], F32)
        lgd = G.sb("lgd", [128, 8], F32)
        wsT = G.sb("wsT", [128, 4, 128], BF16)

        def rowsT(dst, src_ap, nrows, P):
            with B.scope() as sc:
                stg = sc.sb("rt_stg", [128, 128], F32)
                for r0 in range(0, nrows, 128):
                    n = min(128, nrows - r0)
                    B.ld(stg[0:n, :], src_ap[r0:r0 + n, :])
                    ps = B.psum()
                    B.tr(ps[:, 0:n], stg[0:n, :], ident_f[0:n, 0:n])
                    B.cp("dve", dst[:, r0:r0 + n], ps[:, 0:n])

        cast_cnt = [0]
        cast_gate = [[]]

        def cast_mat(src, dst, key, stf, stb, W, rowscale=None):
            for r0 in range(0, src.shape[0], 128):
                for c0 in range(0, src.shape[1], W):
                    n = min(W, src.shape[1] - c0)
                    i = cast_cnt[0] % len(stf)
                    cast_cnt[0] += 1
                    B.ld(stf[i][:, 0:n], src[r0:r0 + 128, c0:c0 + n], rk=cast_gate[0])
                    col = rowscale(r0) if rowscale is not None else None
                    if col is not None:
                        B.ts("dve", stb[i][:, 0:n], stf[i][:, 0:n], col, ALU.mult)
                    else:
                        B.cp(("act", "dve", "pool")[cast_cnt[0] % 3], stb[i][:, 0:n], stf[i][:, 0:n])
                    B.stq(dst[r0:r0 + 128, c0:c0 + n], stb[i][:, 0:n], wk=[key])

        def cast_up(l, stf, stb, JB):
            for kc in range(8):
                for half in range(2):
                    for j0 in range(0, NJ, JB):
                        nj = min(JB, NJ - j0)
                        n = nj * 128
                        i = cast_cnt[0] % len(stf)
                        cast_cnt[0] += 1
                        B.ld(stf[i][:, 0:n], ffn_up[l, kc * 128:(kc + 1) * 128, half * DFF + j0 * 128:half * DFF + j0 * 128 + n], rk=cast_gate[0])
                        B.cp(("act", "dve", "pool")[cast_cnt[0] % 3], stb[i][:, 0:n], stf[i][:, 0:n])
                        B.stq(up_r[l, j0:j0 + nj, :, kc, half, :].rearrange("j p c -> p j c"),
                              stb[i][:, 0:n].rearrange("p (j c) -> p j c", c=128), wk=[("up_r", l)])

        def deferred_casts(stf, stb, W, JB):
            cast_gate[0] = [("defer_gate",)] if 0 in layers else []
            if 1 in layers:
                cast_mat(ret_w_in, w_in1b, ("w_in1b",), stf, stb, W)
                cast_mat(ret_w_out, w_out1b, ("w_out1b",), stf, stb, W)
                cast_up(1, stf, stb, JB)
                cast_mat(ffn_down[1], down_b[1], ("down_b", 1), stf, stb, W)

        dgate = G.sb("dgate", [128, 1], F32)
        dcf = [G.sb("dcf%d" % i, [128, 1024], F32) for i in range(2)]
        dcb = [G.sb("dcb%d" % i, [128, 1024], BF16) for i in range(2)]

        S.mark("prologue")
        with B.scope() as P:
            rowsT(pre_gT, pre_g.rearrange("a (c p) -> (a c) p", p=128), 32, P)
            rowsT(qkcT, ab_qk_conv, 24, P)
            rowsT(fcvT, ffn_conv, 2 * 9 * NJ, P)
            rowsT(sgbT, ab_sgu_b, 4, P)
            rowsT(hgT, ab_head_g.rearrange("a (c p) -> (a c) p", p=128), 4, P)
            B.ld(gateb[:, :], ab_gate_b[0, :].partition_broadcast(128))
            B.ld(lgd[:, :], ret_decay[0, :].partition_broadcast(128))
            wstg = P.sb("wstg", [128, 128], F32)
            for g in range(4):
                B.ld(wstg[:, :], ab_sgu_w[g, :, :])
                ps = B.psum()
                B.tr(ps[:, 0:128], wstg[:, :], ident_f[:, :])
                B.cp("act", wsT[:, g, :], ps[:, 0:128])
            crow = P.sb("crow", [R, D], F32)
            B.ld(crow[0:NS, :], c_in[:, :])
            B.ld(crow[NS:R, :], cctx_in[:, :])
            srow = P.sb("srow", [R, D], F32)
            B.act(srow[:, :], crow[:, :], AF.Silu)
            ps = B.psum()
            for kc in range(8):
                B.tr(ps[:, kc * 8:kc * 8 + R], srow[:, kc * 128:(kc + 1) * 128], ident_f[0:R, 0:R])
            B.cp("dve", sT[:, :, :], ps[:, 0:64].rearrange("p (k r) -> p k r", r=8)[:, :, 0:R])
            adbT = P.sb("adbT", [128, 96], F32)
            rowsT(adbT, ada_b.rearrange("l (c p) -> (l c) p", p=128), 96, P)
            adw = [P.sb("adw%d" % i, [128, 8, D], F32) for i in range(2)]
            brow = P.sb("brow", [R, D], F32); grow = P.sb("grow", [R, D], F32); prow = P.sb("prow", [R, D], F32)
            for l in range(2):
                for m in range(6):
                    w = adw[(l * 6 + m) % 2]
                    B.ld(w[:, :, :], ada_w[l, :, m * D:(m + 1) * D].rearrange("(k p) n -> p k n", p=128))
                    ps = B.psum()
                    for fc in range(8):
                        for kc in range(8):
                            B.mm(ps[:, fc * 8:fc * 8 + R], w[:, kc, fc * 128:(fc + 1) * 128], sT[:, kc, :], start=(kc == 0), stop=(kc == 7))
                    for fc in range(8):
                        B.ts("dve", modsT[:, l, m * 8 + fc, :], ps[:, fc * 8:fc * 8 + R], adbT[:, l * 48 + m * 8 + fc:l * 48 + m * 8 + fc + 1], ALU.add)
                    if m in (2, 5):
                        which = 0 if m == 2 else 1
                        ps2 = B.psum(2)
                        for half in range(2):
                            for kc in range(8):
                                B.mm(ps2[0:R, half * 512:(half + 1) * 512], sT[:, kc, :], w[:, kc, half * 512:(half + 1) * 512], start=(kc == 0), stop=(kc == 7))
                        B.ld(brow[:, :], ada_b[l, m * D:(m + 1) * D].partition_broadcast(R))
                        B.ld(prow[:, :], post_g[l * 2 + which, :].partition_broadcast(R))
                        B.tt("dve", grow[:, :], ps2[0:R, :], brow[:, :], ALU.add)
                        B.tt("dve", grow[:, :], grow[:, :], prow[:, :], ALU.mult)
                        B.stq(gsc[l, which, :, :], grow[:, :], wk=[("gsc", l, which)])
                for which in range(2):
                    msc = 1 if which == 0 else 4
                    for r in range(R):
                        B.stt(g1c[:, l, which, r, :], modsT[:, l, msc * 8:(msc + 1) * 8, r], 1.0,
                              pre_gT[:, (l * 2 + which) * 8:(l * 2 + which) * 8 + 8], ALU.add, ALU.mult)

            NSTG = 4
            cst_f = [P.sb("cst_f%d" % i, [128, 2048], F32) for i in range(NSTG)]
            cst_b = [P.sb("cst_b%d" % i, [128, 2048], BF16) for i in range(NSTG)]
            if 0 in layers:
                cast_mat(ab_w_in, w_in0b, ("w_in0b",), cst_f, cst_b, 2048)
                cast_mat(ab_w_out, w_out0b, ("w_out0b",), cst_f, cst_b, 2048,
                         rowscale=lambda r0: hgT[:, (r0 - 512) // 128:(r0 - 512) // 128 + 1] if r0 >= 512 else None)
                cast_up(0, cst_f, cst_b, 11)
                cast_mat(ffn_down[0], down_b[0], ("down_b", 0), cst_f, cst_b, 2048)
            if 0 not in layers:
                deferred_casts(cst_f, cst_b, 2048, 11)

        def src_tile(layer, b, i):
            if layer == 0:
                if i < 2:
                    return ctx_in[b, i * 128:(i + 1) * 128, :], None
                return x_in[b, (i - 2) * 128:(i - 1) * 128, :], None
            return xo_s[b, i * 128:(i + 1) * 128, :], ("xo", b, i)

        def prenorm_to_hT(W, xt, layer, which, row, i, tagk):
            junk, ss, tmp1, rs, xn = W["junk"], W["ss"], W["tmp1"], W["rs"], W["xn"]
            B.act(junk[:, :], xt[:, :], AF.Square, accum=ss[:, 0:1])
            B.rstd(rs[:, 0:1], ss[:, 0:1], D, tmp1[:, 0:1])
            B.ts("dve", xn[:, :], xt[:, :], rs[:, 0:1], ALU.mult)
            msh = 0 if which == 0 else 3
            ps = B.psum(2)
            for kc in range(8):
                B.tr(ps[:, kc * 128:(kc + 1) * 128], xn[:, kc * 128:(kc + 1) * 128], ident_f[:, :])
            for kc in range(8):
                B.act(hT[:, kc, i * 128:(i + 1) * 128], ps[:, kc * 128:(kc + 1) * 128], AF.Identity,
                      bias=modsT[:, layer, msh * 8 + kc, row:row + 1], scale=g1c[:, layer, which, row, kc:kc + 1],
                      wk=[("hT", i)])

        def post_residual(W, psY, xt, gt, layer, out_ap, out_key, xm):
            junk, ss, tmp1, rs, t1 = W["junk"], W["ss"], W["tmp1"], W["rs"], W["t1"]
            B.act(junk[:, :], psY[:, :], AF.Square, accum=ss[:, 1:2])
            B.rstd(rs[:, 1:2], ss[:, 1:2], D, tmp1[:, 1:2])
            B.stt(t1[:, :], psY[:, :], rs[:, 1:2], gt[:, :], ALU.mult, ALU.mult)
            B.tt("pool", xm[:, :], t1[:, :], xt[:, :], ALU.add)
            B.stq(out_ap, xm[:, :], wk=[out_key])

        def ffn(layer, b, tiles):
            par = b % 2
            has_ctx = 0 in tiles
            S.mark("L%d ffn_up" % layer)
            with B.scope() as F:
                gp = [F.sb("gp%d" % i, [128, 34, 66], F32) for i in range(2)]
                gc_ = [F.sb("gc", [128, 258], F32) for _i in range(2)]
                ptap_ = [F.sb("ptap", [128, SEQ], F32) for _i in range(2)] if POOL_TAP else [None, None]
                ab_ = [F.sb("ab%d" % i, [128, T], F32) for i in range(2)]
                acc_ = [F.sb("acc", [128, T], F32) for _i in range(2)]
                gl_ = [F.sb("gl", [128, T], F32) for _i in range(2)]
                ao = [F.sb("ao%d" % i, [128, T], BF16) for i in range(2)]
                wj = [F.sb("wj%d" % i, [128, 8, 256], BF16) for i in range(2)]
                for i in range(2):
                    B.memset("pool", gp[i][:, :, :], 0.0)
                for _i in range(2):
                    B.memset("pool", gc_[_i][:, :], 0.0)
                blocks = ([(0, 256)] if has_ctx else []) + [(256 + 512 * k, 512) for k in range(4)]
                for j in range(NJ):
                    w = wj[j % 2]
                    B.ld(w[:, :, :], up_r[layer, j, :, :, :, :].rearrange("p k h c -> p k (h c)"), rk=[("up_r", layer)])
                    gpj, abj, aoj = gp[j % 2], ab_[j % 2], ao[j % 2]
                    acc, gl, gc, ptap = acc_[j % 2], gl_[j % 2], gc_[j % 2], ptap_[j % 2]
                    for (t0, n) in blocks:
                        ps = B.psum(2)
                        rk = [w] + [("hT", t0 // 128 + q) for q in range(n // 128)]
                        for kc in range(8):
                            B.mm(ps[:, 0:n], w[:, kc, 128:256], hT[:, kc, t0:t0 + n], start=(kc == 0), stop=(kc == 7), rk=rk)
                        for kc in range(8):
                            B.mm(ps[:, 512:512 + n], w[:, kc, 0:128], hT[:, kc, t0:t0 + n], start=(kc == 0), stop=(kc == 7), rk=rk)
                        if t0 == 0:
                            B.cp("act", gc[:, 1:257], ps[:, 0:256])
                        else:
                            r0 = (t0 - 256) // 64
                            B.cp("act", gpj[:, 1 + r0:9 + r0, 1:65], ps[:, 0:512].rearrange("p (r c) -> p r c", c=64))
                        B.cp("act", abj[:, t0:t0 + n], ps[:, 512:512 + n])
                    accv = acc[:, 256:T].rearrange("p (r c) -> p r c", c=64)
                    first = True
                    for ty in range(3):
                        for tx in range(3):
                            wcol = fcvT[:, (layer * 9 + ty * 3 + tx) * NJ + j:(layer * 9 + ty * 3 + tx) * NJ + j + 1]
                            src = gpj[:, ty:ty + 32, tx:tx + 64]
                            if POOL_TAP and ty == 2 and tx == 2:
                                B.ts("pool", ptap[:, :].rearrange("p (r c) -> p r c", c=64), src, wcol, ALU.mult)
                                continue
                            if first:
                                B.act(accv, src, AF.Copy, scale=wcol)
                                first = False
                            else:
                                B.stt(accv, src, wcol, accv, ALU.mult, ALU.add)
                    if has_ctx:
                        for tx in range(3):
                            wcol = fcvT[:, (layer * 9 + 3 + tx) * NJ + j:(layer * 9 + 3 + tx) * NJ + j + 1]
                            if tx == 0:
                                B.act(acc[:, 0:256], gc[:, 0:256], AF.Copy, scale=wcol)
                            else:
                                B.stt(acc[:, 0:256], gc[:, tx:tx + 256], wcol, acc[:, 0:256], ALU.mult, ALU.add)
                    t00 = 0 if has_ctx else 256
                    if POOL_TAP:
                        B.tt("pool", acc[:, 256:T], acc[:, 256:T], ptap[:, :], ALU.add)
                    B.act(gl[:, t00:T], acc[:, t00:T], AF.Gelu_apprx_tanh)
                    B.tt("pool", aoj[:, t00:T], gl[:, t00:T], abj[:, t00:T], ALU.mult)
                    B.stq(act_s[par, j, :, t00:T], aoj[:, t00:T], wk=[("act_s", par, j)])
            S.mark("L%d ffn_down" % layer)
            with B.scope() as F:
                wd = F.sb("wd", [128, NJ, D], BF16)
                B.ld(wd[:, :, :], down_b[layer].rearrange("(j p) n -> p j n", p=128), rk=[("down_b", layer)])
                ablk = [F.sb("ablk%d" % i, [128, NJ, 256], BF16) for i in range(2)]
                g5 = F.sb("g5", [128, D], F32); g5c = F.sb("g5c", [128, D], F32)
                B.ld(g5[:, :], gsc[layer, 1, b, :].partition_broadcast(128), rk=[("gsc", layer, 1)])
                if has_ctx:
                    B.ld(g5c[:, :], gsc[layer, 1, NS, :].partition_broadcast(128), rk=[("gsc", layer, 1)])
                Wks = [{"junk": F.sb("f_junk", [128, D], BF16), "ss": F.sb("f_ss", [128, 2], F32), "tmp1": F.sb("f_tmp1", [128, 2], F32),
                        "rs": F.sb("f_rs", [128, 2], F32), "t1": F.sb("f_t1", [128, D], F32)} for _i in range(2)]
                xts = [F.sb("f_xt%d" % i, [128, D], F32) for i in range(2)]
                xos = [F.sb("f_xo%d" % i, [128, D], F32) for i in range(2)]
                nb = 0
                for t0 in range(0 if has_ctx else 256, T, 256):
                    blk = ablk[nb % 2]
                    nb += 1
                    B.ld(blk[:, :, :], act_s[par, :, :, t0:t0 + 256].rearrange("j p t -> p j t"), rk=[("act_s", par, j) for j in range(NJ)])
                    for q in range(2):
                        i = t0 // 128 + q
                        xt = xts[i % 2]; xo = xos[i % 2]
                        B.ld(xt[:, :], xm_s[b, i * 128:(i + 1) * 128, :], rk=[("xm", b, i)])
                        ps = B.psum(2)
                        for half in range(2):
                            for j in range(NJ):
                                B.mm(ps[:, half * 512:(half + 1) * 512], blk[:, j, q * 128:(q + 1) * 128], wd[:, j, half * 512:(half + 1) * 512],
                                     start=(j == 0), stop=(j == NJ - 1))
                        if layer == 1:
                            oap, okey = out_t[b, (i - 2) * 128:(i - 1) * 128, :], ("out", b, i)
                        else:
                            oap, okey = xo_s[b, i * 128:(i + 1) * 128, :], ("xo", b, i)
                        post_residual(Wks[i % 2], ps, xt, g5c if i < 2 else g5, layer, oap, okey, xo)

        def layer0(b):
            par = b % 2
            with B.scope() as M:
                KT = M.sb("KT", [128, 4, T], BF16); QT = M.sb("QT", [128, 4, T], BF16)
                Vx = M.sb("Vx", [128, NT, 4, 129], BF16)
                Gt = M.sb("Gt", [128, NT, 16], F32)
                EA = M.sb("EA", [128, NT, 8], F32); EAW = M.sb("EAW", [128, NT, 8], F32)
                EB = M.sb("EB", [128, NT, 8], F32); EBT = M.sb("EBT", [128, NT, 8], F32)
                C32 = M.sb("C32", [128, 8, 129], F32); Cbf = M.sb("Cbf", [128, 8, 129], BF16)
                Wks, xts = LW["Wks"], LW["xts"]
                NX = len(xts)
                yb = [M.sb("yb%d" % i, [128, 512], F32) for i in range(2)]
                PT = [M.sb("PT%d" % i, [128, 4, 128], BF16) for i in range(2)]
                Kw = [M.sb("Kw%d" % i, [128, 4, 128], BF16) for i in range(2)]
                dn_ = [M.sb("dn", [128, 4], F32) for _i in range(2)]; r2_ = [M.sb("r2", [128, 4], F32) for _i in range(2)]
                S.mark("L0 P1")
                for i in range(NT):
                    xt = xts[i % NX]
                    sap, skey = src_tile(0, b, i)
                    B.ld(xt[:, :], sap, rk=[skey] if skey else [])
                    prenorm_to_hT(Wks[i % 2], xt, 0, 0, NS if i < 2 else b, i, None)
                allh = [("hT", i) for i in range(NT)]
                S.mark("L0 P2a")
                with B.scope() as P2:
                    Wa = P2.sb("Wa", [128, 8, 1024], BF16)
                    B.ld(Wa[:, :, 0:512], w_in0b[:, 0:512].rearrange("(k p) n -> p k n", p=128), rk=[("w_in0b",)])
                    B.ld(Wa[:, :, 512:1024], w_in0b[:, 1040:1552].rearrange("(k p) n -> p k n", p=128), rk=[("w_in0b",)])
                    raw = [P2.sb("raw%d" % i, [128, T + 4], F32) for i in range(2)]
                    cacc = P2.sb("cacc", [128, T + 4], F32)
                    for i in range(2):
                        B.memset("pool", raw[i][:, :], 0.0)
                    for fc in range(8):
                        col0 = fc * 128
                        cch = 4 + fc if fc < 4 else fc - 4
                        rw = raw[fc % 2]
                        blocks = [(0, 256, 1)] + [(256 + 512 * k, 512, 259 + 512 * k) for k in range(4)]
                        for bi in range(0, len(blocks), 2):
                            ps = B.psum(2)
                            for q, (t0, n, off) in enumerate(blocks[bi:bi + 2]):
                                for kc in range(8):
                                    B.mm(ps[:, q * 512:q * 512 + n], Wa[:, kc, col0:col0 + 128], hT[:, kc, t0:t0 + n],
                                         start=(kc == 0), stop=(kc == 7), rk=[Wa] + allh)
                                B.cp("act", rw[:, off:off + n], ps[:, q * 512:q * 512 + n])
                        L = T + 4
                        B.ts("dve", cacc[:, 1:L - 1], rw[:, 0:L - 2], qkcT[:, cch:cch + 1], ALU.mult)
                        B.stt(cacc[:, 1:L - 1], rw[:, 1:L - 1], qkcT[:, 8 + cch:9 + cch], cacc[:, 1:L - 1], ALU.mult, ALU.add)
                        B.stt(cacc[:, 1:L - 1], rw[:, 2:L], qkcT[:, 16 + cch:17 + cch], cacc[:, 1:L - 1], ALU.mult, ALU.add)
                        dst = KT if fc < 4 else QT
                        B.act(dst[:, fc % 4, 0:256], cacc[:, 1:257], AF.Silu)
                        B.act(dst[:, fc % 4, 256:T], cacc[:, 259:259 + SEQ], AF.Silu)
                S.mark("L0 P2b")
                with B.scope() as P2:
                    Wr = P2.sb("Wr", [128, 8, 2064], BF16)
                    B.ld(Wr[:, :, 0:528], w_in0b[:, 512:1040].rearrange("(k p) n -> p k n", p=128), rk=[("w_in0b",)])
                    B.ld(Wr[:, :, 528:2064], w_in0b[:, 1552:3088].rearrange("(k p) n -> p k n", p=128), rk=[("w_in0b",)])
                    ogt = [P2.sb("ogt%d" % i, [128, 512], F32) for i in range(2)]
                    sgt = [P2.sb("sgt%d" % i, [128, 512], BF16) for i in range(2)]
                    u_ = [P2.sb("u", [128, 512], F32) for _i in range(2)]; va_ = [P2.sb("va", [128, 512], F32) for _i in range(2)]
                    sq_ = [P2.sb("sq", [128, 512], BF16) for _i in range(2)]
                    vc_ = [P2.sb("vc", [128, 512], BF16) for _i in range(2)]; s1_ = [P2.sb("s1", [128, 4], F32) for _i in range(2)]
                    def gates_and_prep():
                        S.mark("L0 gates")
                        psg = B.psum(1)
                        for i in range(NT):
                            for kc in range(8):
                                B.mm(psg[:, i * 16:(i + 1) * 16], hT[:, kc, i * 128:(i + 1) * 128], Wr[:, kc, 512:528], start=(kc == 0), stop=(kc == 7), rk=[Wr, ("hT", i)])
                        B.tt("dve", Gt[:, :, :], psg[:, 0:NT * 16].rearrange("p (t g) -> p t g", g=16), gateb[:, :].unsqueeze(1).to_broadcast([128, NT, 16]), ALU.add)
                        E1 = P2.sb("E1", [128, NT, 8], F32); Pn = P2.sb("Pn", [128, NT, 8], F32); A1 = P2.sb("A1", [128, NT, 8], F32); A2 = P2.sb("A2", [128, NT, 8], F32)
                        gv = Gt[:, :, :].rearrange("p t (g h) -> p t g h", h=4)
                        for d in range(2):
                            B.act(E1[:, :, d * 4:(d + 1) * 4], gv[:, :, 1 + 2 * d, :], AF.Exp, scale=-1.0)
                        B.act(Pn[:, :, :], E1[:, :, :], AF.Ln, bias=1.0)
                        ps = B.psum(1)
                        B.mm(ps[:, 0:72], triU[:, :], Pn[:, :, 0:4])
                        B.mm(ps[:, 72:144], triL[:, :], Pn[:, :, 4:8])
                        B.mm(ps[:, 144:288], ones_f[:, :], Pn[:, :, :])
                        for d in range(2):
                            B.tt("dve", A1[:, :, d * 4:(d + 1) * 4], gv[:, :, 2 * d, :], ps[:, d * 72:(d + 1) * 72].rearrange("p (t h) -> p t h", h=4), ALU.add)
                        B.tt("dve", A2[:, :, :], A1[:, :, :], ps[:, 144:288].rearrange("p (t j) -> p t j", j=8), ALU.subtract)
                        B.act(EA[:, :, :], A1[:, :, :], AF.Exp, bias=LNS_ap[:, 0:1])
                        B.act(EAW[:, :, :], A2[:, :, :], AF.Exp, bias=LNS_ap[:, 0:1])
                        for d in range(2):
                            B.act(EB[:, :, d * 4:(d + 1) * 4], ps[:, d * 72:(d + 1) * 72].rearrange("p (t h) -> p t h", h=4), AF.Exp, scale=-1.0)
                        B.act(EBT[:, :, :], ps[:, 144:288].rearrange("p (t j) -> p t j", j=8), AF.Exp, scale=-1.0)
                        if b == 0:
                            B.cp("pool", dgate[:, :], EBT[:, 0, 0:1], wk=[dgate, ("defer_gate",)])

                    B.memset("pool", Vx[:, :, :, :], 1.0)
                    for i in range(NT):
                        tk = slice(i * 128, (i + 1) * 128)
                        psV = B.psum(1)
                        for kc in range(8):
                            B.mm(psV[:, 0:512], hT[:, kc, tk], Wr[:, kc, 0:512], start=(kc == 0), stop=(kc == 7), rk=[Wr, ("hT", i)])
                        B.cp("act" if i % 2 == 0 else "dve", Vx[:, i, :, 0:128], psV[:, 0:512].rearrange("p (h d) -> p h d", h=4))
                    gates_and_prep()
                    for i in range(NT):
                        tk = slice(i * 128, (i + 1) * 128)
                        rk = [Wr, ("hT", i)]
                        u, va, sq, vc, s1 = u_[i % 2], va_[i % 2], sq_[i % 2], vc_[i % 2], s1_[i % 2]
                        psO = B.psum(2); psU = B.psum(2)
                        for (dst, c0, n) in ((psO[:, 0:512], 528, 512),
                                             (psO[:, 512:1024], 1040, 512), (psU[:, 0:512], 1552, 512)):
                            for kc in range(8):
                                B.mm(dst, hT[:, kc, tk], Wr[:, kc, c0:c0 + n], start=(kc == 0), stop=(kc == 7), rk=rk)
                        B.act(ogt[i % 2][:, :], psO[:, 0:512], AF.Sigmoid)
                        B.stq(og_s[par, i, :, :], ogt[i % 2][:, :], wk=[("og", par, i)])
                        B.act(u[:, :], psO[:, 512:1024], AF.Gelu_apprx_tanh)
                        B.act(va[:, :], psU[:, 0:512], AF.Gelu_apprx_tanh, accum=s1[:, 0:1])
                        B.ts("dve", s1[:, 1:2], s1[:, 0:1], -1.0 / 512, ALU.mult)
                        B.act(sq[:, :], va[:, :], AF.Square, bias=s1[:, 1:2], accum=s1[:, 2:3])
                        B.rstd(s1[:, 3:4], s1[:, 2:3], 512, s1[:, 2:3])
                        B.ts("dve", vc[:, :], va[:, :], s1[:, 1:2], ALU.add, s1[:, 3:4], ALU.mult)
                        for g in range(4):
                            B.mm(psU[:, 512 + g * 128:512 + (g + 1) * 128], wsT[:, g, :], vc[:, g * 128:(g + 1) * 128])
                        for g in range(4):
                            B.stt(sgt[i % 2][:, g * 128:(g + 1) * 128], psU[:, 512 + g * 128:512 + (g + 1) * 128], sgbT[:, g:g + 1],
                                  u[:, g * 128:(g + 1) * 128], ALU.add, ALU.mult)
                        B.stq(sgu_s[par, i, :, :], sgt[i % 2][:, :], wk=[("sgu", par, i)])
                step = [0]
                cur_r2 = [None]

                def scan_step(d, c):
                    k = step[0] % 2
                    step[0] += 1
                    dn, r2 = dn_[k], r2_[k]
                    cur_r2[0] = r2
                    msk = mskU if d == 0 else mskL
                    tk = slice(c * 128, (c + 1) * 128)
                    psA = B.psum(1); psAt = B.psum(1)
                    psAb = psAt[:, 0:512].bitcast(BF16)
                    for h in range(4):
                        B.mm(psA[:, h * 128:(h + 1) * 128], KT[:, h, tk], QT[:, h, tk])
                    for h in range(4):
                        B.tr(psAb[:, h * 128:(h + 1) * 128], KT[:, h, tk], ident_b[:, :])
                    for h in range(4):
                        B.stt(PT[k][:, h, :], psA[:, h * 128:(h + 1) * 128], EA[:, c, d * 4 + h:d * 4 + h + 1], msk[:, :], ALU.mult, ALU.mult)
                    for h in range(4):
                        B.act(Kw[k][:, h, :], psAb[:, h * 128:(h + 1) * 128], AF.Copy, scale=EAW[:, c, d * 4 + h:d * 4 + h + 1])
                    psB = B.psum(2)
                    for h in range(4):
                        B.mm(psB[:, h * 256:h * 256 + 129], PT[k][:, h, :], Vx[:, c, h, :], start=True, stop=False)
                        B.mm(psB[:, h * 256:h * 256 + 129], QT[:, h, tk], Cbf[:, d * 4 + h, :], start=False, stop=True)
                    B.tt("dve", dn[:, :], psB[:, 128::256], EB[:, c, d * 4:d * 4 + 4], ALU.mult)
                    B.act(dn[:, :], dn[:, :], AF.Abs)
                    B.ts("dve", dn[:, :], dn[:, :], 1.0, ALU.max)
                    B.recip(dn[:, :], dn[:, :])
                    B.tt("dve", r2[:, :], dn[:, :], EB[:, c, d * 4:d * 4 + 4], ALU.mult)
                    psD = B.psum(2)
                    for h in range(4):
                        B.mm(psD[:, h * 256:h * 256 + 129], Kw[k][:, h, :], Vx[:, c, h, :])
                    for h in range(4):
                        j = d * 4 + h
                        B.stt(C32[:, j, :], C32[:, j, :], EBT[:, c, j:j + 1], psD[:, h * 256:h * 256 + 129], ALU.mult, ALU.add)
                    B.cp("act", Cbf[:, d * 4:d * 4 + 4, :], C32[:, d * 4:d * 4 + 4, :])
                    return psB

                B.memset("pool", C32[:, :, :], 0.0)
                B.memset("pool", Cbf[:, :, :], 0.0)
                S.mark("L0 bwd")
                for c in [1, 0] + list(range(NT - 1, 1, -1)):
                    psB = scan_step(1, c)
                    y = yb[c % 2]
                    B.tt("dve", y[:, :].rearrange("p (h d) -> p h d", h=4), psB[:, :].rearrange("p (h d) -> p h d", d=256)[:, :, 0:128],
                         cur_r2[0][:, :].unsqueeze(2).to_broadcast([128, 4, 128]), ALU.mult)
                    B.stq(yb_s[par, c, :, :], y[:, :], wk=[("yb", par, c)])

                S.mark("L0 fwd+P4")
                with B.scope() as P4:
                    Wo = P4.sb("Wo", [128, 8, D], BF16)
                    B.ld(Wo[:, :, :], w_out0b.rearrange("(k p) n -> p k n", p=128), rk=[("w_out0b",)])
                    g2 = P4.sb("g2", [128, D], F32)
                    B.ld(g2[:, :], gsc[0, 0, NS, :].partition_broadcast(128), rk=[("gsc", 0, 0)])
                    ybl = [P4.sb("ybl%d" % i, [128, 512], F32) for i in range(2)]
                    ogl = [P4.sb("ogl%d" % i, [128, 512], F32) for i in range(2)]
                    ys_ = [P4.sb("ys", [128, 512], F32) for _i in range(2)]; sq4_ = [P4.sb("sq4", [128, 512], F32) for _i in range(2)]
                    st4_ = [P4.sb("st4", [128, 4], F32) for _i in range(2)]; st4b_ = [P4.sb("st4b", [128, 4], F32) for _i in range(2)]
                    st4c_ = [P4.sb("st4c", [128, 4], F32) for _i in range(2)]
                    cat = [P4.sb("cat%d" % i, [128, D], BF16) for i in range(2)]
                    catT_ = [P4.sb("catT", [128, 8, 128], BF16) for _i in range(2)]
                    for c in range(NT):
                        row = NS if c < 2 else b
                        if c == 2:
                            B.ld(g2[:, :], gsc[0, 0, b, :].partition_broadcast(128), rk=[("gsc", 0, 0)])
                        ct = cat[c % 2]
                        ys, sq, st4, st4b, st4c, catT = ys_[c % 2], sq4_[c % 2], st4_[c % 2], st4b_[c % 2], st4c_[c % 2], catT_[c % 2]
                        B.ld(ybl[c % 2][:, :], yb_s[par, c, :, :], rk=[("yb", par, c)])
                        B.ld(ogl[c % 2][:, :], og_s[par, c, :, :], rk=[("og", par, c)])
                        B.ld(ct[:, 0:512], sgu_s[par, c, :, :], rk=[("sgu", par, c)])
                        psB = scan_step(0, c)
                        B.tt("dve", ys[:, :].rearrange("p (h d) -> p h d", h=4), psB[:, :].rearrange("p (h d) -> p h d", d=256)[:, :, 0:128],
                             cur_r2[0][:, :].unsqueeze(2).to_broadcast([128, 4, 128]), ALU.mult)
                        B.tt("pool", ys[:, :], ys[:, :], ybl[c % 2][:, :], ALU.add)
                        B.tt("pool", ys[:, :], ys[:, :], ogl[c % 2][:, :], ALU.mult)
                        z3 = ys[:, :].rearrange("p (h d) -> p h d", h=4)
                        B.red(st4[:, :], z3)
                        B.ts("dve", st4[:, :], st4[:, :], 1.0 / 128, ALU.mult)
                        B.tt("dve", z3, z3, st4[:, :].unsqueeze(2).to_broadcast([128, 4, 128]), ALU.subtract)
                        B.tt("pool", sq[:, :], ys[:, :], ys[:, :], ALU.mult)
                        B.red(st4b[:, :], sq[:, :].rearrange("p (h d) -> p h d", h=4))
                        B.rstd(st4c[:, :], st4b[:, :], 128, st4b[:, :])
                        B.tt("dve", ct[:, 512:1024].rearrange("p (h d) -> p h d", h=4), z3, st4c[:, :].unsqueeze(2).to_broadcast([128, 4, 128]), ALU.mult)
                        psT = B.psum(1)
                        psTb = psT[:, :].bitcast(BF16)
                        for kc in range(8):
                            B.tr(psTb[:, kc * 128:(kc + 1) * 128], ct[:, kc * 128:(kc + 1) * 128], ident_b[:, :])
                        B.cp("act", catT[:, :, :], psTb[:, 0:1024].rearrange("p (k t) -> p k t", t=128))
                        psY = B.psum(2)
                        for half in range(2):
                            for kc in range(8):
                                B.mm(psY[:, half * 512:(half + 1) * 512], catT[:, kc, :], Wo[:, kc, half * 512:(half + 1) * 512], start=(kc == 0), stop=(kc == 7))
                        xt = xts[c % NX]
                        sap, skey = src_tile(0, b, c)
                        B.ld(xt[:, :], sap, rk=[skey] if skey else [])
                        post_residual(Wks[c % 2], psY, xt, g2, 0, xm_s[b, c * 128:(c + 1) * 128, :], ("xm", b, c), xt)
                        prenorm_to_hT(Wks[c % 2], xt, 0, 1, row, c, None)
            ffn(0, b, list(range(NT)))

        EPS_t = G.sb("EPS_t", [128, 1], F32)
        B.memset("pool", EPS_t[:, :], EPS)
        B.eps_ap = EPS_t[:, 0:1]
        LNS_ap = G.sb("LNS_ap", [128, 1], F32)
        B.memset("pool", LNS_ap[:, :], math.log(128.0 ** -0.5))

        lg = G.sb("lg", [128, 8], F32); e1 = G.sb("e1", [128, 8], F32); pp1 = G.sb("pp1", [128, 1], F32)
        xi = G.sb("xi", [128, 8], F32); zeta = G.sb("zeta", [128, 8], F32); colA = G.sb("colA", [128, 8], F32); g128 = G.sb("g128", [128, 8], F32)
        rtmp = G.sb("rtmp", [128, 8], F32)
        DmT = G.sb("DmT", [128, 8, 128], BF16)
        if 1 in layers:
            LNK = math.log(256.0 ** -0.5)
            B.act(rtmp[:, :], lgd[:, :], AF.Exp, scale=-1.0)
            B.act(lg[:, :], rtmp[:, :], AF.Ln, bias=1.0)
            B.ts("dve", lg[:, :], lg[:, :], -1.0, ALU.mult)
            ps = B.psum()
            B.mm(ps[:, 0:1], triU[:, :], ones_f[:, 0:1])
            B.cp("dve", pp1[:, :], ps[:, 0:1])
            B.ts("dve", e1[:, :], lg[:, :], pp1[:, 0:1], ALU.mult)
            B.act(g128[:, :], lg[:, :], AF.Exp, scale=128.0)
            B.act(xi[:, 0:4], e1[:, 0:4], AF.Exp)
            B.ts("dve", rtmp[:, 0:4], e1[:, 0:4], -1.0, ALU.mult, LNK, ALU.add)
            B.act(colA[:, 0:4], rtmp[:, 0:4], AF.Exp)
            B.stt(rtmp[:, 0:4], lg[:, 0:4], 128.0, e1[:, 0:4], ALU.mult, ALU.subtract)
            B.ts("dve", rtmp[:, 0:4], rtmp[:, 0:4], LNK, ALU.add)
            B.act(zeta[:, 0:4], rtmp[:, 0:4], AF.Exp)
            B.stt(rtmp[:, 4:8], lg[:, 4:8], 129.0, e1[:, 4:8], ALU.mult, ALU.subtract)
            B.act(xi[:, 4:8], rtmp[:, 4:8], AF.Exp)
            B.ts("dve", rtmp[:, 4:8], rtmp[:, 4:8], -1.0, ALU.mult, LNK, ALU.add)
            B.act(colA[:, 4:8], rtmp[:, 4:8], AF.Exp)
            B.stt(rtmp[:, 4:8], lg[:, 4:8], -1.0, e1[:, 4:8], ALU.mult, ALU.add)
            B.ts("dve", rtmp[:, 4:8], rtmp[:, 4:8], LNK, ALU.add)
            B.act(zeta[:, 4:8], rtmp[:, 4:8], AF.Exp)
            for j in range(8):
                B.ts("dve", DmT[:, j, :], (triU if j < 4 else triL)[:, :], colA[:, j:j + 1], ALU.mult)

        def layer1(b):
            par = b % 2
            with B.scope() as M:
                Wks, xts = LW["Wks"], LW["xts"]
                S.mark("L1 P1")
                for i in range(NT):
                    xt = xts[i % len(xts)]
                    sap, skey = src_tile(1, b, i)
                    B.ld(xt[:, :], sap, rk=[skey])
                    prenorm_to_hT(Wks[i % 2], xt, 1, 0, NS if i < 2 else b, i, None)
                allh = [("hT", i) for i in range(NT)]
                for h in range(4):
                    with B.scope() as H:
                        Wh = H.sb("Wh", [128, 8, 1536], BF16)
                        for (d0, s0, n) in ((0, h * 256, 256), (256, 1024 + h * 512, 512), (768, 3072 + h * 256, 256), (1024, 4096 + h * 512, 512)):
                            B.ld(Wh[:, :, d0:d0 + n], w_in1b[:, s0:s0 + n].rearrange("(k p) n -> p k n", p=128), rk=[("w_in1b",)])
                        KTh = H.sb("KTh", [128, 2, T], BF16); QTh = H.sb("QTh", [128, 2, T], BF16)
                        Vh = H.sb("Vh", [128, NT, 512], BF16)
                        R32_ = [H.sb("R32", [128, 2, 512], F32) for _i in range(2)]; Rbf_ = [H.sb("Rbf", [128, 2, 512], BF16) for _i in range(2)]
                        PT = [H.sb("PT%d" % i, [128, 128], BF16) for i in range(2)]
                        Kz = [H.sb("Kz%d" % i, [128, 256], BF16) for i in range(2)]
                        hgh = H.sb("hgh", [128, 512], F32)
                        B.ld(hgh[:, :], ret_head_g[0, h * 512:(h + 1) * 512].partition_broadcast(128))
                        S.mark("L1 h%d proj" % h)
                        for fc in range(4):
                            col0 = fc * 128 if fc < 2 else 768 + (fc - 2) * 128
                            dst = KTh if fc < 2 else QTh
                            blocks = ([(0, 256)] if fc < 2 else []) + [(256 + 512 * k, 512) for k in range(4)]
                            for bi in range(0, len(blocks), 2):
                                ps = B.psum(2)
                                for q, (t0, n) in enumerate(blocks[bi:bi + 2]):
                                    for kc in range(8):
                                        B.mm(ps[:, q * 512:q * 512 + n], Wh[:, kc, col0:col0 + 128], hT[:, kc, t0:t0 + n],
                                             start=(kc == 0), stop=(kc == 7), rk=[Wh] + allh)
                                    B.cp("act" if q == 0 else "dve", dst[:, fc % 2, t0:t0 + n], ps[:, q * 512:q * 512 + n])
                        for i in range(NT):
                            ps = B.psum()
                            for kc in range(8):
                                B.mm(ps[:, 0:512], hT[:, kc, i * 128:(i + 1) * 128], Wh[:, kc, 256:768], start=(kc == 0), stop=(kc == 7), rk=[Wh, ("hT", i)])
                            B.cp("act" if i % 2 == 0 else "dve", Vh[:, i, :], ps[:, 0:512])
                        step = [0]

                        def scan_step(d, c, with_q):
                            k = step[0] % 2
                            step[0] += 1
                            j = d * 4 + h
                            R32, Rbf = R32_[d], Rbf_[d]
                            tk = slice(c * 128, (c + 1) * 128)
                            psA = B.psum(1); psAt = B.psum(1)
                            psAb = psAt[:, 0:512].bitcast(BF16)
                            if with_q:
                                for kc in range(2):
                                    B.mm(psA[:, 0:128], KTh[:, kc, tk], QTh[:, kc, tk], start=(kc == 0), stop=(kc == 1))
                            for kc in range(2):
                                B.tr(psAb[:, kc * 128:(kc + 1) * 128], KTh[:, kc, tk], ident_b[:, :])
                            psB = None
                            if with_q:
                                B.tt("dve", PT[k][:, :], psA[:, 0:128], DmT[:, j, :], ALU.mult)
                            B.act(Kz[k][:, :], psAb[:, 0:256], AF.Copy, scale=zeta[:, j:j + 1])
                            if with_q:
                                psB = B.psum(1)
                                B.mm(psB[:, 0:512], PT[k][:, :], Vh[:, c, :], start=True, stop=False)
                                B.mm(psB[:, 0:512], QTh[:, 0, tk], Rbf[:, 0, :], start=False, stop=False)
                                B.mm(psB[:, 0:512], QTh[:, 1, tk], Rbf[:, 1, :], start=False, stop=True)
                            psD = B.psum(2)
                            for kc in range(2):
                                B.mm(psD[:, kc * 512:(kc + 1) * 512], Kz[k][:, kc * 128:(kc + 1) * 128], Vh[:, c, :])
                            B.stt(R32[:, :, :], R32[:, :, :], g128[:, j:j + 1], psD[:, :].rearrange("p (k v) -> p k v", k=2), ALU.mult, ALU.add)
                            B.cp("act", Rbf[:, :, :], R32[:, :, :])
                            return psB

                        S.mark("L1 h%d scan" % h)
                        with B.scope() as PB:
                            yb = [PB.sb("yb%d" % i, [128, 512], F32) for i in range(4)]
                            for d in range(2):
                                B.memset("pool", R32_[d][:, :, :], 0.0); B.memset("pool", Rbf_[d][:, :, :], 0.0)
                            bw = [1, 0] + list(range(NT - 1, 1, -1))
                            for kk in range(NT):
                                for d, c in ((1, bw[kk]), (0, kk)):
                                    psB = scan_step(d, c, c >= 2)
                                    if c >= 2:
                                        y = yb[(2 * kk + d) % 4]
                                        B.act(y[:, :], psB[:, 0:512], AF.Copy, scale=xi[:, d * 4 + h:d * 4 + h + 1])
                                        if d == 1:
                                            B.stq(yb_s[par, c, :, :], y[:, :], wk=[("yb", par, c)])
                                        else:
                                            B.stq(yf_s[par, c, :, :], y[:, :], wk=[("yf", par, c)])
                        S.mark("L1 h%d merge" % h)
                        with B.scope() as PF:
                            ND = 3
                            ybl = [PF.sb("ybl%d" % i, [128, 512], F32) for i in range(ND)]
                            yfl = [PF.sb("yfl%d" % i, [128, 512], F32) for i in range(ND)]
                            ys_ = [PF.sb("ys", [128, 512], F32) for _i in range(ND)]; sqj_ = [PF.sb("sqj", [128, 512], BF16) for _i in range(ND)]
                            sg_ = [PF.sb("sg", [128, 512], F32) for _i in range(ND)]
                            zt = [PF.sb("zt%d" % i, [128, 512], BF16) for i in range(ND)]
                            s1_ = [PF.sb("s1r", [128, 4], F32) for _i in range(ND)]
                            for c in range(2, NT):
                                tk = slice(c * 128, (c + 1) * 128)
                                ys, sqj, sg, s1 = ys_[c % ND], sqj_[c % ND], sg_[c % ND], s1_[c % ND]
                                B.ld(ybl[c % ND][:, :], yb_s[par, c, :, :], rk=[("yb", par, c)])
                                B.ld(yfl[c % ND][:, :], yf_s[par, c, :, :], rk=[("yf", par, c)])
                                B.tt("dve", ys[:, :], yfl[c % ND][:, :], ybl[c % ND][:, :], ALU.add)
                                psG = B.psum()
                                for kc in range(8):
                                    B.mm(psG[:, 0:512], hT[:, kc, tk], Wh[:, kc, 1024:1536], start=(kc == 0), stop=(kc == 7), rk=[Wh, ("hT", c)])
                                B.act(sg[:, :], psG[:, 0:512], AF.Silu)
                                B.tt("pool", sg[:, :], sg[:, :], hgh[:, :], ALU.mult)
                                B.red(s1[:, 0:1], ys[:, :])
                                B.ts("dve", s1[:, 1:2], s1[:, 0:1], -1.0 / 512, ALU.mult)
                                B.act(sqj[:, :], ys[:, :], AF.Square, bias=s1[:, 1:2], accum=s1[:, 2:3])
                                B.rstd(s1[:, 3:4], s1[:, 2:3], 512, s1[:, 2:3])
                                B.ts("dve", ys[:, :], ys[:, :], s1[:, 1:2], ALU.add, s1[:, 3:4], ALU.mult)
                                B.tt("dve", zt[c % ND][:, :], ys[:, :], sg[:, :], ALU.mult)
                                B.stq(z_s[par, (c - 2) * 128:(c - 1) * 128, h * 512:(h + 1) * 512], zt[c % ND][:, :], wk=[("z", par, c)])
                S.mark("L1 P4")
                with B.scope() as P4:
                    Wo = P4.sb("Wo1", [128, 16, D], BF16)
                    B.ld(Wo[:, :, :], w_out1b.rearrange("(k p) n -> p k n", p=128), rk=[("w_out1b",)])
                    g2 = P4.sb("g2", [128, D], F32)
                    B.ld(g2[:, :], gsc[1, 0, b, :].partition_broadcast(128), rk=[("gsc", 1, 0)])
                    zl = [P4.sb("zl%d" % i, [128, 2048], BF16) for i in range(2)]
                    zT_ = [P4.sb("zT", [128, 16, 128], BF16) for _i in range(2)]
                    for c in range(2, NT):
                        z = zl[c % 2]
                        zT = zT_[c % 2]
                        B.ld(z[:, :], z_s[par, (c - 2) * 128:(c - 1) * 128, :], rk=[("z", par, c)])
                        psT = B.psum(2)
                        psTb = psT[:, :].bitcast(BF16)
                        for kc in range(16):
                            B.tr(psTb[:, kc * 128:(kc + 1) * 128], z[:, kc * 128:(kc + 1) * 128], ident_b[:, :])
                        B.cp("act", zT[:, :, :], psTb[:, :].rearrange("p (k t) -> p k t", t=128))
                        psY = B.psum(2)
                        for half in range(2):
                            for kc in range(16):
                                B.mm(psY[:, half * 512:(half + 1) * 512], zT[:, kc, :], Wo[:, kc, half * 512:(half + 1) * 512], start=(kc == 0), stop=(kc == 15))
                        xt = xts[c % len(xts)]
                        sap, skey = src_tile(1, b, c)
                        B.ld(xt[:, :], sap, rk=[skey])
                        post_residual(Wks[c % 2], psY, xt, g2, 1, xm_s[b, c * 128:(c + 1) * 128, :], ("xm", b, c), xt)
                        prenorm_to_hT(Wks[c % 2], xt, 1, 1, b, c, None)
            ffn(1, b, list(range(2, NT)))

        LW = {}

        def layer_ws(sc, nx):
            Wks = []
            for _i in range(2):
                _w = {"junk": sc.sb("junk", [128, D], BF16), "ss": sc.sb("ss", [128, 2], F32), "tmp1": sc.sb("tmp1", [128, 2], F32),
                      "rs": sc.sb("rs", [128, 2], F32), "xn": sc.sb("xn", [128, D], F32)}
                _w["t1"] = _w["xn"]
                Wks.append(_w)
            LW["Wks"] = Wks
            LW["xts"] = [sc.sb("xt%d" % i, [128, D], F32) for i in range(nx)]

        if 0 in layers:
            with B.scope() as LG:
                layer_ws(LG, 3)
                for b in range(NS):
                    layer0(b)
                    if b == 0:
                        S.mark("deferred casts")
                        deferred_casts(dcf, dcb, 1024, 8)
        if 1 in layers:
            with B.scope() as LG:
                layer_ws(LG, 3)
                for b in range(NS):
                    layer1(b)
        S.finish()
    return nc, S


_CACHE = {}


def make_in_map(inp, b0, NS):
    f = lambda a: np.ascontiguousarray(np.asarray(a, dtype=np.float32))
    return {
        "x": f(inp["x"][b0:b0 + NS]), "c": f(inp["c"][b0:b0 + NS]), "ctx": f(inp["ctx"][b0:b0 + NS]),
        "c_ctx": f(inp["c_ctx"]).reshape(1, D),
        "ada_w": f(inp["ada_w"]), "ada_b": f(inp["ada_b"]),
        "pre_g": f(inp["pre_g"]).reshape(4, D), "post_g": f(inp["post_g"]).reshape(4, D),
        "ffn_up": f(inp["ffn_up"]), "ffn_conv": f(inp["ffn_conv"]).reshape(2 * 9 * NJ, 128), "ffn_down": f(inp["ffn_down"]),
        "ab_w_in": f(inp["ab_w_in"]).reshape(D, AB_PROJ), "ab_qk_conv": f(inp["ab_qk_conv"]).reshape(24, 128),
        "ab_gate_b": f(inp["ab_gate_b"]).reshape(1, 16), "ab_sgu_w": f(inp["ab_sgu_w"]).reshape(4, 128, 128),
        "ab_sgu_b": f(inp["ab_sgu_b"]).reshape(4, 128), "ab_head_g": f(inp["ab_head_g"]).reshape(1, 512),
        "ab_w_out": f(inp["ab_w_out"]).reshape(D, D), "ret_w_in": f(inp["ret_w_in"]).reshape(D, C_PROJ),
        "ret_decay": f(inp["ret_decay"]).reshape(1, 8), "ret_head_g": f(inp["ret_head_g"]).reshape(1, 2048),
        "ret_w_out": f(inp["ret_w_out"]).reshape(2048, D),
    }


def kernel(**inputs):
    NS = inputs["x"].shape[0] // N_CORES
    if "nc" not in _CACHE:
        _CACHE["nc"] = build_program(NS)[0]
    nc = _CACHE["nc"]
    in_maps = [make_in_map(inputs, i * NS, NS) for i in range(N_CORES)]
    res = run_bass_kernel_spmd(nc, in_maps, core_ids=list(range(N_CORES)))
    return np.concatenate([np.asarray(r["out"], dtype=np.float32) for r in res.results], axis=0)
```
